# Optimizing a Trainium2 kernel written in Bass

```python
import jax, jax.numpy as jnp
from jax import lax
import numpy as np

D_MODEL = 1024
BATCH = 8
SEQ = 4096
DEPTH = 4

HEAD_DIM = 64
PLE_DIM = 256
D_FF = 4 * D_MODEL
EPS = 1e-6

GDN_WIDTH = D_MODEL // 4
GDN_HEADS = GDN_WIDTH // HEAD_DIM
CONV_WIDTH = 4
GDN_CHUNK = 64

MLSTM_WIDTH = D_MODEL // 4
MLSTM_HEADS = MLSTM_WIDTH // HEAD_DIM
MLSTM_CHUNK = 64
GATE_SOFTCAP = 15.0

SWA_WIDTH = D_MODEL - GDN_WIDTH - MLSTM_WIDTH
SWA_Q_HEADS = SWA_WIDTH // HEAD_DIM
SWA_KV_HEADS = SWA_Q_HEADS // 4
SWA_GROUP = SWA_Q_HEADS // SWA_KV_HEADS
SWA_WINDOW = 128
SWA_BLOCK = 128
ROPE_THETA = 500000.0
ROPE_DIM = HEAD_DIM // 4

MIX_WIDTH = GDN_WIDTH + MLSTM_WIDTH + SWA_WIDTH
IN_SPLITS = (
    GDN_WIDTH, GDN_WIDTH, GDN_WIDTH, GDN_WIDTH, GDN_HEADS, GDN_HEADS,
    MLSTM_WIDTH, MLSTM_WIDTH, MLSTM_WIDTH, MLSTM_WIDTH, MLSTM_HEADS, MLSTM_HEADS,
    SWA_WIDTH, SWA_KV_HEADS * HEAD_DIM, SWA_KV_HEADS * HEAD_DIM,
)
IN_COLS = sum(IN_SPLITS)

kernel_name = 'hybrid_gdn_mlstm_swa_trunk'


def rms_norm(x, g):
    xf = x.astype(jnp.float32)
    y = xf * lax.rsqrt(jnp.mean(xf * xf, axis=-1, keepdims=True) + EPS)
    return (y * g.astype(jnp.float32)).astype(x.dtype)


def split_cols(t, sizes):
    offs = np.cumsum(np.array(sizes))[:-1].tolist()
    return jnp.split(t, offs, axis=-1)


def l2_normalize(x):
    return x * lax.rsqrt(jnp.sum(x * x, axis=-1, keepdims=True) + EPS)


def softcap(x, cap):
    return cap * jnp.tanh(x / cap)


def causal_depthwise_conv(x, w):
    K = w.shape[0]
    seq = x.shape[1]
    xp = jnp.pad(x, ((0, 0), (K - 1, 0), (0, 0)))
    out = xp[:, 0:seq] * w[0]
    for j in range(1, K):
        out = out + xp[:, j:j + seq] * w[j]
    return out


def to_chunked_heads(t, n_heads, chunk):
    bsz, seq, _ = t.shape
    return t.reshape(bsz, seq // chunk, chunk, n_heads, -1).transpose(0, 3, 1, 2, 4)


def to_chunked_gates(t, chunk):
    bsz, seq, n_heads = t.shape
    return t.reshape(bsz, seq // chunk, chunk, n_heads).transpose(0, 3, 1, 2)


def from_scan_heads(o):
    n, bsz, h, c, d = o.shape
    return o.transpose(1, 0, 3, 2, 4).reshape(bsz, n * c, h, d)


def rope_tables(positions):
    inv_freq = ROPE_THETA ** (-jnp.arange(0, ROPE_DIM, 2, dtype=jnp.float32) / ROPE_DIM)
    ang = positions.astype(jnp.float32)[..., None] * inv_freq
    return jnp.cos(ang), jnp.sin(ang)


def apply_partial_rope(x, cos, sin):
    half = ROPE_DIM // 2
    cos = cos.astype(x.dtype)
    sin = sin.astype(x.dtype)
    x1 = x[..., :half]
    x2 = x[..., half:ROPE_DIM]
    return jnp.concatenate([x1 * cos - x2 * sin, x2 * cos + x1 * sin, x[..., ROPE_DIM:]], axis=-1)


def gated_delta_net(q, k, v, z, beta_pre, a_pre, conv_w, a_log, dt_bias, norm_w):
    bsz, seq, _ = q.shape
    H, D, C = GDN_HEADS, HEAD_DIM, GDN_CHUNK
    f32 = jnp.float32
    qkv = jax.nn.silu(causal_depthwise_conv(jnp.concatenate([q, k, v], axis=-1), conv_w).astype(f32))
    q, k, v = (to_chunked_heads(t, H, C) for t in jnp.split(qkv, 3, axis=-1))
    q = l2_normalize(q) * (D ** -0.5)
    k = l2_normalize(k)
    beta = to_chunked_gates(jax.nn.sigmoid(beta_pre.astype(f32)), C)
    g = -jnp.exp(a_log.astype(f32)) * jax.nn.softplus(a_pre.astype(f32) + dt_bias.astype(f32))
    gc = jnp.cumsum(to_chunked_gates(g, C), axis=-1)
    causal = jnp.tril(jnp.ones((C, C), dtype=bool))
    strict = jnp.tril(jnp.ones((C, C), dtype=bool), -1)
    decay = jnp.exp(jnp.where(causal, gc[..., :, None] - gc[..., None, :], -jnp.inf))
    k_beta = k * beta[..., None]
    kk = jnp.einsum('bhncd,bhnsd->bhncs', k_beta, k) * decay
    lhs = jnp.eye(C, dtype=f32) + jnp.where(strict, kk, 0.0)
    rhs = jnp.concatenate([v * beta[..., None], k_beta * jnp.exp(gc)[..., None]], axis=-1)
    u, w = jnp.split(lax.linalg.triangular_solve(lhs, rhs, left_side=True, lower=True), 2, axis=-1)
    qk = jnp.einsum('bhncd,bhnsd->bhncs', q, k) * decay
    q_dec = q * jnp.exp(gc)[..., None]
    g_last = gc[..., -1]
    k_dec = k * jnp.exp(g_last[..., None] - gc)[..., None]

    def step(state, xs):
        u_c, w_c, qk_c, qd_c, kd_c, gl_c = xs
        v_new = u_c - jnp.einsum('bhcd,bhde->bhce', w_c, state)
        o = jnp.einsum('bhcd,bhde->bhce', qd_c, state) + jnp.einsum('bhcs,bhse->bhce', qk_c, v_new)
        state = state * jnp.exp(gl_c)[..., None, None] + jnp.einsum('bhcd,bhce->bhde', kd_c, v_new)
        return state, o

    xs = tuple(jnp.moveaxis(t, 2, 0) for t in (u, w, qk, q_dec, k_dec, g_last))
    _, o = lax.scan(step, jnp.zeros((bsz, H, D, D), f32), xs)
    o = from_scan_heads(o)
    o = rms_norm(o, norm_w) * jax.nn.silu(z.astype(f32).reshape(bsz, seq, H, D))
    return o.reshape(bsz, seq, H * D)


def mlstm(q, k, v, o_pre, i_pre, f_pre, i_bias, f_bias, norm_w):
    bsz, seq, _ = q.shape
    H, D, L = MLSTM_HEADS, HEAD_DIM, MLSTM_CHUNK
    f32 = jnp.float32
    q = to_chunked_heads(q.astype(f32), H, L)
    k = to_chunked_heads(k.astype(f32), H, L) * (D ** -0.5)
    v = to_chunked_heads(v.astype(f32), H, L)
    ig = to_chunked_gates(softcap(i_pre.astype(f32) + i_bias.astype(f32), GATE_SOFTCAP), L)
    lf = to_chunked_gates(jax.nn.log_sigmoid(softcap(f_pre.astype(f32) + f_bias.astype(f32), GATE_SOFTCAP)), L)
    b = jnp.cumsum(lf, axis=-1)
    causal = jnp.tril(jnp.ones((L, L), dtype=bool))
    dmat = jnp.where(causal, b[..., :, None] - b[..., None, :] + ig[..., None, :], -jnp.inf)
    m_intra = jnp.max(dmat, axis=-1)
    qk = jnp.einsum('bhnld,bhnsd->bhnls', q, k) * jnp.exp(dmat - m_intra[..., None])
    num_intra = jnp.einsum('bhnls,bhnse->bhnle', qk, v)
    den_intra = jnp.sum(qk, axis=-1)
    w_end = b[..., -1:] - b + ig
    m_chunk = jnp.max(w_end, axis=-1)
    e_end = jnp.exp(w_end - m_chunk[..., None])
    c_chunk = jnp.einsum('bhnl,bhnld,bhnle->bhnde', e_end, k, v)
    n_chunk = jnp.einsum('bhnl,bhnld->bhnd', e_end, k)
    b_last = b[..., -1]

    def step(carry, xs):
        c, n, m = carry
        q_c, b_c, mi_c, num_c, den_c, bl_c, mc_c, cc_c, nc_c = xs
        a = b_c + m[..., None]
        m_t = jnp.maximum(a, mi_c)
        s_inter = jnp.exp(a - m_t)
        s_intra = jnp.exp(mi_c - m_t)
        num = s_inter[..., None] * jnp.einsum('bhld,bhde->bhle', q_c, c) + s_intra[..., None] * num_c
        den = s_inter * jnp.einsum('bhld,bhd->bhl', q_c, n) + s_intra * den_c
        h = num / jnp.maximum(jnp.abs(den), jnp.exp(-m_t))[..., None]
        m_new = jnp.maximum(bl_c + m, mc_c)
        s_old = jnp.exp(bl_c + m - m_new)
        s_new = jnp.exp(mc_c - m_new)
        c = s_old[..., None, None] * c + s_new[..., None, None] * cc_c
        n = s_old[..., None] * n + s_new[..., None] * nc_c
        return (c, n, m_new), h

    xs = tuple(jnp.moveaxis(t, 2, 0) for t in (q, b, m_intra, num_intra, den_intra, b_last, m_chunk, c_chunk, n_chunk))
    init = (jnp.zeros((bsz, H, D, D), f32), jnp.zeros((bsz, H, D), f32), jnp.zeros((bsz, H), f32))
    _, h = lax.scan(step, init, xs)
    h = rms_norm(from_scan_heads(h), norm_w.reshape(H, D))
    h = h * jax.nn.sigmoid(o_pre.astype(f32).reshape(bsz, seq, H, D))
    return h.reshape(bsz, seq, H * D)


def sliding_window_attention(q, k, v, sinks, cos, sin):
    bsz, seq, _ = q.shape
    Hkv, G, D, T = SWA_KV_HEADS, SWA_GROUP, HEAD_DIM, SWA_BLOCK
    NB = seq // T
    q = apply_partial_rope(q.reshape(bsz, seq, Hkv, G, D), cos[:, :, None, None, :], sin[:, :, None, None, :])
    k = apply_partial_rope(k.reshape(bsz, seq, Hkv, D), cos[:, :, None, :], sin[:, :, None, :])
    v = v.reshape(bsz, seq, Hkv, D)
    qb = q.reshape(bsz, NB, T, Hkv, G, D)

    def band(t):
        tb = t.reshape(bsz, NB, T, Hkv, D)
        prev = jnp.pad(tb, ((0, 0), (1, 0), (0, 0), (0, 0), (0, 0)))[:, :-1]
        return jnp.concatenate([prev, tb], axis=2)

    kw, vw = band(k), band(v)
    s = jnp.einsum('bnqhgd,bnkhd->bnhgqk', qb, kw).astype(jnp.float32) * (D ** -0.5)
    qi = jnp.arange(T)[:, None] + T
    ki = jnp.arange(2 * T)[None, :]
    in_window = (ki <= qi) & (ki > qi - SWA_WINDOW)
    has_prev = (jnp.arange(NB) > 0)[:, None, None] | (ki >= T)[None]
    mask = in_window[None] & has_prev
    s = jnp.where(mask[None, :, None, None], s, -jnp.inf)
    sink = jnp.broadcast_to(sinks.astype(jnp.float32).reshape(1, 1, Hkv, G, 1, 1), s.shape[:-1] + (1,))
    probs = jax.nn.softmax(jnp.concatenate([s, sink], axis=-1), axis=-1)[..., :-1]
    out = jnp.einsum('bnhgqk,bnkhd->bnqhgd', probs.astype(v.dtype), vw)
    return out.reshape(bsz, seq, Hkv * G * D)


def setup_inputs(seed: int = 0) -> dict:
    key = jax.random.key(seed)
    ks = jax.random.split(key, 24)
    f32 = jnp.float32

    def nrm(k_, shape, scale):
        return jax.random.normal(k_, shape, f32) * scale

    def gain(k_, shape):
        return 1.0 + 0.02 * jax.random.normal(k_, shape, f32)

    x = jax.random.normal(ks[0], (BATCH, SEQ, D_MODEL), f32)
    p = jax.random.normal(ks[1], (DEPTH, BATCH, SEQ, PLE_DIM), f32)
    offsets = jax.random.randint(ks[2], (BATCH, 1), 0, 1024, dtype=jnp.int32)
    positions = offsets + jnp.arange(SEQ, dtype=jnp.int32)[None, :]
    w_in = nrm(ks[3], (DEPTH, D_MODEL, IN_COLS), D_MODEL ** -0.5)
    conv_w = nrm(ks[4], (DEPTH, CONV_WIDTH, 3 * GDN_WIDTH), CONV_WIDTH ** -0.5)
    gdn_a_log = jnp.log(jax.random.uniform(ks[5], (DEPTH, GDN_HEADS), f32, 1.0, 16.0))
    dt = jnp.exp(jax.random.uniform(ks[6], (DEPTH, GDN_HEADS), f32, np.log(1e-3), np.log(1e-1)))
    gdn_dt_bias = dt + jnp.log(-jnp.expm1(-dt))
    gdn_norm = gain(ks[7], (DEPTH, HEAD_DIM))
    mlstm_i_bias = nrm(ks[8], (DEPTH, MLSTM_HEADS), 0.1)
    mlstm_f_bias = jax.random.uniform(ks[9], (DEPTH, MLSTM_HEADS), f32, 3.0, 6.0)
    mlstm_norm = gain(ks[10], (DEPTH, MLSTM_WIDTH))
    attn_sinks = nrm(ks[11], (DEPTH, SWA_Q_HEADS), 0.5)
    w_out = nrm(ks[12], (DEPTH, MIX_WIDTH, D_MODEL), MIX_WIDTH ** -0.5)
    norm_mix = gain(ks[13], (DEPTH, D_MODEL))
    norm_mlp = gain(ks[14], (DEPTH, D_MODEL))
    w_up = nrm(ks[15], (DEPTH, D_MODEL, D_FF), D_MODEL ** -0.5)
    w_down = nrm(ks[16], (DEPTH, D_FF, D_MODEL), D_FF ** -0.5)
    norm_ple = gain(ks[17], (DEPTH, D_MODEL))
    w_ple_gate = nrm(ks[18], (DEPTH, D_MODEL, D_MODEL), D_MODEL ** -0.5)
    w_ple_proj = nrm(ks[19], (DEPTH, PLE_DIM, D_MODEL), PLE_DIM ** -0.5)
    norm_final = gain(ks[20], (D_MODEL,))
    return {'x': x, 'p': p, 'positions': positions, 'w_in': w_in, 'conv_w': conv_w,
            'gdn_a_log': gdn_a_log, 'gdn_dt_bias': gdn_dt_bias, 'gdn_norm': gdn_norm,
            'mlstm_i_bias': mlstm_i_bias, 'mlstm_f_bias': mlstm_f_bias, 'mlstm_norm': mlstm_norm,
            'attn_sinks': attn_sinks, 'w_out': w_out, 'norm_mix': norm_mix, 'norm_mlp': norm_mlp,
            'w_up': w_up, 'w_down': w_down, 'norm_ple': norm_ple, 'w_ple_gate': w_ple_gate,
            'w_ple_proj': w_ple_proj, 'norm_final': norm_final}


def reference(x, p, positions, w_in, conv_w, gdn_a_log, gdn_dt_bias, gdn_norm,
              mlstm_i_bias, mlstm_f_bias, mlstm_norm, attn_sinks, w_out, norm_mix, norm_mlp,
              w_up, w_down, norm_ple, w_ple_gate, w_ple_proj, norm_final):
    cos, sin = rope_tables(positions)
    for i in range(DEPTH):
        h = rms_norm(x, norm_mix[i])
        proj = jnp.einsum('bsd,dc->bsc', h, w_in[i])
        (gq, gk, gv, gz, gb, ga, mq, mk, mv, mo, mi, mf, sq, sk, sv) = split_cols(proj, IN_SPLITS)
        y_a = gated_delta_net(gq, gk, gv, gz, gb, ga, conv_w[i], gdn_a_log[i], gdn_dt_bias[i], gdn_norm[i])
        y_b = mlstm(mq, mk, mv, mo, mi, mf, mlstm_i_bias[i], mlstm_f_bias[i], mlstm_norm[i])
        y_c = sliding_window_attention(sq, sk, sv, attn_sinks[i], cos, sin)
        y = jnp.concatenate([y_a.astype(x.dtype), y_b.astype(x.dtype), y_c.astype(x.dtype)], axis=-1)
        x = x + jnp.einsum('bsc,cd->bsd', y, w_out[i])
        u = jnp.einsum('bsd,df->bsf', rms_norm(x, norm_mlp[i]), w_up[i])
        x = x + jnp.einsum('bsf,fd->bsd', jnp.square(jax.nn.relu(u)), w_down[i])
        gate = jax.nn.sigmoid(jnp.einsum('bsd,de->bse', rms_norm(x, norm_ple[i]), w_ple_gate[i]))
        x = x + gate * jnp.einsum('bsk,kd->bsd', p[i], w_ple_proj[i])
    return rms_norm(x, norm_final)
```

```python
import contextlib
import os
import numpy as np
import concourse.bass as bass
import concourse.mybir as mybir
from concourse.bass_utils import run_bass_kernel_spmd
import ml_dtypes

F32 = mybir.dt.float32
BF16 = mybir.dt.bfloat16
I32 = mybir.dt.int32
AF = mybir.ActivationFunctionType
ALU = mybir.AluOpType
AX = mybir.AxisListType


class Buf:
    __slots__ = ("name", "w", "r", "dsem", "dcnt", "kind")

    def __init__(self, name, kind="sb"):
        self.name = name
        self.kind = kind
        self.w = None
        self.r = {}
        self.dsem = None
        self.dcnt = 0


class V:
    __slots__ = ("ap", "bufs")

    def __init__(self, ap, bufs):
        self.ap = ap
        self.bufs = bufs

    def __getitem__(self, key):
        return V(self.ap[key], self.bufs)

    def re(self, s, **kw):
        return V(self.ap.rearrange(s, **kw), self.bufs)

    def bc(self, shape):
        return V(self.ap.to_broadcast(shape), self.bufs)

    def bitcast(self, dt):
        return V(self.ap.bitcast(dt), self.bufs)

    @property
    def shape(self):
        return self.ap.shape


class Sched:
    def __init__(self, nc, es):
        self.nc = nc
        self.es = es
        self.sems = {}
        self.cnt = {}
        self.known = {}
        self.eng = {"pe": nc.tensor, "act": nc.scalar, "dve": nc.vector,
                    "pool": nc.gpsimd, "sp": nc.sync}
        for k in self.eng:
            self.sems[k] = es.enter_context(nc.semaphore("s_" + k))
            self.cnt[k] = 0
            self.known[k] = {}
        self.ndsem = 0
        self.pe_pending = False
        self.out_events = []
        self.ninstr = 0

    def sb(self, name, shape, dt):
        t = self.es.enter_context(self.nc.sbuf_tensor(name, list(shape), dt))
        return V(t[:], [Buf(name)])

    def ps(self, name, shape, dt):
        t = self.es.enter_context(self.nc.psum_tensor(name, list(shape), dt))
        return V(t[:], [Buf(name, "ps")])

    def dram(self, ap, name):
        return V(ap, [Buf(name, "dram")])

    def _dsem(self, buf):
        if buf.dsem is None:
            key = "d%d" % self.ndsem
            self.ndsem += 1
            self.sems[key] = self.es.enter_context(self.nc.semaphore(key))
            buf.dsem = key
        return buf.dsem

    def _deps(self, ek, reads, writes):
        deps = {}

        def add(ev):
            if ev is None:
                return
            k, v = ev
            if deps.get(k, 0) < v:
                deps[k] = v
        for b in reads:
            add(b.w)
            if b.kind == "ps":
                for k, v in b.r.items():
                    if k != ek:
                        add((k, v))
        for b in writes:
            add(b.w)
            for k, v in b.r.items():
                add((k, v))
        kn = self.known[ek]
        for k, v in deps.items():
            if k == "pe" and ek == "pe":
                continue
            if kn.get(k, 0) >= v:
                continue
            assert not (k == "pe" and v > self.cnt["pe"]), "wait on unsignalled PE event (deadlock)"
            self.eng[ek].wait_ge(self.sems[k], v)
            kn[k] = v

    def _commit(self, ev, reads, writes):
        k, v = ev
        for b in writes:
            b.w = ev
            b.r = {}
        for b in reads:
            if b.r.get(k, 0) < v:
                b.r[k] = v

    def op(self, ek, fn, outs, ins, signal=True):
        reads = [b for v in ins for b in v.bufs]
        writes = [b for v in outs for b in v.bufs]
        self._deps(ek, reads, writes)
        ins_ = fn()
        self.ninstr += 1
        if ek == "pe" and not signal:
            ev = ("pe", self.cnt["pe"] + 1)
            self.pe_pending = True
        else:
            self.cnt[ek] += 1
            ev = (ek, self.cnt[ek])
            ins_.then_inc(self.sems[ek], 1)
            if ek == "pe":
                self.pe_pending = False
        self._commit(ev, reads, writes)
        return ev

    def dma(self, qk, out, in_, cast=False, **kw):
        reads = list(in_.bufs)
        writes = list(out.bufs)
        self._deps(qk, reads, writes)
        owner = writes[0]
        if owner.kind == "dram" and reads and reads[0].kind != "dram":
            owner = reads[0]
        key = self._dsem(owner)
        owner.dcnt += 16
        ev = (key, owner.dcnt)
        self.eng[qk].dma_start(out=out.ap, in_=in_.ap, **kw).then_inc(self.sems[key], 16)
        self.ninstr += 1
        self._commit(ev, reads, writes)
        return ev

    def handoff(self, old, new):
        evs = {}
        for v in old:
            for b in v.bufs:
                if b.w is not None:
                    evs[b.w[0]] = max(evs.get(b.w[0], 0), b.w[1])
                for k, val in b.r.items():
                    evs[k] = max(evs.get(k, 0), val)
        for v in new:
            for b in v.bufs:
                b.w = None
                b.r = dict(evs)

    def finish(self, evs):
        for k, v in evs:
            self.eng["sp"].wait_ge(self.sems[k], v)

    def mm(self, out, lhsT, rhs, start=True, stop=True, signal=None, **kw):
        if signal is None:
            signal = stop
        rows = lhsT.ap.shape[0]
        rg = (lhsT.ap.base_partition(), rows) if rows < 128 else None
        if rg is not None:
            signal = True
            last = getattr(self, "last_rg", None)
            if last is not None and last != rg and self.cnt["pe"] > 0:
                self.nc.tensor.wait_ge(self.sems["pe"], self.cnt["pe"])
        self.last_rg = rg
        return self.op("pe", lambda: self.nc.tensor.matmul(
            out.ap, lhsT=lhsT.ap, rhs=rhs.ap, start=start, stop=stop, **kw),
            [out], [lhsT, rhs], signal=signal)

    def tr(self, out, in_, ident, signal=True):
        self.last_rg = None
        return self.op("pe", lambda: self.nc.tensor.transpose(out.ap, in_.ap, ident.ap),
                       [out], [in_, ident], signal=signal)

    def act(self, out, in_, func, bias=None, scale=1.0, accum=None, eng="act"):
        ins = [in_]
        kw = {}
        if bias is not None:
            if isinstance(bias, V):
                ins.append(bias)
                kw["bias"] = bias.ap
            else:
                kw["bias"] = bias
        if isinstance(scale, V):
            ins.append(scale)
            kw["scale"] = scale.ap
        else:
            kw["scale"] = scale
        outs = [out]
        if accum is not None:
            outs.append(accum)
            kw["accum_out"] = accum.ap
        return self.op("act", lambda: self.nc.scalar.activation(
            out=out.ap, in_=in_.ap, func=func, **kw), outs, ins)

    pool_busy = False

    def _ve(self, eng):
        return self.nc.vector if eng == "dve" else self.nc.gpsimd

    def _rm(self, eng):
        return "dve" if (eng == "pool" and self.pool_busy) else eng

    def tt(self, out, a, b, op, eng="dve"):
        eng = self._rm(eng)
        return self.op(eng, lambda: self._ve(eng).tensor_tensor(
            out=out.ap, in0=a.ap, in1=b.ap, op=op), [out], [a, b])

    def ts(self, out, a, s1, op0, s2=None, op1=None, eng="dve", accum=None):
        eng = self._rm(eng)
        ins = [a]
        s1a = s1.ap if isinstance(s1, V) else s1
        s2a = s2.ap if isinstance(s2, V) else s2
        if isinstance(s1, V):
            ins.append(s1)
        if isinstance(s2, V):
            ins.append(s2)
        kw = {}
        if op1 is not None:
            kw["op1"] = op1
        outs = [out]
        if accum is not None:
            outs.append(accum)
            kw["accum_out"] = accum.ap
        return self.op(eng, lambda: self._ve(eng).tensor_scalar(
            out=out.ap, in0=a.ap, scalar1=s1a, scalar2=s2a, op0=op0, **kw), outs, ins)

    def stt(self, out, a, s, b, op0, op1, eng="dve"):
        eng = "dve"
        ins = [a, b]
        sa = s.ap if isinstance(s, V) else s
        if isinstance(s, V):
            ins.append(s)
        return self.op(eng, lambda: self._ve(eng).scalar_tensor_tensor(
            out=out.ap, in0=a.ap, scalar=sa, in1=b.ap, op0=op0, op1=op1), [out], ins)

    def copy(self, out, in_, eng="dve"):
        eng = self._rm(eng)
        if eng == "act":
            return self.op("act", lambda: self.nc.scalar.copy(out=out.ap, in_=in_.ap), [out], [in_])
        return self.op(eng, lambda: self._ve(eng).tensor_copy(out=out.ap, in_=in_.ap), [out], [in_])

    def memset(self, out, val, eng="dve"):
        eng = self._rm(eng)
        return self.op(eng, lambda: self._ve(eng).memset(out.ap, val), [out], [])

    def reduce(self, out, in_, op, axis=AX.X, eng="dve"):
        eng = self._rm(eng)
        return self.op(eng, lambda: self._ve(eng).tensor_reduce(
            out=out.ap, in_=in_.ap, axis=axis, op=op), [out], [in_])

    def recip(self, out, in_):
        return self.op("dve", lambda: self.nc.vector.reciprocal(out=out.ap, in_=in_.ap), [out], [in_])


DM = 1024
TT = 512
NEG = -30000.0
EPS = 1e-6
NSLOT = 3
NCHUNK = 30
G_MIX, G_MLP, G_PLE, G_FIN = 0, 32, 64, 96
P_CONV, P_GDNN, P_MLN, P_ALOG, P_DTB, P_IB, P_FB, P_SINK, P_C1, P_SGN = 104, 200, 204, 212, 228, 244, 260, 276, 308, 309
NPAR = 310
C_ID, C_BD, C_U, C_SU, C_MT, C_MCS, C_STR, C_E, C_O, C_SW, C_SW0 = 0, 128, 256, 384, 512, 640, 768, 896, 1024, 1152, 1408
NCST = 1664


def host_tables(inp):
    par = np.zeros((128, NPAR), np.float32)
    p = np.arange(128)
    for l in range(4):
        for kt in range(8):
            par[:, G_MIX + l * 8 + kt] = inp["norm_mix"][l, kt * 128:(kt + 1) * 128]
            par[:, G_MLP + l * 8 + kt] = inp["norm_mlp"][l, kt * 128:(kt + 1) * 128]
            par[:, G_PLE + l * 8 + kt] = inp["norm_ple"][l, kt * 128:(kt + 1) * 128]
        for m in range(6):
            for j in range(4):
                par[:, P_CONV + l * 24 + m * 4 + j] = inp["conv_w"][l, j, m * 128:(m + 1) * 128]
        par[:, P_GDNN + l] = inp["gdn_norm"][l, p % 64]
        for pr in range(2):
            par[:, P_MLN + l * 2 + pr] = inp["mlstm_norm"][l, pr * 128:(pr + 1) * 128]
        for h in range(4):
            par[:, P_ALOG + l * 4 + h] = inp["gdn_a_log"][l, h]
            par[:, P_DTB + l * 4 + h] = inp["gdn_dt_bias"][l, h]
            par[:, P_IB + l * 4 + h] = inp["mlstm_i_bias"][l, h]
            par[:, P_FB + l * 4 + h] = inp["mlstm_f_bias"][l, h]
        for i in range(4):
            for g in range(2):
                par[:, P_SINK + l * 8 + i * 2 + g] = inp["attn_sinks"][l, i + 4 * g]
    for kt in range(8):
        par[:, G_FIN + kt] = inp["norm_final"][kt * 128:(kt + 1) * 128]
    inv_freq = (500000.0 ** (-np.arange(0, 16, 2, dtype=np.float32) / 16)).astype(np.float32)
    d = p % 64
    par[:, P_C1] = np.where(d < 16, inv_freq[d % 8] / (2 * np.pi), 0.0)
    par[:, P_SGN] = np.where(d < 8, -1.0, np.where(d < 16, 1.0, 0.0))
    cst = np.zeros((128, NCST), np.float32)
    a = p[:, None]
    b = p[None, :]
    same = (a // 64) == (b // 64)
    cst[:, C_ID:C_ID + 128] = (a == b)
    cst[:, C_BD:C_BD + 128] = same
    cst[:, C_U:C_U + 128] = same & (a <= b)
    cst[:, C_SU:C_SU + 128] = same & (a > b)
    cst[:, C_MT:C_MT + 128] = np.where(same & (a <= b), 0.0, NEG)
    cst[:, C_MCS:C_MCS + 128] = np.where(same & (b <= a), 0.0, NEG)
    cst[:, C_STR:C_STR + 128] = same & (b < a)
    cst[:, C_E:C_E + 128] = (b < 64)
    cst[:, C_O:C_O + 128] = (b >= 64)
    k = np.arange(256)[None, :]
    ok = (k > a) & (k <= a + 128)
    cst[:, C_SW:C_SW + 256] = np.where(ok, 0.0, NEG)
    cst[:, C_SW0:C_SW0 + 256] = np.where(ok & (k >= 128), 0.0, NEG)
    return par, cst


def build(NT, NL):
    STOP = int(os.environ.get('MK_STOP', '99'))
    XQ = os.environ.get('MK_XQ', 'act')
    SKIP = os.environ.get('MK_SKIP', '')
    SWA_ST = int(os.environ.get('MK_SWA', '9'))
    SEQ = NT * TT
    nc = bass.Bass("TRN2", target_bir_lowering=False)
    es = contextlib.ExitStack()
    with es:
        S = Sched(nc, es)
        mult, add, sub, mx_, mn_ = ALU.mult, ALU.add, ALU.subtract, ALU.max, ALU.min

        def din(name, shape, dt=F32):
            return S.dram(nc.dram_tensor(name, list(shape), dt, kind="ExternalInput").ap(), name)
        x_d = din("x", [SEQ, DM])
        p_d = din("p", [4, SEQ, 256])
        pos_d = din("pos", [1, SEQ], I32)
        w_in = din("w_in", [4, DM, 2832])
        w_out = din("w_out", [4, DM, DM])
        w_up = din("w_up", [4, DM, 4096])
        w_dn = din("w_down", [4, 4096, DM])
        w_gt = din("w_gate", [4, DM, DM])
        w_pj = din("w_proj", [4, 256, DM])
        par_d = din("par", [128, NPAR])
        cst_d = din("cst", [128, NCST])
        out_d = S.dram(nc.dram_tensor("out", [SEQ, DM], F32, kind="ExternalOutput").ap(), "out")
        wsc_ap = nc.dram_tensor("wsc", [4, NCHUNK, 128, 4096], BF16, kind="Internal").ap()
        GROUPS = {"in": range(0, 8), "out": range(8, 10), "up": range(10, 18),
                  "dn": range(18, 26), "pl": range(26, 29)}
        wbuf = [{g: Buf("w%d%s" % (l, g), "dram") for g in GROUPS} for l in range(4)]

        def wsc(l, c):
            for g, r in GROUPS.items():
                if c in r:
                    return V(wsc_ap[l, c], [wbuf[l][g]])

        par = S.sb("par_sb", [128, NPAR], F32)
        cst = S.sb("cst_sb", [128, NCST], F32)
        xT = [S.sb("xT%d" % k, [128, TT], F32) for k in range(8)]
        hnbuf = es.enter_context(nc.sbuf_tensor("hnbuf", [128, 2048], F32))
        hnbuf_ap = hnbuf[:]
        hn = [V(hnbuf_ap[:, k * 256:(k + 1) * 256].bitcast(BF16), [Buf("hn%d" % k)]) for k in range(8)]
        sqb = [S.sb("sqb%d" % k, [128, TT], BF16) for k in range(2)]
        rstd = S.sb("rstd", [128, TT], F32)
        tmpA = S.sb("tmpA", [128, TT], F32)
        tmpB = S.sb("tmpB", [128, TT], F32)
        slots = [S.sb("slot%d" % k, [128, 4096], BF16) for k in range(NSLOT)]
        identb = S.sb("identb", [128, 128], BF16)
        onesb = S.sb("onesb", [128, 128], BF16)
        negonesb = S.sb("negonesb", [128, 128], BF16)
        bdonesb = S.sb("bdonesb", [128, 128], BF16)
        maskTb = S.sb("maskTb", [128, 128], BF16)
        maskCSb = S.sb("maskCSb", [128, 128], BF16)
        onesEb = S.sb("onesEb", [128, 128], BF16)
        onesOb = S.sb("onesOb", [128, 128], BF16)
        nA = S.sb("nA", [128, 16], F32)
        negsink = S.sb("negsink", [128, 32], F32)
        raw = [S.sb("raw%d" % m, [128, TT + 3], F32) for m in range(6)]
        qkvc = [S.sb("qkvc%d" % m, [128, TT], F32) for m in range(6)]
        gzs = [S.sb("gzs%d" % m, [128, TT], BF16) for m in range(2)]
        mos = [S.sb("mos%d" % m, [128, TT], BF16) for m in range(2)]
        ov2 = es.enter_context(nc.sbuf_tensor("ov2", [128, 2048], F32))
        ov2_ap = ov2[:]
        mqf = [V(ov2_ap[:, m * 512:(m + 1) * 512], [Buf("mqf%d" % m)]) for m in range(2)]
        mkf = [V(ov2_ap[:, 1024 + m * 512:1024 + (m + 1) * 512], [Buf("mkf%d" % m)]) for m in range(2)]
        sqf = [S.sb("sqf%d" % m, [128, TT], BF16) for m in range(4)]
        skf = S.sb("skf", [128, 128 + TT], BF16)
        ropeC = S.sb("ropeC", [128, TT], F32)
        ropeS = S.sb("ropeS", [128, TT], F32)
        posi = S.sb("posi", [128, TT], I32)
        mk_tm = S.sb("mk_tm", [128, 4, 256], F32)
        mvpad = S.sb("mvpad", [128, 4, 2, 2, 128], BF16)
        svpad = S.sb("svpad", [128, 5, 2, 128], BF16)
        graw = S.sb("graw", [128, 4, 16], F32)
        yA = [S.sb("yA%d" % m, [128, TT], BF16) for m in range(4)]
        ysw = S.sb("ysw", [128, 4, TT], BF16)
        oT = S.sb("oT", [128, 2, TT], F32)
        pin = S.sb("pin", [128, 4, 256], F32)
        pT = [S.sb("pT%d" % k, [128, TT], BF16) for k in range(2)]
        xin = S.sb("xin", [128, DM], F32)
        Sbd = [[S.sb("Sbd%d_%d" % (l, pr), [128, 128], F32) for pr in range(2)] for l in range(NL)]
        CN = [[S.sb("CN%d_%d" % (l, pr), [128, 256], F32) for pr in range(2)] for l in range(NL)]
        CNb = [[S.sb("CNb%d_%d" % (l, pr), [128, 256], BF16) for pr in range(2)] for l in range(NL)]
        ctail = [S.sb("ctail%d" % l, [128, 6, 3], F32) for l in range(NL)]
        kprev = [S.sb("kprev%d" % l, [128, 128], BF16) for l in range(NL)]
        vprev = [S.sb("vprev%d" % l, [128, 2, 128], BF16) for l in range(NL)]
        g_ig = S.sb("g_ig", [128, 4, 4], F32)
        g_lf = S.sb("g_lf", [128, 4, 4], F32)
        g_b16 = S.sb("g_b16", [128, 4, 4], BF16)
        g_beta = S.sb("g_beta", [128, 4, 4], F32)
        g_g = S.sb("g_g", [128, 4, 4], F32)
        g_t = S.sb("g_t", [128, 4, 4], F32)
        g_t2 = S.sb("g_t2", [128, 4, 4], F32)
        g_b162 = S.sb("g_b162", [128, 4, 4], BF16)
        st8 = [S.sb("st8_%d" % k, [128, 8], F32) for k in range(6)]
        st4 = [S.sb("st4_%d" % k, [128, 4], F32) for k in range(4)]
        dsel = S.sb("dsel", [128, 2, 2], F32)
        dsel2 = S.sb("dsel2", [128, 2, 2], F32)
        st4b = [S.sb("st4b_%d" % k, [128, 4], F32) for k in range(4)]
        oT2 = S.sb("oT2", [128, 2, TT], F32)
        ARN = 9216
        arena = es.enter_context(nc.sbuf_tensor("arena", [128, ARN], F32))
        arena_ap = arena[:]

        class Carver:
            def __init__(self, tag, base=None, size=None):
                self.off = 0
                self.tag = tag
                self.views = []
                self.base = arena_ap if base is None else base
                self.size = ARN if size is None else size

            def f32(self, shape):
                n = int(np.prod(shape[1:]))
                ap = self.base[:, self.off:self.off + n]
                self.off += n
                assert self.off <= self.size, (self.tag, self.off)
                v = V(ap, [Buf("%s%d" % (self.tag, len(self.views)))])
                self.views.append(v)
                if len(shape) == 3:
                    return v.re("p (a b) -> p a b", a=shape[1])
                return v

            def bf16(self, shape):
                n = int(np.prod(shape[1:]))
                assert n % 2 == 0
                ap = self.base[:, self.off:self.off + n // 2].bitcast(BF16)
                self.off += n // 2
                assert self.off <= self.size, (self.tag, self.off)
                v = V(ap, [Buf("%s%d" % (self.tag, len(self.views)))])
                self.views.append(v)
                if len(shape) == 3:
                    return v.re("p (a b) -> p a b", a=shape[1])
                return v
        cU = Carver("u")
        u = [cU.bf16([128, TT]) for _ in range(32)]
        cH = Carver("h")
        hn32 = [cH.f32([128, TT]) for _ in range(8)]
        w32 = cH.f32([128, 8, 512])
        cR = Carver("r")
        sqt1 = [cR.f32([128, TT]) for _ in range(4)]
        cW = Carver("w", base=hnbuf_ap, size=2048)
        sm = [cW.f32([128, 2, 256]) for _ in range(2)]
        Pn = [cW.bf16([128, 2, 256]) for _ in range(2)]
        PTs = cW.bf16([128, 8, 128])
        cM = Carver("gm")
        cG = cM
        mG1 = cM.bf16([128, 4, 128])
        mEb = cM.f32([128, 4, 128])
        mPT = cM.f32([128, 4, 128])
        mW = cM.bf16([128, 4, 128])
        mqdec = [cM.bf16([128, 128]) for _ in range(2)]
        mkdec = cM.bf16([128, 4, 128])
        mdab = cM.f32([128, 2, 64])
        gG1 = cG.bf16([128, 4, 128])
        gEg = cG.f32([128, 4, 128])
        gdec = cG.f32([128, 4, 128])
        gA = cG.f32([128, 4, 128])
        gL = cG.f32([128, 4, 128])
        gQKd = cG.f32([128, 4, 128])
        gMt = cG.f32([128, 4, 128])
        gQKdT = cG.f32([128, 4, 128])
        gLp = [gL, gdec]
        gMp = [gMt, gA]
        gP = gQKd
        gkbg = cG.f32([128, 4, 128])
        gkdec = cG.f32([128, 4, 128])
        gvb = cG.f32([128, 4, 64])
        gu = cG.f32([128, 256])
        gwT = cG.f32([128, 2, 128])
        gqdT = [cG.f32([128, 128]) for _ in range(2)]
        gvnE = [cG.f32([128, 128]) for _ in range(2)]
        gvnO = [cG.f32([128, 128]) for _ in range(2)]
        arena_user = [None]

        def arena_switch(c):
            if arena_user[0] is not None and arena_user[0] is not c:
                S.handoff(arena_user[0].views, c.views)
            arena_user[0] = c

        pd = [S.ps("pd%d" % k, [128, 512], F32) for k in range(4)]
        pm = [S.ps("pm%d" % k, [128, 512], F32) for k in range(4)]
        rot = {"d": 0, "m": 0}

        def pdn():
            rot["d"] = (rot["d"] + 1) % 4
            return pd[rot["d"]]

        def pmn():
            rot["m"] = (rot["m"] + 1) % 4
            return pm[rot["m"]]

        rot3 = {"m": 0}

        def pmn3():
            rot3["m"] = rot3["m"] % 3 + 1
            return pm[rot3["m"]]

        def mkrot(banks):
            st_ = {"i": -1}

            def nxt():
                st_["i"] = (st_["i"] + 1) % len(banks)
                return banks[st_["i"]]
            return nxt
        rs = mkrot([pd[2], pd[3]])
        rm = mkrot([pm[3], pd[0], pd[1]])
        rg = mkrot([pm[1], pm[2]])

        alt = {"e": 0}

        def evac_eng():
            alt["e"] ^= 1
            return "act" if alt["e"] else "dve"

        def evac(out, in_):
            S.copy(out, in_, eng=evac_eng())

        def cs(off, n=128):
            return cst[:, off:off + n]

        def pc(col, n=1):
            return par[:, col:col + n]

        uses = []
        for t in range(NT):
            for l in range(NL):
                for c in [1] + list(range(3, 29)):
                    uses.append((l, c))
        wstate = {"next_use": 0, "issued": 0}

        def issue_w():
            i = wstate["issued"]
            if i >= len(uses):
                return
            l, c = uses[i]
            n_ = {7: 1152, 26: 2048}.get(c, 4096)
            S.dma("sp", slots[i % NSLOT][:, 0:n_], wsc(l, c)[:, 0:n_])
            wstate["issued"] = i + 1

        def use_w(l, c):
            i = wstate["next_use"]
            assert uses[i] == (l, c), (uses[i], l, c)
            assert wstate["issued"] > i
            wstate["next_use"] = i + 1
            return slots[i % NSLOT]

        def done_w():
            issue_w()

        CASTSEL = os.environ.get('MK_CAST', 'all')
        ncast = [0]

        def castdma(dst, src):
            ncast[0] += 1
            if CASTSEL != 'all':
                lo, hi = [int(v) for v in CASTSEL.split(':')]
                if not (lo <= ncast[0] - 1 < hi):
                    return
            S.dma("pool", dst, src)

        def wview(l, c, kt, ncol):
            return wsc(l, c)[:, 0:kt * ncol].re("p (k c) -> p k c", k=kt)

        def srcv(w, l, r0, nrow, c0, ncol):
            return V(w.ap[l, r0:r0 + nrow, c0:c0 + ncol], w.bufs).re("(k p) c -> p k c", p=128)

        def emit_casts(l):
            castdma(wview(l, 1, 8, 512), srcv(w_in, l, 0, DM, 512, 512))
            c3 = wview(l, 3, 8, 512)
            castdma(c3[:, :, 0:256], srcv(w_in, l, 0, DM, 1800, 256))
            castdma(c3[:, :, 256:384], srcv(w_in, l, 0, DM, 2576, 128))
            skr = c3[:, :, 384:512].re("p k (g d) -> p k g d", g=2)
            sks = srcv(w_in, l, 0, DM, 2576, 128).re("p k (g d) -> p k g d", g=2)
            for g in range(2):
                castdma(skr[:, :, g, 0:8], sks[:, :, g, 8:16])
                castdma(skr[:, :, g, 8:16], sks[:, :, g, 0:8])
                castdma(skr[:, :, g, 16:64], sks[:, :, g, 16:64])
            c4 = wview(l, 4, 8, 512).re("p k (i g d) -> p k i g d", i=4, g=2)
            c5 = wview(l, 5, 8, 512).re("p k (i g d) -> p k i g d", i=4, g=2)
            for g in range(2):
                sv_ = srcv(w_in, l, 0, DM, 2064 + g * 256, 256).re("p k (i d) -> p k i d", i=4)
                for i in range(4):
                    castdma(c4[:, :, i, g, :], sv_[:, :, i, :])
                    castdma(c5[:, :, i, g, 0:8], sv_[:, :, i, 8:16])
                    castdma(c5[:, :, i, g, 8:16], sv_[:, :, i, 0:8])
                    castdma(c5[:, :, i, g, 16:64], sv_[:, :, i, 16:64])
            castdma(wview(l, 6, 8, 512), srcv(w_in, l, 0, DM, 1288, 512))
            c7 = wview(l, 7, 8, 144)
            castdma(c7[:, :, 0:128], srcv(w_in, l, 0, DM, 2704, 128))
            castdma(c7[:, :, 128:136], srcv(w_in, l, 0, DM, 1024, 8))
            castdma(c7[:, :, 136:144], srcv(w_in, l, 0, DM, 2056, 8))
            for c in range(2):
                dv = wview(l, 8 + c, 8, 512)
                castdma(dv[:, 0:4, :], srcv(w_out, l, 0, 512, c * 512, 512))
                for i in range(4):
                    for g in range(2):
                        r0 = 512 + (i + 4 * g) * 64
                        castdma(V(dv.ap[g * 64:(g + 1) * 64, 4 + i, :], dv.bufs),
                                V(w_out.ap[l, r0:r0 + 64, c * 512:(c + 1) * 512], w_out.bufs))
            for c in range(8):
                castdma(wview(l, 10 + c, 8, 512), srcv(w_up, l, 0, DM, c * 512, 512))
            for c in range(8):
                castdma(wview(l, 18 + c, 32, 128), srcv(w_dn, l, 0, 4096, c * 128, 128))
            castdma(wview(l, 26, 2, 1024), srcv(w_pj, l, 0, 256, 0, 1024))
            for c in range(2):
                castdma(wview(l, 27 + c, 8, 512), srcv(w_gt, l, 0, DM, c * 512, 512))

        S.dma("sp", par, par_d)
        S.dma("sp", cst, cst_d)
        S.pool_busy = True
        if 'c' not in SKIP:
            emit_casts(0)
        S.copy(identb, cs(C_ID))
        S.memset(onesb, 1.0)
        S.memset(negonesb, -1.0)
        S.copy(bdonesb, cs(C_BD))
        S.copy(maskTb, cs(C_MT))
        S.copy(maskCSb, cs(C_MCS))
        S.copy(onesEb, cs(C_E))
        S.copy(onesOb, cs(C_O))
        S.act(nA, par[:, P_ALOG:P_ALOG + 16], AF.Exp)
        S.ts(nA, nA, -1.0, mult)
        S.ts(negsink, par[:, P_SINK:P_SINK + 32], -1.0, mult)
        S.memset(mvpad, 0.0)
        S.memset(svpad, 0.0)
        for l in range(NL):
            for pr in range(2):
                S.memset(Sbd[l][pr], 0.0)
                S.memset(CN[l][pr], 0.0)
                S.memset(CNb[l][pr], 0.0, eng="pool")
            S.memset(ctail[l], 0.0)
            S.memset(kprev[l], 0.0)
            S.memset(vprev[l], 0.0, eng="pool")
        for _ in range(NSLOT if 'w' not in SKIP else 0):
            issue_w()

        Uv = cst[:, C_U:C_U + 128].re("p (o c) -> p o c", o=1).bc([128, 4, 128])

        def rsqrt_from(ps, scale, out):
            S.act(out, ps, AF.Ln, bias=EPS, scale=scale)
            S.act(out, out, AF.Exp, scale=-0.5)

        def rmsnorm(gbase, outs, inplace=False, outs32=None):
            ps = pdn()
            for kt in range(8):
                S.act(sqb[kt % 2], xT[kt], AF.Square)
                S.mm(ps, onesb, sqb[kt % 2], start=(kt == 0), stop=(kt == 7), signal=True)
            rsqrt_from(ps, 1.0 / DM, rstd)
            for kt in range(8):
                if outs32 is not None:
                    S.stt(outs32[kt], xT[kt], pc(gbase + kt), rstd, mult, mult)
                    S.copy(outs[kt], outs32[kt], eng="act")
                else:
                    S.stt(outs[kt], xT[kt], pc(gbase + kt), rstd, mult, mult)

        def dense_fm(slot, kt_n, ncol, mi, rhs_list):
            ps = pdn()
            sv = slot[:, 0:kt_n * ncol].re("p (k c) -> p k c", k=kt_n)
            for k in range(kt_n):
                S.mm(ps, sv[:, k, mi * 128:(mi + 1) * 128], rhs_list[k],
                     start=(k == 0), stop=(k == kt_n - 1))
            return ps

        def headnorm(src, gcol, gate, out):
            S.act(sqb[0], src, AF.Square)
            ps = pmn()
            S.mm(ps, bdonesb, sqb[0])
            rsqrt_from(ps, 1.0 / 64, tmpA)
            S.stt(tmpB, src, pc(gcol), tmpA, mult, mult)
            S.tt(out, tmpB, gate, mult, eng="pool")

        out_events = []
        for t in range(NT):
            tok0 = t * TT
            S.pool_busy = (t == 0)
            for j in range(4 if 'x' not in SKIP else 0):
                S.dma(XQ, xin, x_d[tok0 + j * 128: tok0 + (j + 1) * 128, :])
                for half in range(2):
                    ps = pdn()
                    for q in range(4):
                        kt = half * 4 + q
                        S.tr(ps[:, q * 128:(q + 1) * 128], xin[:, kt * 128:(kt + 1) * 128], cs(C_ID),
                             signal=(q == 3))
                    for q in range(4):
                        kt = half * 4 + q
                        evac(xT[kt][:, j * 128:(j + 1) * 128], ps[:, q * 128:(q + 1) * 128])
            if 'r' not in SKIP:
              S.dma("pool", posi, V(pos_d.ap[0:1, tok0:tok0 + TT].partition_broadcast(128), pos_d.bufs))
              S.copy(tmpA, posi)
              for which, dst in ((0, ropeC), (1, ropeS)):
                  S.ts(tmpB, tmpA, pc(P_C1), mult)
                  if which == 0:
                      S.ts(tmpB, tmpB, 0.25, add)
                  S.copy(posi, tmpB)
                  S.copy(dst, posi)
                  S.tt(tmpB, tmpB, dst, sub)
                  S.stt(tmpB, tmpB, 0.5, tmpB, ALU.is_gt, sub)
                  S.act(dst, tmpB, AF.Sin, scale=-6.28318)
              S.ts(ropeS, ropeS, pc(P_SGN), mult)

            for l in range(NL):
                if STOP <= 0:
                    break
                if t == 0 and l + 1 < NL:
                    emit_casts(l + 1)
                S.dma(XQ, pin, V(p_d.ap[l, tok0:tok0 + TT, :], p_d.bufs).re("(j p) c -> p j c", p=128))
                for k2 in range(2):
                    ps = pdn()
                    for j in range(4):
                        S.tr(ps[:, j * 128:(j + 1) * 128], pin[:, j, k2 * 128:(k2 + 1) * 128], cs(C_ID),
                             signal=(j == 3))
                    evac(pT[k2], ps)

                arena_switch(cH)
                rmsnorm(G_MIX + l * 8, hn, outs32=hn32)
                for m in range(6):
                    S.copy(raw[m][:, 0:3], ctail[l][:, m, :], eng="pool")
                S.copy(skf[:, 0:128], kprev[l], eng="pool")
                S.copy(svpad[:, 0], vprev[l], eng="pool")
                def dense32(mi):
                    ps = pdn()
                    for k in range(8):
                        S.mm(ps, w32[:, k, mi * 128:(mi + 1) * 128], hn32[k], start=(k == 0), stop=(k == 7))
                    return ps
                S.dma("sp", w32, srcv(w_in, l, 0, DM, 0, 512))
                for mi in range(4):
                    ps = dense32(mi)
                    evac(raw[mi][:, 3:TT + 3], ps)
                S.dma("sp", w32, srcv(w_in, l, 0, DM, 1032, 512))
                for mi in range(4):
                    ps = dense32(mi)
                    if mi < 2:
                        evac(mqf[mi], ps)
                    else:
                        S.act(mkf[mi - 2], ps, AF.Copy, scale=0.125)
                arena_switch(cR)
                sl = use_w(l, 1)
                for mi in range(4):
                    ps = dense_fm(sl, 8, 512, mi, hn)
                    if mi < 2:
                        evac(raw[4 + mi][:, 3:TT + 3], ps)
                    else:
                        S.act(gzs[mi - 2], ps, AF.Silu)
                done_w()
                sl = use_w(l, 3)
                for mi in range(2):
                    ps = dense_fm(sl, 8, 512, mi, hn)
                    S.act(mos[mi], ps, AF.Sigmoid)
                ps = dense_fm(sl, 8, 512, 2, hn)
                S.tt(tmpA, ps, ropeC, mult)
                ps = dense_fm(sl, 8, 512, 3, hn)
                S.tt(tmpB, ps, ropeS, mult)
                S.tt(skf[:, 128:128 + TT], tmpA, tmpB, add, eng="pool")
                done_w()
                S.copy(kprev[l], skf[:, TT:TT + 128], eng="pool")
                sl = use_w(l, 4)
                for mi in range(4):
                    ps = dense_fm(sl, 8, 512, mi, hn)
                    S.tt(sqt1[mi], ps, ropeC, mult)
                done_w()
                sl = use_w(l, 5)
                for mi in range(4):
                    ps = dense_fm(sl, 8, 512, mi, hn)
                    S.tt(tmpB, ps, ropeS, mult)
                    S.tt(sqf[mi], sqt1[mi], tmpB, add, eng="pool")
                done_w()
                sl = use_w(l, 6)
                sv3 = sl[:, 0:4096].re("p (k c) -> p k c", k=8)
                for j in range(4):
                    ps = pdn()
                    for kt in range(8):
                        S.mm(ps, hn[kt][:, j * 128:(j + 1) * 128], sv3[:, kt, :], start=(kt == 0), stop=(kt == 7))
                    S.act(mk_tm[:, j, :], ps[:, 0:256], AF.Copy, scale=0.125)
                    pv = ps[:, 256:512].re("p (pr hf d) -> p pr hf d", pr=2, hf=2)
                    S.copy(mvpad[:, j, :, 0, 0:64], pv[:, :, 0, :])
                    S.copy(mvpad[:, j, :, 1, 64:128], pv[:, :, 1, :])
                done_w()
                sl = use_w(l, 7)
                sv3 = sl[:, 0:8 * 144].re("p (k c) -> p k c", k=8)
                for j in range(4):
                    ps = pdn()
                    for kt in range(8):
                        S.mm(ps[:, 0:144], hn[kt][:, j * 128:(j + 1) * 128], sv3[:, kt, :], start=(kt == 0), stop=(kt == 7))
                    S.copy(svpad[:, 1 + j, 0, 0:64], ps[:, 0:64], eng="act")
                    S.copy(svpad[:, 1 + j, 1, 64:128], ps[:, 64:128])
                    S.copy(graw[:, j, :], ps[:, 128:144])
                done_w()
                S.copy(vprev[l], svpad[:, 4], eng="pool")

                if STOP <= 1:
                    break
                arena_switch(cM)
                S.handoff(hn, cW.views)
                def swa_gen():
                    mxs, nbs, ess, rss, dens, t8 = st8
                    for j in range(4):
                        mcol = C_SW0 if (t == 0 and j == 0) else C_SW
                        maskv = cst[:, mcol:mcol + 256].re("p (o k) -> p o k", o=1).bc([128, 2, 256])
                        for hv in range(2):
                            for ii in range(2):
                                i = hv * 2 + ii
                                ps = rs()
                                psv = ps.re("p (g k) -> p g k", g=2)
                                for g in range(2):
                                    S.mm(psv[:, g, :], sqf[i][g * 64:(g + 1) * 64, j * 128:(j + 1) * 128],
                                         skf[g * 64:(g + 1) * 64, j * 128:j * 128 + 256])
                                S.tt(sm[ii], psv, maskv, add)
                                S.reduce(mxs[:, ii * 2:ii * 2 + 2], sm[ii], mx_)
                                yield
                            ns = negsink[:, l * 8 + hv * 4:l * 8 + hv * 4 + 4]
                            S.stt(nbs[:, 0:4], mxs[:, 0:4], -0.125, ns, mult, mn_)
                            S.tt(t8[:, 0:4], nbs[:, 0:4], ns, sub)
                            S.act(ess[:, 0:4], t8[:, 0:4], AF.Exp)
                            yield
                            for ii in range(2):
                                for g in range(2):
                                    s_ = ii * 2 + g
                                    S.act(sm[ii][:, g, :], sm[ii][:, g, :], AF.Exp, bias=nbs[:, s_:s_ + 1], scale=0.125,
                                          accum=rss[:, s_:s_ + 1])
                                    yield
                            S.tt(dens[:, 0:4], rss[:, 0:4], ess[:, 0:4], add)
                            S.recip(dens[:, 0:4], dens[:, 0:4])
                            yield
                            for ii in range(2):
                                rb = dens[:, ii * 2:ii * 2 + 2].re("p (g o) -> p g o", o=1).bc([128, 2, 256])
                                S.tt(Pn[ii], sm[ii], rb, mult)
                                yield
                            ps = rs()
                            psb = ps.bitcast(BF16)
                            for q in range(8):
                                ii, g, kb = q // 4, (q // 2) % 2, q % 2
                                S.tr(psb[:, q * 128:(q + 1) * 128], Pn[ii][:, g, kb * 128:(kb + 1) * 128], identb,
                                     signal=(q == 7))
                            evac(PTs.re("p a b -> p (a b)"), psb)
                            yield
                            ps = rs()
                            for ii in range(2):
                                n_ = 0
                                for g in range(2):
                                    for kb in range(2):
                                        S.mm(ps[:, ii * 128:(ii + 1) * 128], svpad[:, j + kb, g, :],
                                             PTs[:, (ii * 2 + g) * 2 + kb, :], start=(n_ == 0), stop=(n_ == 3))
                                        n_ += 1
                            evac(ysw[:, hv * 2:hv * 2 + 2, j * 128:(j + 1) * 128],
                                 ps[:, 0:256].re("p (i q) -> p i q", i=2))
                            yield

                def mlstm_gen():
                    S.memset(mkdec, 0.0, eng="pool")
                    ibv = par[:, P_IB + l * 4:P_IB + l * 4 + 4].re("p (o h) -> p o h", o=1).bc([128, 4, 4])
                    fbv = par[:, P_FB + l * 4:P_FB + l * 4 + 4].re("p (o h) -> p o h", o=1).bc([128, 4, 4])
                    S.tt(g_t, graw[:, :, 8:12], ibv, add)
                    S.act(g_t, g_t, AF.Tanh, scale=1.0 / 15.0)
                    S.ts(g_ig, g_t, 15.0, mult)
                    S.tt(g_t, graw[:, :, 12:16], fbv, add)
                    S.act(g_t, g_t, AF.Tanh, scale=1.0 / 15.0)
                    S.act(g_t, g_t, AF.Exp, scale=-15.0)
                    S.act(g_t, g_t, AF.Ln, bias=1.0)
                    S.ts(g_b16, g_t, -1.0, mult)
                    S.copy(g_lf, g_b16)
                    yield
                    Uv = cst[:, C_U:C_U + 128].re("p (o c) -> p o c", o=1).bc([128, 4, 128])
                    for j in range(4):
                        cols = slice(j * 128, (j + 1) * 128)
                        lfj = g_lf[:, j, :]
                        S.tt(mG1, Uv, lfj.re("p (h o) -> p h o", o=1).bc([128, 4, 128]), mult)
                        ps = rm()
                        S.mm(ps, onesb, mG1.re("p h c -> p (h c)"))
                        S.act(mEb.re("p h c -> p (h c)"), ps, AF.Exp)
                        yield
                        psD = rm()
                        for h in range(4):
                            o_ = psD[:, h * 128:(h + 1) * 128]
                            S.mm(o_, onesb, mG1[:, h, :], start=True, stop=False)
                            S.mm(o_, mG1[:, h, :], negonesb, start=False, stop=False)
                            S.mm(o_, identb, maskTb, start=False, stop=True)
                            yield
                        for h in range(4):
                            S.act(mPT[:, h, :], psD[:, h * 128:(h + 1) * 128], AF.Exp, bias=g_ig[:, j, h:h + 1])
                            yield
                        psS = rm()
                        for h in range(4):
                            pr, hf = h // 2, h % 2
                            S.mm(psS[:, h * 128:(h + 1) * 128], mkf[pr][hf * 64:(hf + 1) * 64, cols],
                                 mqf[pr][hf * 64:(hf + 1) * 64, cols])
                        S.tt(mW.re("p h c -> p (h c)"), mPT.re("p h c -> p (h c)"), psS, mult)
                        yield
                        for pr in range(2):
                            for hf in range(2):
                                rows = slice(hf * 64, (hf + 1) * 64)
                                S.tt(mqdec[pr][rows, :], mqf[pr][rows, cols], mEb[rows, 2 * pr + hf, :], mult)
                        psw = rm()
                        S.mm(psw[:, 0:4], cs(C_SU), lfj)
                        S.tt(st4[0], psw[:, 0:4], g_ig[:, j, :], add)
                        S.act(st4[1], st4[0], AF.Exp)
                        yield
                        for h in range(4):
                            hf = h % 2
                            S.act(mkdec[:, h, hf * 64:(hf + 1) * 64], mk_tm[:, j, h * 64:(h + 1) * 64],
                                  AF.Copy, scale=st4[1][:, h:h + 1])
                            if t == 0 and l == 0 and j == 0:
                                pass
                        ebv = mEb.re("p h (ch c) -> p h ch c", ch=2)[:, :, :, 63].re("p (pr hf) ch -> p pr hf ch", hf=2)
                        S.copy(dsel[0:64], ebv[0:64, :, 0, :])
                        S.copy(dsel[64:128], ebv[64:128, :, 1, :])
                        yield
                        for ch in range(2):
                            rows = slice(ch * 64, (ch + 1) * 64)
                            cc = slice(ch * 64, (ch + 1) * 64)
                            psN = rm()
                            nv = psN[:, 0:256].re("p (pr kd c) -> p pr kd c", pr=2, kd=2)
                            psU = [rm(), rm()]
                            for pr in range(2):
                                S.mm(nv[:, pr, 0, :], CNb[l][pr][:, 0:128], mqdec[pr][:, cc], start=True, stop=False)
                                S.mm(nv[:, pr, 0, :], mvpad[rows, j, pr, 0, :], mW[rows, 2 * pr, cc], start=False, stop=False)
                                S.mm(nv[:, pr, 0, :], mvpad[rows, j, pr, 1, :], mW[rows, 2 * pr + 1, cc], start=False, stop=True)
                                S.mm(nv[:, pr, 1, :], CNb[l][pr][:, 128:256], mqdec[pr][:, cc], start=True, stop=False)
                                S.mm(nv[:, pr, 1, :], onesEb[rows, :], mW[rows, 2 * pr, cc], start=False, stop=False)
                                S.mm(nv[:, pr, 1, :], onesOb[rows, :], mW[rows, 2 * pr + 1, cc], start=False, stop=True)
                                uu = psU[pr]
                                S.mm(uu[:, 0:128], mkdec[rows, 2 * pr, :], mvpad[rows, j, pr, 0, :], start=True, stop=False)
                                S.mm(uu[:, 0:128], mkdec[rows, 2 * pr + 1, :], mvpad[rows, j, pr, 1, :], start=False, stop=True)
                                S.mm(uu[:, 128:256], mkdec[rows, 2 * pr, :], onesEb[rows, :], start=True, stop=False)
                                S.mm(uu[:, 128:256], mkdec[rows, 2 * pr + 1, :], onesOb[rows, :], start=False, stop=True)
                                yield
                            for pr in range(2):
                                S.stt(CN[l][pr], CN[l][pr], dsel[:, pr, ch:ch + 1], psU[pr][:, 0:256], mult, add)
                                S.copy(CNb[l][pr], CN[l][pr], eng="act")
                                yield
                            S.act(mdab, nv[:, :, 1, :], AF.Abs)
                            S.ts(mdab, mdab, 1.0, mx_)
                            S.recip(mdab, mdab)
                            S.tt(oT[:, :, j * 128 + ch * 64: j * 128 + (ch + 1) * 64], nv[:, :, 0, :], mdab, mult)
                            yield
                def gdn_gen():
                    for m in range(6):
                        e_ = "dve"
                        cw = P_CONV + l * 24 + m * 4
                        S.ts(tmpA if m % 2 == 0 else tmpB, raw[m][:, 0:TT], pc(cw), mult, eng=e_)
                        acc = tmpA if m % 2 == 0 else tmpB
                        for jj in range(1, 4):
                            S.stt(acc, raw[m][:, jj:jj + TT], pc(cw + jj), acc, mult, add, eng=e_)
                        S.act(qkvc[m], acc, AF.Silu)
                        yield
                        S.copy(ctail[l][:, m, :], raw[m][:, TT:TT + 3], eng="pool")
                    for m in range(4):
                        S.act(sqb[m % 2], qkvc[m], AF.Square)
                        ps = rg()
                        S.mm(ps, bdonesb, sqb[m % 2])
                        rsqrt_from(ps, 1.0, tmpA)
                        S.stt(qkvc[m], qkvc[m], (0.125 if m < 2 else 1.0), tmpA, mult, mult)
                        yield
                    dtv = par[:, P_DTB + l * 4:P_DTB + l * 4 + 4].re("p (o h) -> p o h", o=1).bc([128, 4, 4])
                    nAv = nA[:, l * 4:l * 4 + 4].re("p (o h) -> p o h", o=1).bc([128, 4, 4])
                    S.act(g_beta, graw[:, :, 0:4], AF.Sigmoid)
                    S.tt(g_t2, graw[:, :, 4:8], dtv, add)
                    S.act(g_t2, g_t2, AF.Exp)
                    S.act(g_t2, g_t2, AF.Ln, bias=1.0)
                    S.tt(g_b162, g_t2, nAv, mult)
                    S.copy(g_g, g_b162)
                    yield
                    for j in range(4):
                        ps = rg()
                        for q in range(4):
                            src = qkvc[2 + q] if q < 2 else qkvc[4 + (q - 2)]
                            S.tr(ps[:, q * 128:(q + 1) * 128], src[:, j * 128:(j + 1) * 128], cs(C_ID), signal=(q == 3))
                        evac(raw[j][:, 0:512], ps)
                        yield
                    for pr in range(2):
                        S.memset(gvnE[pr], 0.0, eng="pool")
                        S.memset(gvnO[pr], 0.0, eng="pool")
                    S.memset(gkbg, 0.0, eng="pool")
                    S.memset(gkdec, 0.0, eng="pool")
                    yield
                    strv = cst[:, C_STR:C_STR + 128].re("p (o c) -> p o c", o=1).bc([128, 4, 128])
                    idv = cst[:, C_ID:C_ID + 128].re("p (o c) -> p o c", o=1).bc([128, 4, 128])
                    for j in range(4):
                        cols = slice(j * 128, (j + 1) * 128)
                        gj = g_g[:, j, :]
                        bj = g_beta[:, j, :]
                        S.tt(gG1, Uv, gj.re("p (h o) -> p h o", o=1).bc([128, 4, 128]), mult)
                        ps = rg()
                        S.mm(ps, onesb, gG1.re("p h c -> p (h c)"))
                        S.act(gEg.re("p h c -> p (h c)"), ps, AF.Exp)
                        yield
                        psD = rg()
                        for h in range(4):
                            o_ = psD[:, h * 128:(h + 1) * 128]
                            S.mm(o_, gG1[:, h, :], onesb, start=True, stop=False)
                            S.mm(o_, negonesb, gG1[:, h, :], start=False, stop=False)
                            S.mm(o_, identb, maskCSb, start=False, stop=True)
                        S.act(gdec.re("p h c -> p (h c)"), psD, AF.Exp)
                        yield
                        psK = rg()
                        psQ = rg()
                        for h in range(4):
                            pr, hf = h // 2, h % 2
                            rows = slice(hf * 64, (hf + 1) * 64)
                            S.mm(psK[:, h * 128:(h + 1) * 128], qkvc[2 + pr][rows, cols], qkvc[2 + pr][rows, cols])
                            S.mm(psQ[:, h * 128:(h + 1) * 128], qkvc[pr][rows, cols], qkvc[2 + pr][rows, cols])
                        S.tt(gA.re("p h c -> p (h c)"), gdec.re("p h c -> p (h c)"), psK, mult)
                        S.tt(gQKd.re("p h c -> p (h c)"), gdec.re("p h c -> p (h c)"), psQ, mult)
                        yield
                        S.tt(gA, gA, bj.re("p (h o) -> p h o", o=1).bc([128, 4, 128]), mult)
                        S.tt(gL, gA, strv, mult)
                        yield
                        ps = rg()
                        for h in range(4):
                            S.tr(ps[:, h * 128:(h + 1) * 128], gL[:, h, :], cs(C_ID), signal=(h == 3))
                        evac(gMt.re("p h c -> p (h c)"), ps)
                        yield
                        ps = rg()
                        for h in range(4):
                            S.tr(ps[:, h * 128:(h + 1) * 128], gQKd[:, h, :], cs(C_ID), signal=(h == 3))
                        evac(gQKdT.re("p h c -> p (h c)"), ps)
                        yield
                        S.tt(gP, idv, gMt, sub)
                        yield
                        cur = 0
                        for it in range(1, 6):
                            Lc, Mc = gLp[cur], gMp[cur]
                            Ln_, Mn_ = gLp[1 - cur], gMp[1 - cur]
                            psL = rg()
                            for h in range(4):
                                S.mm(psL[:, h * 128:(h + 1) * 128], Mc[:, h, :], Lc[:, h, :])
                            if it < 5:
                                psM = rg()
                                for h in range(4):
                                    S.mm(psM[:, h * 128:(h + 1) * 128], Lc[:, h, :], Mc[:, h, :])
                            S.copy(Ln_.re("p h c -> p (h c)"), psL, eng="act")
                            yield
                            if it < 5:
                                S.copy(Mn_.re("p h c -> p (h c)"), psM, eng="dve")
                                yield
                            psP = rg()
                            for h in range(4):
                                S.mm(psP[:, h * 128:(h + 1) * 128], Ln_[:, h, :], gP[:, h, :])
                            S.tt(gP.re("p h c -> p (h c)"), gP.re("p h c -> p (h c)"), psP, add)
                            yield
                            cur = 1 - cur
                        psg = rg()
                        S.mm(psg[:, 0:4], cs(C_U), gj)
                        S.mm(psg[:, 4:8], cs(C_SU), gj)
                        S.act(st4b[0], psg[:, 0:4], AF.Exp)
                        S.act(st4b[1], psg[:, 4:8], AF.Exp)
                        S.tt(st4b[2], st4b[0], bj, mult)
                        yield
                        ktv = raw[j][:, 0:256]
                        vtv = raw[j][:, 256:512]
                        for h in range(4):
                            hf = h % 2
                            S.act(gkbg[:, h, hf * 64:(hf + 1) * 64], ktv[:, h * 64:(h + 1) * 64], AF.Copy,
                                  scale=st4b[2][:, h:h + 1])
                            S.act(gkdec[:, h, hf * 64:(hf + 1) * 64], ktv[:, h * 64:(h + 1) * 64], AF.Copy,
                                  scale=st4b[1][:, h:h + 1])
                        S.tt(gvb, vtv.re("p (h e) -> p h e", h=4), bj.re("p (h o) -> p h o", o=1).bc([128, 4, 64]), mult)
                        yield
                        psu = rg()
                        for h in range(4):
                            S.mm(psu[:, h * 64:(h + 1) * 64], gP[:, h, :], gvb[:, h, :])
                        evac(gu, psu[:, 0:256])
                        yield
                        psw = rg()
                        for pr in range(2):
                            S.mm(psw[:, pr * 128:(pr + 1) * 128], gkbg[:, 2 * pr, :], gP[:, 2 * pr, :], start=True, stop=False)
                            S.mm(psw[:, pr * 128:(pr + 1) * 128], gkbg[:, 2 * pr + 1, :], gP[:, 2 * pr + 1, :], start=False, stop=True)
                        evac(gwT.re("p a b -> p (a b)"), psw[:, 0:256])
                        yield
                        for pr in range(2):
                            for hf in range(2):
                                rows = slice(hf * 64, (hf + 1) * 64)
                                S.tt(gqdT[pr][rows, :], qkvc[pr][rows, cols], gEg[rows, 2 * pr + hf, :], mult)
                        egv = gEg.re("p h (ch c) -> p h ch c", ch=2)[:, :, :, 63].re("p (pr hf) ch -> p pr hf ch", hf=2)
                        S.copy(dsel2[0:64], egv[0:64, :, 0, :])
                        S.copy(dsel2[64:128], egv[64:128, :, 1, :])
                        yield
                        for ch in range(2):
                            rows = slice(ch * 64, (ch + 1) * 64)
                            cc = slice(ch * 64, (ch + 1) * 64)
                            psO = pm[0]
                            for pr in range(2):
                                psV = rg()
                                S.mm(psV[:, 0:128], gwT[:, pr, :], Sbd[l][pr])
                                S.tt(gvnE[pr][rows, 0:64], gu[rows, pr * 128:pr * 128 + 64], psV[rows, 0:64], sub)
                                S.tt(gvnO[pr][rows, 64:128], gu[rows, pr * 128 + 64:pr * 128 + 128], psV[rows, 64:128], sub)
                                yield
                                o_ = psO[:, pr * 64:(pr + 1) * 64]
                                S.mm(o_, Sbd[l][pr], gqdT[pr][:, cc], start=True, stop=False)
                                S.mm(o_, gvnE[pr][rows, :], gQKdT[rows, 2 * pr, cc], start=False, stop=False)
                                S.mm(o_, gvnO[pr][rows, :], gQKdT[rows, 2 * pr + 1, cc], start=False, stop=True)
                                psS_ = rg()
                                S.mm(psS_[:, 0:128], gkdec[rows, 2 * pr, :], gvnE[pr][rows, :], start=True, stop=False)
                                S.mm(psS_[:, 0:128], gkdec[rows, 2 * pr + 1, :], gvnO[pr][rows, :], start=False, stop=True)
                                S.stt(Sbd[l][pr], Sbd[l][pr], dsel2[:, pr, ch:ch + 1], psS_[:, 0:128], mult, add)
                                yield
                            evac(oT2[:, :, j * 128 + ch * 64:j * 128 + (ch + 1) * 64], psO[:, 0:128].re("p (pr c) -> p pr c", pr=2))
                            yield
                gens = [gdn_gen(), swa_gen(), mlstm_gen()]
                while gens:
                    for g_ in list(gens):
                        try:
                            next(g_)
                        except StopIteration:
                            gens.remove(g_)
                S.handoff(cW.views, hn)
                for pr in range(2):
                    headnorm(oT[:, pr, :], P_MLN + l * 2 + pr, mos[pr], yA[2 + pr])

                for pr in range(2):
                    headnorm(oT2[:, pr, :], P_GDNN + l, gzs[pr], yA[pr])

                if STOP <= 4:
                    break
                ymix = [yA[0], yA[1], yA[2], yA[3], ysw[:, 0, :], ysw[:, 1, :], ysw[:, 2, :], ysw[:, 3, :]]
                for c in range(2):
                    sl = use_w(l, 8 + c)
                    for mi in range(4):
                        ps = dense_fm(sl, 8, 512, mi, ymix)
                        m = c * 4 + mi
                        S.tt(xT[m], xT[m], ps, add)
                    done_w()
                rmsnorm(G_MLP + l * 8, hn)
                arena_switch(cU)
                for c in range(8):
                    sl = use_w(l, 10 + c)
                    for mi in range(4):
                        ps = dense_fm(sl, 8, 512, mi, hn)
                        tr_ = tmpA if mi % 2 == 0 else tmpB
                        S.act(tr_, ps, AF.Relu)
                        S.tt(u[c * 4 + mi], tr_, tr_, mult, eng=("pool" if mi % 2 == 0 else "dve"))
                    done_w()
                for c in range(8):
                    sl = use_w(l, 18 + c)
                    ps = dense_fm(sl, 32, 128, 0, u)
                    S.tt(xT[c], xT[c], ps, add)
                    done_w()
                rmsnorm(G_PLE + l * 8, hn)
                slp = use_w(l, 26)
                for c in range(2):
                    sl = use_w(l, 27 + c)
                    for mi in range(4):
                        m = c * 4 + mi
                        psg = dense_fm(sl, 8, 512, mi, hn)
                        psp = dense_fm(slp, 2, 1024, m, pT)
                        S.act(tmpA, psg, AF.Sigmoid)
                        S.tt(tmpB, tmpA, psp, mult)
                        S.tt(xT[m], xT[m], tmpB, add, eng="pool")
                done_w()
                done_w()
                done_w()

            if 'f' not in SKIP:
                rmsnorm(G_FIN, xT, inplace=True)
            for j in range(4):
                for half in range(2):
                    ps = pdn()
                    for q in range(4):
                        kt = half * 4 + q
                        S.tr(ps[:, q * 128:(q + 1) * 128], xT[kt][:, j * 128:(j + 1) * 128], cs(C_ID), signal=(q == 3))
                    evac(xin[:, half * 512:(half + 1) * 512], ps)
                ev = S.dma(XQ, out_d[tok0 + j * 128:tok0 + (j + 1) * 128, :], xin)
                out_events.append(ev)
        S.finish(out_events[-8:])
        print("instructions:", S.ninstr, "dma sems:", S.ndsem, flush=True)
    return nc


def _bcast_inputs(inp, b, NT, par, cst):
    SEQ = NT * TT
    return {
        "x": np.ascontiguousarray(inp["x"][b, :SEQ]),
        "p": np.ascontiguousarray(inp["p"][:, b, :SEQ]),
        "pos": np.ascontiguousarray(inp["positions"][b:b + 1, :SEQ]).astype(np.int32),
        "w_in": inp["w_in"], "w_out": inp["w_out"], "w_up": inp["w_up"], "w_down": inp["w_down"],
        "w_gate": inp["w_ple_gate"], "w_proj": inp["w_ple_proj"],
        "par": par, "cst": cst,
    }


def kernel(**inputs):
    inp = {k: np.asarray(v) for k, v in inputs.items()}
    NT, NL = 8, 4
    par, cst = host_tables(inp)
    nc = build(NT, NL)
    in_maps = [_bcast_inputs(inp, b, NT, par, cst) for b in range(8)]
    res = run_bass_kernel_spmd(nc, in_maps, core_ids=list(range(8)))
    return np.stack([res.results[b]["out"] for b in range(8)], axis=0).astype(np.float32)
```

```python
import contextlib
import os
import numpy as np
import concourse.bass as bass
import concourse.mybir as mybir
from concourse.bass_utils import run_bass_kernel_spmd
import ml_dtypes

F32 = mybir.dt.float32
BF16 = mybir.dt.bfloat16
I32 = mybir.dt.int32
AF = mybir.ActivationFunctionType
ALU = mybir.AluOpType
AX = mybir.AxisListType


class Buf:
    __slots__ = ("name", "w", "r", "dsem", "dcnt", "kind")

    def __init__(self, name, kind="sb"):
        self.name = name
        self.kind = kind
        self.w = None
        self.r = {}
        self.dsem = None
        self.dcnt = 0


class V:
    __slots__ = ("ap", "bufs")

    def __init__(self, ap, bufs):
        self.ap = ap
        self.bufs = bufs

    def __getitem__(self, key):
        return V(self.ap[key], self.bufs)

    def re(self, s, **kw):
        return V(self.ap.rearrange(s, **kw), self.bufs)

    def bc(self, shape):
        return V(self.ap.to_broadcast(shape), self.bufs)

    def bitcast(self, dt):
        return V(self.ap.bitcast(dt), self.bufs)

    @property
    def shape(self):
        return self.ap.shape


class Sched:
    def __init__(self, nc, es):
        self.nc = nc
        self.es = es
        self.sems = {}
        self.cnt = {}
        self.known = {}
        self.eng = {"pe": nc.tensor, "act": nc.scalar, "dve": nc.vector,
                    "pool": nc.gpsimd, "sp": nc.sync}
        for k in self.eng:
            self.sems[k] = es.enter_context(nc.semaphore("s_" + k))
            self.cnt[k] = 0
            self.known[k] = {}
        self.ndsem = 0
        self.pe_pending = False
        self.out_events = []
        self.ninstr = 0

    def sb(self, name, shape, dt):
        t = self.es.enter_context(self.nc.sbuf_tensor(name, list(shape), dt))
        return V(t[:], [Buf(name)])

    def ps(self, name, shape, dt):
        t = self.es.enter_context(self.nc.psum_tensor(name, list(shape), dt))
        return V(t[:], [Buf(name, "ps")])

    def dram(self, ap, name):
        return V(ap, [Buf(name, "dram")])

    def _dsem(self, buf):
        if buf.dsem is None:
            key = "d%d" % self.ndsem
            self.ndsem += 1
            self.sems[key] = self.es.enter_context(self.nc.semaphore(key))
            buf.dsem = key
        return buf.dsem

    def _deps(self, ek, reads, writes):
        deps = {}

        def add(ev):
            if ev is None:
                return
            k, v = ev
            if deps.get(k, 0) < v:
                deps[k] = v
        for b in reads:
            add(b.w)
            if b.kind == "ps":
                for k, v in b.r.items():
                    if k != ek:
                        add((k, v))
        for b in writes:
            add(b.w)
            for k, v in b.r.items():
                add((k, v))
        kn = self.known[ek]
        for k, v in deps.items():
            if k == "pe" and ek == "pe":
                continue
            if kn.get(k, 0) >= v:
                continue
            assert not (k == "pe" and v > self.cnt["pe"]), "wait on unsignalled PE event (deadlock)"
            self.eng[ek].wait_ge(self.sems[k], v)
            kn[k] = v

    def _commit(self, ev, reads, writes):
        k, v = ev
        for b in writes:
            b.w = ev
            b.r = {}
        for b in reads:
            if b.r.get(k, 0) < v:
                b.r[k] = v

    def op(self, ek, fn, outs, ins, signal=True):
        reads = [b for v in ins for b in v.bufs]
        writes = [b for v in outs for b in v.bufs]
        self._deps(ek, reads, writes)
        ins_ = fn()
        self.ninstr += 1
        if ek == "pe" and not signal:
            ev = ("pe", self.cnt["pe"] + 1)
            self.pe_pending = True
        else:
            self.cnt[ek] += 1
            ev = (ek, self.cnt[ek])
            ins_.then_inc(self.sems[ek], 1)
            if ek == "pe":
                self.pe_pending = False
        self._commit(ev, reads, writes)
        return ev

    def dma(self, qk, out, in_, cast=False, **kw):
        reads = list(in_.bufs)
        writes = list(out.bufs)
        self._deps(qk, reads, writes)
        owner = writes[0]
        if owner.kind == "dram" and reads and reads[0].kind != "dram":
            owner = reads[0]
        key = self._dsem(owner)
        owner.dcnt += 16
        ev = (key, owner.dcnt)
        self.eng[qk].dma_start(out=out.ap, in_=in_.ap, **kw).then_inc(self.sems[key], 16)
        self.ninstr += 1
        self._commit(ev, reads, writes)
        return ev

    def handoff(self, old, new):
        evs = {}
        for v in old:
            for b in v.bufs:
                if b.w is not None:
                    evs[b.w[0]] = max(evs.get(b.w[0], 0), b.w[1])
                for k, val in b.r.items():
                    evs[k] = max(evs.get(k, 0), val)
        for v in new:
            for b in v.bufs:
                b.w = None
                b.r = dict(evs)

    def finish(self, evs):
        for k, v in evs:
            self.eng["sp"].wait_ge(self.sems[k], v)

    def mm(self, out, lhsT, rhs, start=True, stop=True, signal=None, **kw):
        if signal is None:
            signal = stop
        rows = lhsT.ap.shape[0]
        rg = (lhsT.ap.base_partition(), rows) if rows < 128 else None
        b = out.bufs[0]
        if not hasattr(self, "bank_rg"):
            self.bank_rg = {}
        if rg is not None:
            signal = True
            prev = self.bank_rg.get(b)
            if prev is not None and prev[0] != rg:
                self.nc.tensor.wait_ge(self.sems["pe"], prev[1])
        else:
            self.bank_rg.pop(b, None)
        ev = self.op("pe", lambda: self.nc.tensor.matmul(
            out.ap, lhsT=lhsT.ap, rhs=rhs.ap, start=start, stop=stop, **kw),
            [out], [lhsT, rhs], signal=signal)
        if rg is not None:
            self.bank_rg[b] = (rg, ev[1])
        return ev

    def tr(self, out, in_, ident, signal=True):
        if hasattr(self, "bank_rg"):
            self.bank_rg.pop(out.bufs[0], None)
        return self.op("pe", lambda: self.nc.tensor.transpose(out.ap, in_.ap, ident.ap),
                       [out], [in_, ident], signal=signal)

    def act(self, out, in_, func, bias=None, scale=1.0, accum=None, eng="act"):
        ins = [in_]
        kw = {}
        if bias is not None:
            if isinstance(bias, V):
                ins.append(bias)
                kw["bias"] = bias.ap
            else:
                kw["bias"] = bias
        if isinstance(scale, V):
            ins.append(scale)
            kw["scale"] = scale.ap
        else:
            kw["scale"] = scale
        outs = [out]
        if accum is not None:
            outs.append(accum)
            kw["accum_out"] = accum.ap
        return self.op("act", lambda: self.nc.scalar.activation(
            out=out.ap, in_=in_.ap, func=func, **kw), outs, ins)

    pool_busy = False

    def _ve(self, eng):
        return self.nc.vector if eng == "dve" else self.nc.gpsimd

    def _rm(self, eng):
        return "dve" if (eng == "pool" and self.pool_busy) else eng

    def tt(self, out, a, b, op, eng="dve"):
        eng = self._rm(eng)
        return self.op(eng, lambda: self._ve(eng).tensor_tensor(
            out=out.ap, in0=a.ap, in1=b.ap, op=op), [out], [a, b])

    def ts(self, out, a, s1, op0, s2=None, op1=None, eng="dve", accum=None):
        eng = self._rm(eng)
        ins = [a]
        s1a = s1.ap if isinstance(s1, V) else s1
        s2a = s2.ap if isinstance(s2, V) else s2
        if isinstance(s1, V):
            ins.append(s1)
        if isinstance(s2, V):
            ins.append(s2)
        kw = {}
        if op1 is not None:
            kw["op1"] = op1
        outs = [out]
        if accum is not None:
            outs.append(accum)
            kw["accum_out"] = accum.ap
        return self.op(eng, lambda: self._ve(eng).tensor_scalar(
            out=out.ap, in0=a.ap, scalar1=s1a, scalar2=s2a, op0=op0, **kw), outs, ins)

    def stt(self, out, a, s, b, op0, op1, eng="dve"):
        eng = "dve"
        ins = [a, b]
        sa = s.ap if isinstance(s, V) else s
        if isinstance(s, V):
            ins.append(s)
        return self.op(eng, lambda: self._ve(eng).scalar_tensor_tensor(
            out=out.ap, in0=a.ap, scalar=sa, in1=b.ap, op0=op0, op1=op1), [out], ins)

    def copy(self, out, in_, eng="dve"):
        eng = self._rm(eng)
        if eng == "act":
            return self.op("act", lambda: self.nc.scalar.copy(out=out.ap, in_=in_.ap), [out], [in_])
        return self.op(eng, lambda: self._ve(eng).tensor_copy(out=out.ap, in_=in_.ap), [out], [in_])

    def memset(self, out, val, eng="dve"):
        eng = self._rm(eng)
        return self.op(eng, lambda: self._ve(eng).memset(out.ap, val), [out], [])

    def reduce(self, out, in_, op, axis=AX.X, eng="dve"):
        eng = self._rm(eng)
        return self.op(eng, lambda: self._ve(eng).tensor_reduce(
            out=out.ap, in_=in_.ap, axis=axis, op=op), [out], [in_])

    def recip(self, out, in_):
        return self.op("dve", lambda: self.nc.vector.reciprocal(out=out.ap, in_=in_.ap), [out], [in_])


DM = 1024
TT = 512
NEG = -30000.0
EPS = 1e-6
NSLOT = 3
NCHUNK = 30
G_MIX, G_MLP, G_PLE, G_FIN = 0, 32, 64, 96
P_CONV, P_GDNN, P_MLN, P_ALOG, P_DTB, P_IB, P_FB, P_SINK, P_C1, P_SGN = 104, 200, 204, 212, 228, 244, 260, 276, 308, 309
NPAR = 310
C_ID, C_BD, C_U, C_SU, C_MT, C_MCS, C_STR, C_E, C_O, C_SW, C_SW0 = 0, 128, 256, 384, 512, 640, 768, 896, 1024, 1152, 1408
NCST = 1664


def host_tables(inp):
    par = np.zeros((128, NPAR), np.float32)
    p = np.arange(128)
    for l in range(4):
        for kt in range(8):
            par[:, G_MIX + l * 8 + kt] = inp["norm_mix"][l, kt * 128:(kt + 1) * 128]
            par[:, G_MLP + l * 8 + kt] = inp["norm_mlp"][l, kt * 128:(kt + 1) * 128]
            par[:, G_PLE + l * 8 + kt] = inp["norm_ple"][l, kt * 128:(kt + 1) * 128]
        for m in range(6):
            for j in range(4):
                par[:, P_CONV + l * 24 + m * 4 + j] = inp["conv_w"][l, j, m * 128:(m + 1) * 128]
        par[:, P_GDNN + l] = inp["gdn_norm"][l, p % 64]
        for pr in range(2):
            par[:, P_MLN + l * 2 + pr] = inp["mlstm_norm"][l, pr * 128:(pr + 1) * 128]
        for h in range(4):
            par[:, P_ALOG + l * 4 + h] = inp["gdn_a_log"][l, h]
            par[:, P_DTB + l * 4 + h] = inp["gdn_dt_bias"][l, h]
            par[:, P_IB + l * 4 + h] = inp["mlstm_i_bias"][l, h]
            par[:, P_FB + l * 4 + h] = inp["mlstm_f_bias"][l, h]
        for i in range(4):
            for g in range(2):
                par[:, P_SINK + l * 8 + i * 2 + g] = inp["attn_sinks"][l, i + 4 * g]
    for kt in range(8):
        par[:, G_FIN + kt] = inp["norm_final"][kt * 128:(kt + 1) * 128]
    inv_freq = (500000.0 ** (-np.arange(0, 16, 2, dtype=np.float32) / 16)).astype(np.float32)
    d = p % 64
    par[:, P_C1] = np.where(d < 16, inv_freq[d % 8] / (2 * np.pi), 0.0)
    par[:, P_SGN] = np.where(d < 8, -1.0, np.where(d < 16, 1.0, 0.0))
    cst = np.zeros((128, NCST), np.float32)
    a = p[:, None]
    b = p[None, :]
    same = (a // 64) == (b // 64)
    cst[:, C_ID:C_ID + 128] = (a == b)
    cst[:, C_BD:C_BD + 128] = same
    cst[:, C_U:C_U + 128] = same & (a <= b)
    cst[:, C_SU:C_SU + 128] = same & (a > b)
    cst[:, C_MT:C_MT + 128] = np.where(same & (a <= b), 0.0, NEG)
    cst[:, C_MCS:C_MCS + 128] = np.where(same & (b <= a), 0.0, NEG)
    cst[:, C_STR:C_STR + 128] = same & (b < a)
    cst[:, C_E:C_E + 128] = (b < 64)
    cst[:, C_O:C_O + 128] = (b >= 64)
    k = np.arange(256)[None, :]
    ok = (k > a) & (k <= a + 128)
    cst[:, C_SW:C_SW + 256] = np.where(ok, 0.0, NEG)
    cst[:, C_SW0:C_SW0 + 256] = np.where(ok & (k >= 128), 0.0, NEG)
    return par, cst


def build(NT, NL):
    STOP = int(os.environ.get('MK_STOP', '99'))
    XQ = os.environ.get('MK_XQ', 'act')
    SKIP = os.environ.get('MK_SKIP', '')
    SWA_ST = int(os.environ.get('MK_SWA', '9'))
    SEQ = NT * TT
    nc = bass.Bass("TRN2", target_bir_lowering=False)
    es = contextlib.ExitStack()
    with es:
        S = Sched(nc, es)
        mult, add, sub, mx_, mn_ = ALU.mult, ALU.add, ALU.subtract, ALU.max, ALU.min

        def din(name, shape, dt=F32):
            return S.dram(nc.dram_tensor(name, list(shape), dt, kind="ExternalInput").ap(), name)
        x_d = din("x", [SEQ, DM])
        p_d = din("p", [4, SEQ, 256])
        pos_d = din("pos", [1, SEQ], I32)
        w_in = din("w_in", [4, DM, 2832])
        w_out = din("w_out", [4, DM, DM])
        w_up = din("w_up", [4, DM, 4096])
        w_dn = din("w_down", [4, 4096, DM])
        w_gt = din("w_gate", [4, DM, DM])
        w_pj = din("w_proj", [4, 256, DM])
        par_d = din("par", [128, NPAR])
        cst_d = din("cst", [128, NCST])
        out_d = S.dram(nc.dram_tensor("out", [SEQ, DM], F32, kind="ExternalOutput").ap(), "out")
        wsc_ap = nc.dram_tensor("wsc", [4, NCHUNK, 128, 4096], BF16, kind="Internal").ap()
        GROUPS = {"in": range(0, 8), "out": range(8, 10), "up": range(10, 18),
                  "dn": range(18, 26), "pl": range(26, 29)}
        wbuf = [{g: Buf("w%d%s" % (l, g), "dram") for g in GROUPS} for l in range(4)]

        def wsc(l, c):
            for g, r in GROUPS.items():
                if c in r:
                    return V(wsc_ap[l, c], [wbuf[l][g]])

        par = S.sb("par_sb", [128, NPAR], F32)
        cst = S.sb("cst_sb", [128, NCST], F32)
        xT = [S.sb("xT%d" % k, [128, TT], F32) for k in range(8)]
        hnbuf = es.enter_context(nc.sbuf_tensor("hnbuf", [128, 2048], F32))
        hnbuf_ap = hnbuf[:]
        hn = [V(hnbuf_ap[:, k * 256:(k + 1) * 256].bitcast(BF16), [Buf("hn%d" % k)]) for k in range(8)]
        sqb = [S.sb("sqb%d" % k, [128, TT], BF16) for k in range(2)]
        rstd = S.sb("rstd", [128, TT], F32)
        tmpA = S.sb("tmpA", [128, TT], F32)
        tmpB = S.sb("tmpB", [128, TT], F32)
        slots = [S.sb("slot%d" % k, [128, 4096], BF16) for k in range(NSLOT)]
        identb = S.sb("identb", [128, 128], BF16)
        onesb = S.sb("onesb", [128, 128], BF16)
        negonesb = S.sb("negonesb", [128, 128], BF16)
        bdonesb = S.sb("bdonesb", [128, 128], BF16)
        maskTb = S.sb("maskTb", [128, 128], BF16)
        maskCSb = S.sb("maskCSb", [128, 128], BF16)
        onesEb = S.sb("onesEb", [128, 128], BF16)
        onesOb = S.sb("onesOb", [128, 128], BF16)
        nA = S.sb("nA", [128, 16], F32)
        negsink = S.sb("negsink", [128, 32], F32)
        raw = [S.sb("raw%d" % m, [128, TT + 3], F32) for m in range(6)]
        qkvc = [S.sb("qkvc%d" % m, [128, TT], F32) for m in range(6)]
        gzs = [S.sb("gzs%d" % m, [128, TT], BF16) for m in range(2)]
        mos = [S.sb("mos%d" % m, [128, TT], BF16) for m in range(2)]
        ov2 = es.enter_context(nc.sbuf_tensor("ov2", [128, 2048], F32))
        ov2_ap = ov2[:]
        mqf = [V(ov2_ap[:, m * 512:(m + 1) * 512], [Buf("mqf%d" % m)]) for m in range(2)]
        mkf = [V(ov2_ap[:, 1024 + m * 512:1024 + (m + 1) * 512], [Buf("mkf%d" % m)]) for m in range(2)]
        sqf = [S.sb("sqf%d" % m, [128, TT], BF16) for m in range(4)]
        skf = S.sb("skf", [128, 128 + TT], BF16)
        ropeC = S.sb("ropeC", [128, TT], F32)
        ropeS = S.sb("ropeS", [128, TT], F32)
        posi = S.sb("posi", [128, TT], I32)
        mk_tm = S.sb("mk_tm", [128, 4, 256], F32)
        mvpad = S.sb("mvpad", [128, 4, 2, 2, 128], BF16)
        svpad = S.sb("svpad", [128, 5, 2, 128], BF16)
        graw = S.sb("graw", [128, 4, 16], F32)
        yA = [S.sb("yA%d" % m, [128, TT], BF16) for m in range(4)]
        ysw = S.sb("ysw", [128, 4, TT], BF16)
        oT = S.sb("oT", [128, 2, TT], F32)
        pin = S.sb("pin", [128, 4, 256], F32)
        pT = [S.sb("pT%d" % k, [128, TT], BF16) for k in range(2)]
        xin = S.sb("xin", [128, DM], F32)
        Sbd = [[S.sb("Sbd%d_%d" % (l, pr), [128, 128], F32) for pr in range(2)] for l in range(NL)]
        CN = [[S.sb("CN%d_%d" % (l, pr), [128, 256], F32) for pr in range(2)] for l in range(NL)]
        CNb = [[S.sb("CNb%d_%d" % (l, pr), [128, 256], BF16) for pr in range(2)] for l in range(NL)]
        ctail = [S.sb("ctail%d" % l, [128, 6, 3], F32) for l in range(NL)]
        kprev = [S.sb("kprev%d" % l, [128, 128], BF16) for l in range(NL)]
        vprev = [S.sb("vprev%d" % l, [128, 2, 128], BF16) for l in range(NL)]
        g_ig = S.sb("g_ig", [128, 4, 4], F32)
        g_lf = S.sb("g_lf", [128, 4, 4], F32)
        g_b16 = S.sb("g_b16", [128, 4, 4], BF16)
        g_beta = S.sb("g_beta", [128, 4, 4], F32)
        g_g = S.sb("g_g", [128, 4, 4], F32)
        g_t = S.sb("g_t", [128, 4, 4], F32)
        g_t2 = S.sb("g_t2", [128, 4, 4], F32)
        g_b162 = S.sb("g_b162", [128, 4, 4], BF16)
        st8 = [S.sb("st8_%d" % k, [128, 8], F32) for k in range(6)]
        st4 = [S.sb("st4_%d" % k, [128, 4], F32) for k in range(4)]
        dsel = S.sb("dsel", [128, 2, 2], F32)
        dsel2 = S.sb("dsel2", [128, 2, 2], F32)
        st4b = [S.sb("st4b_%d" % k, [128, 4], F32) for k in range(4)]
        oT2 = S.sb("oT2", [128, 2, TT], F32)
        ARN = 9216
        arena = es.enter_context(nc.sbuf_tensor("arena", [128, ARN], F32))
        arena_ap = arena[:]

        class Carver:
            def __init__(self, tag, base=None, size=None):
                self.off = 0
                self.tag = tag
                self.views = []
                self.base = arena_ap if base is None else base
                self.size = ARN if size is None else size

            def f32(self, shape):
                n = int(np.prod(shape[1:]))
                ap = self.base[:, self.off:self.off + n]
                self.off += n
                assert self.off <= self.size, (self.tag, self.off)
                v = V(ap, [Buf("%s%d" % (self.tag, len(self.views)))])
                self.views.append(v)
                if len(shape) == 3:
                    return v.re("p (a b) -> p a b", a=shape[1])
                return v

            def bf16(self, shape):
                n = int(np.prod(shape[1:]))
                assert n % 2 == 0
                ap = self.base[:, self.off:self.off + n // 2].bitcast(BF16)
                self.off += n // 2
                assert self.off <= self.size, (self.tag, self.off)
                v = V(ap, [Buf("%s%d" % (self.tag, len(self.views)))])
                self.views.append(v)
                if len(shape) == 3:
                    return v.re("p (a b) -> p a b", a=shape[1])
                return v
        cU = Carver("u")
        u = [cU.bf16([128, TT]) for _ in range(32)]
        cH = Carver("h")
        hn32 = [cH.f32([128, TT]) for _ in range(8)]
        w32 = cH.f32([128, 8, 512])
        cR = Carver("r")
        sqt1 = [cR.f32([128, TT]) for _ in range(4)]
        cW = Carver("w", base=hnbuf_ap, size=2048)
        sm = [cW.f32([128, 2, 256]) for _ in range(2)]
        Pn = [cW.bf16([128, 2, 256]) for _ in range(2)]
        PTs = cW.bf16([128, 8, 128])
        cM = Carver("gm")
        cG = cM
        mG1 = cM.bf16([128, 4, 128])
        mEb = cM.f32([128, 4, 128])
        mPT = cM.f32([128, 4, 128])
        mW = cM.bf16([128, 4, 128])
        mqdec = [cM.bf16([128, 128]) for _ in range(2)]
        mkdec = cM.bf16([128, 4, 128])
        mdab = cM.f32([128, 2, 64])
        gG1 = cG.bf16([128, 4, 128])
        gEg = cG.f32([128, 4, 128])
        gdec = cG.f32([128, 4, 128])
        gA = cG.f32([128, 4, 128])
        gL = cG.f32([128, 4, 128])
        gQKd = cG.f32([128, 4, 128])
        gMt = cG.f32([128, 4, 128])
        gQKdT = cG.f32([128, 4, 128])
        gLp = [gL, gdec]
        gMp = [gMt, gA]
        gP = gQKd
        gkbg = cG.f32([128, 4, 128])
        gkdec = cG.f32([128, 4, 128])
        gvb = cG.f32([128, 4, 64])
        gu = cG.f32([128, 256])
        gwT = cG.f32([128, 2, 128])
        gqdT = [cG.f32([128, 128]) for _ in range(2)]
        gvnE = [cG.f32([128, 128]) for _ in range(2)]
        gvnO = [cG.f32([128, 128]) for _ in range(2)]
        arena_user = [None]

        def arena_switch(c):
            if arena_user[0] is not None and arena_user[0] is not c:
                S.handoff(arena_user[0].views, c.views)
            arena_user[0] = c

        pd = [S.ps("pd%d" % k, [128, 512], F32) for k in range(4)]
        pm = [S.ps("pm%d" % k, [128, 512], F32) for k in range(4)]
        rot = {"d": 0, "m": 0}

        def pdn():
            rot["d"] = (rot["d"] + 1) % 4
            return pd[rot["d"]]

        def pmn():
            rot["m"] = (rot["m"] + 1) % 4
            return pm[rot["m"]]

        rot3 = {"m": 0}

        def pmn3():
            rot3["m"] = rot3["m"] % 3 + 1
            return pm[rot3["m"]]

        def mkrot(banks):
            st_ = {"i": -1}

            def nxt():
                st_["i"] = (st_["i"] + 1) % len(banks)
                return banks[st_["i"]]
            return nxt
        rs = mkrot([pd[2], pd[3]])
        rm = mkrot([pm[3], pd[0], pd[1]])
        rg = mkrot([pm[1], pm[2]])

        alt = {"e": 0}

        def evac_eng():
            alt["e"] ^= 1
            return "act" if alt["e"] else "dve"

        def evac(out, in_):
            S.copy(out, in_, eng=evac_eng())

        def cs(off, n=128):
            return cst[:, off:off + n]

        def pc(col, n=1):
            return par[:, col:col + n]

        uses = []
        for t in range(NT):
            for l in range(NL):
                for c in [1] + list(range(3, 29)):
                    uses.append((l, c))
        wstate = {"next_use": 0, "issued": 0}

        def issue_w():
            i = wstate["issued"]
            if i >= len(uses):
                return
            l, c = uses[i]
            n_ = {7: 1152, 26: 2048}.get(c, 4096)
            S.dma("sp", slots[i % NSLOT][:, 0:n_], wsc(l, c)[:, 0:n_])
            wstate["issued"] = i + 1

        def use_w(l, c):
            i = wstate["next_use"]
            assert uses[i] == (l, c), (uses[i], l, c)
            assert wstate["issued"] > i
            wstate["next_use"] = i + 1
            return slots[i % NSLOT]

        def done_w():
            issue_w()

        CASTSEL = os.environ.get('MK_CAST', 'all')
        ncast = [0]

        def castdma(dst, src):
            ncast[0] += 1
            if CASTSEL != 'all':
                lo, hi = [int(v) for v in CASTSEL.split(':')]
                if not (lo <= ncast[0] - 1 < hi):
                    return
            S.dma("pool", dst, src)

        def wview(l, c, kt, ncol):
            return wsc(l, c)[:, 0:kt * ncol].re("p (k c) -> p k c", k=kt)

        def srcv(w, l, r0, nrow, c0, ncol):
            return V(w.ap[l, r0:r0 + nrow, c0:c0 + ncol], w.bufs).re("(k p) c -> p k c", p=128)

        def emit_casts(l):
            castdma(wview(l, 1, 8, 512), srcv(w_in, l, 0, DM, 512, 512))
            c3 = wview(l, 3, 8, 512)
            castdma(c3[:, :, 0:256], srcv(w_in, l, 0, DM, 1800, 256))
            castdma(c3[:, :, 256:384], srcv(w_in, l, 0, DM, 2576, 128))
            skr = c3[:, :, 384:512].re("p k (g d) -> p k g d", g=2)
            sks = srcv(w_in, l, 0, DM, 2576, 128).re("p k (g d) -> p k g d", g=2)
            for g in range(2):
                castdma(skr[:, :, g, 0:8], sks[:, :, g, 8:16])
                castdma(skr[:, :, g, 8:16], sks[:, :, g, 0:8])
                castdma(skr[:, :, g, 16:64], sks[:, :, g, 16:64])
            c4 = wview(l, 4, 8, 512).re("p k (i g d) -> p k i g d", i=4, g=2)
            c5 = wview(l, 5, 8, 512).re("p k (i g d) -> p k i g d", i=4, g=2)
            for g in range(2):
                sv_ = srcv(w_in, l, 0, DM, 2064 + g * 256, 256).re("p k (i d) -> p k i d", i=4)
                for i in range(4):
                    castdma(c4[:, :, i, g, :], sv_[:, :, i, :])
                    castdma(c5[:, :, i, g, 0:8], sv_[:, :, i, 8:16])
                    castdma(c5[:, :, i, g, 8:16], sv_[:, :, i, 0:8])
                    castdma(c5[:, :, i, g, 16:64], sv_[:, :, i, 16:64])
            castdma(wview(l, 6, 8, 512), srcv(w_in, l, 0, DM, 1288, 512))
            c7 = wview(l, 7, 8, 144)
            castdma(c7[:, :, 0:128], srcv(w_in, l, 0, DM, 2704, 128))
            castdma(c7[:, :, 128:136], srcv(w_in, l, 0, DM, 1024, 8))
            castdma(c7[:, :, 136:144], srcv(w_in, l, 0, DM, 2056, 8))
            for c in range(2):
                dv = wview(l, 8 + c, 8, 512)
                castdma(dv[:, 0:4, :], srcv(w_out, l, 0, 512, c * 512, 512))
                for i in range(4):
                    for g in range(2):
                        r0 = 512 + (i + 4 * g) * 64
                        castdma(V(dv.ap[g * 64:(g + 1) * 64, 4 + i, :], dv.bufs),
                                V(w_out.ap[l, r0:r0 + 64, c * 512:(c + 1) * 512], w_out.bufs))
            for c in range(8):
                castdma(wview(l, 10 + c, 8, 512), srcv(w_up, l, 0, DM, c * 512, 512))
            for c in range(8):
                castdma(wview(l, 18 + c, 32, 128), srcv(w_dn, l, 0, 4096, c * 128, 128))
            castdma(wview(l, 26, 2, 1024), srcv(w_pj, l, 0, 256, 0, 1024))
            for c in range(2):
                castdma(wview(l, 27 + c, 8, 512), srcv(w_gt, l, 0, DM, c * 512, 512))

        S.dma("sp", par, par_d)
        S.dma("sp", cst, cst_d)
        S.pool_busy = True
        if 'c' not in SKIP:
            emit_casts(0)
        S.copy(identb, cs(C_ID))
        S.memset(onesb, 1.0)
        S.memset(negonesb, -1.0)
        S.copy(bdonesb, cs(C_BD))
        S.copy(maskTb, cs(C_MT))
        S.copy(maskCSb, cs(C_MCS))
        S.copy(onesEb, cs(C_E))
        S.copy(onesOb, cs(C_O))
        S.act(nA, par[:, P_ALOG:P_ALOG + 16], AF.Exp)
        S.ts(nA, nA, -1.0, mult)
        S.ts(negsink, par[:, P_SINK:P_SINK + 32], -1.0, mult)
        S.memset(mvpad, 0.0)
        S.memset(svpad, 0.0)
        for l in range(NL):
            for pr in range(2):
                S.memset(Sbd[l][pr], 0.0)
                S.memset(CN[l][pr], 0.0)
                S.memset(CNb[l][pr], 0.0, eng="pool")
            S.memset(ctail[l], 0.0)
            S.memset(kprev[l], 0.0)
            S.memset(vprev[l], 0.0, eng="pool")
        for _ in range(NSLOT if 'w' not in SKIP else 0):
            issue_w()

        Uv = cst[:, C_U:C_U + 128].re("p (o c) -> p o c", o=1).bc([128, 4, 128])

        def rsqrt_from(ps, scale, out):
            S.act(out, ps, AF.Ln, bias=EPS, scale=scale)
            S.act(out, out, AF.Exp, scale=-0.5)

        def rmsnorm(gbase, outs, inplace=False, outs32=None):
            ps = pdn()
            for kt in range(8):
                S.act(sqb[kt % 2], xT[kt], AF.Square)
                S.mm(ps, onesb, sqb[kt % 2], start=(kt == 0), stop=(kt == 7), signal=True)
            rsqrt_from(ps, 1.0 / DM, rstd)
            for kt in range(8):
                if outs32 is not None:
                    S.stt(outs32[kt], xT[kt], pc(gbase + kt), rstd, mult, mult)
                    S.copy(outs[kt], outs32[kt], eng="act")
                else:
                    S.stt(outs[kt], xT[kt], pc(gbase + kt), rstd, mult, mult)

        def dense_fm(slot, kt_n, ncol, mi, rhs_list):
            ps = pdn()
            sv = slot[:, 0:kt_n * ncol].re("p (k c) -> p k c", k=kt_n)
            for k in range(kt_n):
                S.mm(ps, sv[:, k, mi * 128:(mi + 1) * 128], rhs_list[k],
                     start=(k == 0), stop=(k == kt_n - 1))
            return ps

        def headnorm(src, gcol, gate, out):
            S.act(sqb[0], src, AF.Square)
            ps = pmn()
            S.mm(ps, bdonesb, sqb[0])
            rsqrt_from(ps, 1.0 / 64, tmpA)
            S.stt(tmpB, src, pc(gcol), tmpA, mult, mult)
            S.tt(out, tmpB, gate, mult, eng="pool")

        out_events = []
        for t in range(NT):
            tok0 = t * TT
            S.pool_busy = (t == 0)
            for j in range(4 if 'x' not in SKIP else 0):
                S.dma(XQ, xin, x_d[tok0 + j * 128: tok0 + (j + 1) * 128, :])
                for half in range(2):
                    ps = pdn()
                    for q in range(4):
                        kt = half * 4 + q
                        S.tr(ps[:, q * 128:(q + 1) * 128], xin[:, kt * 128:(kt + 1) * 128], cs(C_ID),
                             signal=(q == 3))
                    for q in range(4):
                        kt = half * 4 + q
                        evac(xT[kt][:, j * 128:(j + 1) * 128], ps[:, q * 128:(q + 1) * 128])
            if 'r' not in SKIP:
              S.dma("pool", posi, V(pos_d.ap[0:1, tok0:tok0 + TT].partition_broadcast(128), pos_d.bufs))
              S.copy(tmpA, posi)
              for which, dst in ((0, ropeC), (1, ropeS)):
                  S.ts(tmpB, tmpA, pc(P_C1), mult)
                  if which == 0:
                      S.ts(tmpB, tmpB, 0.25, add)
                  S.copy(posi, tmpB)
                  S.copy(dst, posi)
                  S.tt(tmpB, tmpB, dst, sub)
                  S.stt(tmpB, tmpB, 0.5, tmpB, ALU.is_gt, sub)
                  S.act(dst, tmpB, AF.Sin, scale=-6.28318)
              S.ts(ropeS, ropeS, pc(P_SGN), mult)

            for l in range(NL):
                if STOP <= 0:
                    break
                if t == 0 and l + 1 < NL:
                    emit_casts(l + 1)
                S.dma(XQ, pin, V(p_d.ap[l, tok0:tok0 + TT, :], p_d.bufs).re("(j p) c -> p j c", p=128))
                for k2 in range(2):
                    ps = pdn()
                    for j in range(4):
                        S.tr(ps[:, j * 128:(j + 1) * 128], pin[:, j, k2 * 128:(k2 + 1) * 128], cs(C_ID),
                             signal=(j == 3))
                    evac(pT[k2], ps)

                arena_switch(cH)
                rmsnorm(G_MIX + l * 8, hn, outs32=hn32)
                for m in range(6):
                    S.copy(raw[m][:, 0:3], ctail[l][:, m, :], eng="pool")
                S.copy(skf[:, 0:128], kprev[l], eng="pool")
                S.copy(svpad[:, 0], vprev[l], eng="pool")
                def dense32(mi):
                    ps = pdn()
                    for k in range(8):
                        S.mm(ps, w32[:, k, mi * 128:(mi + 1) * 128], hn32[k], start=(k == 0), stop=(k == 7))
                    return ps
                S.dma("sp", w32, srcv(w_in, l, 0, DM, 0, 512))
                for mi in range(4):
                    ps = dense32(mi)
                    evac(raw[mi][:, 3:TT + 3], ps)
                S.dma("sp", w32, srcv(w_in, l, 0, DM, 1032, 512))
                for mi in range(4):
                    ps = dense32(mi)
                    if mi < 2:
                        evac(mqf[mi], ps)
                    else:
                        S.act(mkf[mi - 2], ps, AF.Copy, scale=0.125)
                arena_switch(cR)
                sl = use_w(l, 1)
                for mi in range(4):
                    ps = dense_fm(sl, 8, 512, mi, hn)
                    if mi < 2:
                        evac(raw[4 + mi][:, 3:TT + 3], ps)
                    else:
                        S.act(gzs[mi - 2], ps, AF.Silu)
                done_w()
                sl = use_w(l, 3)
                for mi in range(2):
                    ps = dense_fm(sl, 8, 512, mi, hn)
                    S.act(mos[mi], ps, AF.Sigmoid)
                ps = dense_fm(sl, 8, 512, 2, hn)
                S.tt(tmpA, ps, ropeC, mult)
                ps = dense_fm(sl, 8, 512, 3, hn)
                S.tt(tmpB, ps, ropeS, mult)
                S.tt(skf[:, 128:128 + TT], tmpA, tmpB, add, eng="pool")
                done_w()
                S.copy(kprev[l], skf[:, TT:TT + 128], eng="pool")
                sl = use_w(l, 4)
                for mi in range(4):
                    ps = dense_fm(sl, 8, 512, mi, hn)
                    S.tt(sqt1[mi], ps, ropeC, mult)
                done_w()
                sl = use_w(l, 5)
                for mi in range(4):
                    ps = dense_fm(sl, 8, 512, mi, hn)
                    S.tt(tmpB, ps, ropeS, mult)
                    S.tt(sqf[mi], sqt1[mi], tmpB, add, eng="pool")
                done_w()
                sl = use_w(l, 6)
                sv3 = sl[:, 0:4096].re("p (k c) -> p k c", k=8)
                for j in range(4):
                    ps = pdn()
                    for kt in range(8):
                        S.mm(ps, hn[kt][:, j * 128:(j + 1) * 128], sv3[:, kt, :], start=(kt == 0), stop=(kt == 7))
                    S.act(mk_tm[:, j, :], ps[:, 0:256], AF.Copy, scale=0.125)
                    pv = ps[:, 256:512].re("p (pr hf d) -> p pr hf d", pr=2, hf=2)
                    S.copy(mvpad[:, j, :, 0, 0:64], pv[:, :, 0, :])
                    S.copy(mvpad[:, j, :, 1, 64:128], pv[:, :, 1, :])
                done_w()
                sl = use_w(l, 7)
                sv3 = sl[:, 0:8 * 144].re("p (k c) -> p k c", k=8)
                for j in range(4):
                    ps = pdn()
                    for kt in range(8):
                        S.mm(ps[:, 0:144], hn[kt][:, j * 128:(j + 1) * 128], sv3[:, kt, :], start=(kt == 0), stop=(kt == 7))
                    S.copy(svpad[:, 1 + j, 0, 0:64], ps[:, 0:64], eng="act")
                    S.copy(svpad[:, 1 + j, 1, 64:128], ps[:, 64:128])
                    S.copy(graw[:, j, :], ps[:, 128:144])
                done_w()
                S.copy(vprev[l], svpad[:, 4], eng="pool")

                if STOP <= 1:
                    break
                arena_switch(cM)
                S.handoff(hn, cW.views)
                def swa_gen():
                    mxs, nbs, ess, rss, dens, t8 = st8
                    for j in range(4):
                        mcol = C_SW0 if (t == 0 and j == 0) else C_SW
                        maskv = cst[:, mcol:mcol + 256].re("p (o k) -> p o k", o=1).bc([128, 2, 256])
                        for hv in range(2):
                            for ii in range(2):
                                i = hv * 2 + ii
                                ps = rs()
                                psv = ps.re("p (g k) -> p g k", g=2)
                                for g in range(2):
                                    S.mm(psv[:, g, :], sqf[i][g * 64:(g + 1) * 64, j * 128:(j + 1) * 128],
                                         skf[g * 64:(g + 1) * 64, j * 128:j * 128 + 256])
                                S.tt(sm[ii], psv, maskv, add)
                                S.reduce(mxs[:, ii * 2:ii * 2 + 2], sm[ii], mx_)
                                yield
                            ns = negsink[:, l * 8 + hv * 4:l * 8 + hv * 4 + 4]
                            S.stt(nbs[:, 0:4], mxs[:, 0:4], -0.125, ns, mult, mn_)
                            S.tt(t8[:, 0:4], nbs[:, 0:4], ns, sub)
                            S.act(ess[:, 0:4], t8[:, 0:4], AF.Exp)
                            yield
                            for ii in range(2):
                                for g in range(2):
                                    s_ = ii * 2 + g
                                    S.act(sm[ii][:, g, :], sm[ii][:, g, :], AF.Exp, bias=nbs[:, s_:s_ + 1], scale=0.125,
                                          accum=rss[:, s_:s_ + 1])
                                    yield
                            S.tt(dens[:, 0:4], rss[:, 0:4], ess[:, 0:4], add)
                            S.recip(dens[:, 0:4], dens[:, 0:4])
                            yield
                            for ii in range(2):
                                rb = dens[:, ii * 2:ii * 2 + 2].re("p (g o) -> p g o", o=1).bc([128, 2, 256])
                                S.tt(Pn[ii], sm[ii], rb, mult)
                                yield
                            ps = rs()
                            psb = ps.bitcast(BF16)
                            for q in range(8):
                                ii, g, kb = q // 4, (q // 2) % 2, q % 2
                                S.tr(psb[:, q * 128:(q + 1) * 128], Pn[ii][:, g, kb * 128:(kb + 1) * 128], identb,
                                     signal=(q == 7))
                            evac(PTs.re("p a b -> p (a b)"), psb)
                            yield
                            ps = rs()
                            for ii in range(2):
                                n_ = 0
                                for g in range(2):
                                    for kb in range(2):
                                        S.mm(ps[:, ii * 128:(ii + 1) * 128], svpad[:, j + kb, g, :],
                                             PTs[:, (ii * 2 + g) * 2 + kb, :], start=(n_ == 0), stop=(n_ == 3))
                                        n_ += 1
                            evac(ysw[:, hv * 2:hv * 2 + 2, j * 128:(j + 1) * 128],
                                 ps[:, 0:256].re("p (i q) -> p i q", i=2))
                            yield

                def mlstm_gen():
                    S.memset(mkdec, 0.0, eng="pool")
                    ibv = par[:, P_IB + l * 4:P_IB + l * 4 + 4].re("p (o h) -> p o h", o=1).bc([128, 4, 4])
                    fbv = par[:, P_FB + l * 4:P_FB + l * 4 + 4].re("p (o h) -> p o h", o=1).bc([128, 4, 4])
                    S.tt(g_t, graw[:, :, 8:12], ibv, add)
                    S.act(g_t, g_t, AF.Tanh, scale=1.0 / 15.0)
                    S.ts(g_ig, g_t, 15.0, mult)
                    S.tt(g_t, graw[:, :, 12:16], fbv, add)
                    S.act(g_t, g_t, AF.Tanh, scale=1.0 / 15.0)
                    S.act(g_t, g_t, AF.Exp, scale=-15.0)
                    S.act(g_t, g_t, AF.Ln, bias=1.0)
                    S.ts(g_b16, g_t, -1.0, mult)
                    S.copy(g_lf, g_b16)
                    yield
                    Uv = cst[:, C_U:C_U + 128].re("p (o c) -> p o c", o=1).bc([128, 4, 128])
                    for j in range(4):
                        cols = slice(j * 128, (j + 1) * 128)
                        lfj = g_lf[:, j, :]
                        S.tt(mG1, Uv, lfj.re("p (h o) -> p h o", o=1).bc([128, 4, 128]), mult)
                        ps = rm()
                        S.mm(ps, onesb, mG1.re("p h c -> p (h c)"))
                        S.act(mEb.re("p h c -> p (h c)"), ps, AF.Exp)
                        yield
                        psD = rm()
                        for h in range(4):
                            o_ = psD[:, h * 128:(h + 1) * 128]
                            S.mm(o_, onesb, mG1[:, h, :], start=True, stop=False)
                            S.mm(o_, mG1[:, h, :], negonesb, start=False, stop=False)
                            S.mm(o_, identb, maskTb, start=False, stop=True)
                            yield
                        for h in range(4):
                            S.act(mPT[:, h, :], psD[:, h * 128:(h + 1) * 128], AF.Exp, bias=g_ig[:, j, h:h + 1])
                            yield
                        psS = rm()
                        for h in (0, 2, 1, 3):
                            pr, hf = h // 2, h % 2
                            S.mm(psS[:, h * 128:(h + 1) * 128], mkf[pr][hf * 64:(hf + 1) * 64, cols],
                                 mqf[pr][hf * 64:(hf + 1) * 64, cols])
                        S.tt(mW.re("p h c -> p (h c)"), mPT.re("p h c -> p (h c)"), psS, mult)
                        yield
                        for pr in range(2):
                            for hf in range(2):
                                rows = slice(hf * 64, (hf + 1) * 64)
                                S.tt(mqdec[pr][rows, :], mqf[pr][rows, cols], mEb[rows, 2 * pr + hf, :], mult)
                        psw = rm()
                        S.mm(psw[:, 0:4], cs(C_SU), lfj)
                        S.tt(st4[0], psw[:, 0:4], g_ig[:, j, :], add)
                        S.act(st4[1], st4[0], AF.Exp)
                        yield
                        for h in range(4):
                            hf = h % 2
                            S.act(mkdec[:, h, hf * 64:(hf + 1) * 64], mk_tm[:, j, h * 64:(h + 1) * 64],
                                  AF.Copy, scale=st4[1][:, h:h + 1])
                            if t == 0 and l == 0 and j == 0:
                                pass
                        ebv = mEb.re("p h (ch c) -> p h ch c", ch=2)[:, :, :, 63].re("p (pr hf) ch -> p pr hf ch", hf=2)
                        S.copy(dsel[0:64], ebv[0:64, :, 0, :])
                        S.copy(dsel[64:128], ebv[64:128, :, 1, :])
                        yield
                        for ch in range(2):
                            rows = slice(ch * 64, (ch + 1) * 64)
                            cc = slice(ch * 64, (ch + 1) * 64)
                            psN = rm()
                            nv = psN[:, 0:256].re("p (pr kd c) -> p pr kd c", pr=2, kd=2)
                            psU = [rm(), rm()]
                            for pr in range(2):
                                S.mm(nv[:, pr, 0, :], CNb[l][pr][:, 0:128], mqdec[pr][:, cc], start=True, stop=False)
                                S.mm(nv[:, pr, 0, :], mvpad[rows, j, pr, 0, :], mW[rows, 2 * pr, cc], start=False, stop=False)
                                S.mm(nv[:, pr, 0, :], mvpad[rows, j, pr, 1, :], mW[rows, 2 * pr + 1, cc], start=False, stop=True)
                                S.mm(nv[:, pr, 1, :], CNb[l][pr][:, 128:256], mqdec[pr][:, cc], start=True, stop=False)
                                S.mm(nv[:, pr, 1, :], onesEb[rows, :], mW[rows, 2 * pr, cc], start=False, stop=False)
                                S.mm(nv[:, pr, 1, :], onesOb[rows, :], mW[rows, 2 * pr + 1, cc], start=False, stop=True)
                                uu = psU[pr]
                                S.mm(uu[:, 0:128], mkdec[rows, 2 * pr, :], mvpad[rows, j, pr, 0, :], start=True, stop=False)
                                S.mm(uu[:, 0:128], mkdec[rows, 2 * pr + 1, :], mvpad[rows, j, pr, 1, :], start=False, stop=True)
                                S.mm(uu[:, 128:256], mkdec[rows, 2 * pr, :], onesEb[rows, :], start=True, stop=False)
                                S.mm(uu[:, 128:256], mkdec[rows, 2 * pr + 1, :], onesOb[rows, :], start=False, stop=True)
                                yield
                            for pr in range(2):
                                S.stt(CN[l][pr], CN[l][pr], dsel[:, pr, ch:ch + 1], psU[pr][:, 0:256], mult, add)
                                S.copy(CNb[l][pr], CN[l][pr], eng="act")
                                yield
                            S.act(mdab, nv[:, :, 1, :], AF.Abs)
                            S.ts(mdab, mdab, 1.0, mx_)
                            S.recip(mdab, mdab)
                            S.tt(oT[:, :, j * 128 + ch * 64: j * 128 + (ch + 1) * 64], nv[:, :, 0, :], mdab, mult)
                            yield
                def gdn_gen():
                    for m in range(6):
                        e_ = "dve"
                        cw = P_CONV + l * 24 + m * 4
                        S.ts(tmpA if m % 2 == 0 else tmpB, raw[m][:, 0:TT], pc(cw), mult, eng=e_)
                        acc = tmpA if m % 2 == 0 else tmpB
                        for jj in range(1, 4):
                            S.stt(acc, raw[m][:, jj:jj + TT], pc(cw + jj), acc, mult, add, eng=e_)
                        S.act(qkvc[m], acc, AF.Silu)
                        yield
                        S.copy(ctail[l][:, m, :], raw[m][:, TT:TT + 3], eng="pool")
                    for m in range(4):
                        S.act(sqb[m % 2], qkvc[m], AF.Square)
                        ps = rg()
                        S.mm(ps, bdonesb, sqb[m % 2])
                        rsqrt_from(ps, 1.0, tmpA)
                        S.stt(qkvc[m], qkvc[m], (0.125 if m < 2 else 1.0), tmpA, mult, mult)
                        yield
                    dtv = par[:, P_DTB + l * 4:P_DTB + l * 4 + 4].re("p (o h) -> p o h", o=1).bc([128, 4, 4])
                    nAv = nA[:, l * 4:l * 4 + 4].re("p (o h) -> p o h", o=1).bc([128, 4, 4])
                    S.act(g_beta, graw[:, :, 0:4], AF.Sigmoid)
                    S.tt(g_t2, graw[:, :, 4:8], dtv, add)
                    S.act(g_t2, g_t2, AF.Exp)
                    S.act(g_t2, g_t2, AF.Ln, bias=1.0)
                    S.tt(g_b162, g_t2, nAv, mult)
                    S.copy(g_g, g_b162)
                    yield
                    for j in range(4):
                        ps = rg()
                        for q in range(4):
                            src = qkvc[2 + q] if q < 2 else qkvc[4 + (q - 2)]
                            S.tr(ps[:, q * 128:(q + 1) * 128], src[:, j * 128:(j + 1) * 128], cs(C_ID), signal=(q == 3))
                        evac(raw[j][:, 0:512], ps)
                        yield
                    for pr in range(2):
                        S.memset(gvnE[pr], 0.0, eng="pool")
                        S.memset(gvnO[pr], 0.0, eng="pool")
                    S.memset(gkbg, 0.0, eng="pool")
                    S.memset(gkdec, 0.0, eng="pool")
                    yield
                    strv = cst[:, C_STR:C_STR + 128].re("p (o c) -> p o c", o=1).bc([128, 4, 128])
                    idv = cst[:, C_ID:C_ID + 128].re("p (o c) -> p o c", o=1).bc([128, 4, 128])
                    for j in range(4):
                        cols = slice(j * 128, (j + 1) * 128)
                        gj = g_g[:, j, :]
                        bj = g_beta[:, j, :]
                        S.tt(gG1, Uv, gj.re("p (h o) -> p h o", o=1).bc([128, 4, 128]), mult)
                        ps = rg()
                        S.mm(ps, onesb, gG1.re("p h c -> p (h c)"))
                        S.act(gEg.re("p h c -> p (h c)"), ps, AF.Exp)
                        yield
                        psD = rg()
                        for h in range(4):
                            o_ = psD[:, h * 128:(h + 1) * 128]
                            S.mm(o_, gG1[:, h, :], onesb, start=True, stop=False)
                            S.mm(o_, negonesb, gG1[:, h, :], start=False, stop=False)
                            S.mm(o_, identb, maskCSb, start=False, stop=True)
                        S.act(gdec.re("p h c -> p (h c)"), psD, AF.Exp)
                        yield
                        psK = rg()
                        psQ = rg()
                        for h in (0, 2, 1, 3):
                            pr, hf = h // 2, h % 2
                            rows = slice(hf * 64, (hf + 1) * 64)
                            S.mm(psK[:, h * 128:(h + 1) * 128], qkvc[2 + pr][rows, cols], qkvc[2 + pr][rows, cols])
                            S.mm(psQ[:, h * 128:(h + 1) * 128], qkvc[pr][rows, cols], qkvc[2 + pr][rows, cols])
                        S.tt(gA.re("p h c -> p (h c)"), gdec.re("p h c -> p (h c)"), psK, mult)
                        S.tt(gQKd.re("p h c -> p (h c)"), gdec.re("p h c -> p (h c)"), psQ, mult)
                        yield
                        S.tt(gA, gA, bj.re("p (h o) -> p h o", o=1).bc([128, 4, 128]), mult)
                        S.tt(gL, gA, strv, mult)
                        yield
                        ps = rg()
                        for h in range(4):
                            S.tr(ps[:, h * 128:(h + 1) * 128], gL[:, h, :], cs(C_ID), signal=(h == 3))
                        evac(gMt.re("p h c -> p (h c)"), ps)
                        yield
                        ps = rg()
                        for h in range(4):
                            S.tr(ps[:, h * 128:(h + 1) * 128], gQKd[:, h, :], cs(C_ID), signal=(h == 3))
                        evac(gQKdT.re("p h c -> p (h c)"), ps)
                        yield
                        S.tt(gP, idv, gMt, sub)
                        yield
                        cur = 0
                        for it in range(1, 6):
                            Lc, Mc = gLp[cur], gMp[cur]
                            Ln_, Mn_ = gLp[1 - cur], gMp[1 - cur]
                            psL = rg()
                            for h in range(4):
                                S.mm(psL[:, h * 128:(h + 1) * 128], Mc[:, h, :], Lc[:, h, :])
                            if it < 5:
                                psM = rg()
                                for h in range(4):
                                    S.mm(psM[:, h * 128:(h + 1) * 128], Lc[:, h, :], Mc[:, h, :])
                            S.copy(Ln_.re("p h c -> p (h c)"), psL, eng="act")
                            yield
                            if it < 5:
                                S.copy(Mn_.re("p h c -> p (h c)"), psM, eng="dve")
                                yield
                            psP = rg()
                            for h in range(4):
                                S.mm(psP[:, h * 128:(h + 1) * 128], Ln_[:, h, :], gP[:, h, :])
                            S.tt(gP.re("p h c -> p (h c)"), gP.re("p h c -> p (h c)"), psP, add)
                            yield
                            cur = 1 - cur
                        psg = rg()
                        S.mm(psg[:, 0:4], cs(C_U), gj)
                        S.mm(psg[:, 4:8], cs(C_SU), gj)
                        S.act(st4b[0], psg[:, 0:4], AF.Exp)
                        S.act(st4b[1], psg[:, 4:8], AF.Exp)
                        S.tt(st4b[2], st4b[0], bj, mult)
                        yield
                        ktv = raw[j][:, 0:256]
                        vtv = raw[j][:, 256:512]
                        for h in range(4):
                            hf = h % 2
                            S.act(gkbg[:, h, hf * 64:(hf + 1) * 64], ktv[:, h * 64:(h + 1) * 64], AF.Copy,
                                  scale=st4b[2][:, h:h + 1])
                            S.act(gkdec[:, h, hf * 64:(hf + 1) * 64], ktv[:, h * 64:(h + 1) * 64], AF.Copy,
                                  scale=st4b[1][:, h:h + 1])
                        S.tt(gvb, vtv.re("p (h e) -> p h e", h=4), bj.re("p (h o) -> p h o", o=1).bc([128, 4, 64]), mult)
                        yield
                        psu = rg()
                        for h in range(4):
                            S.mm(psu[:, h * 64:(h + 1) * 64], gP[:, h, :], gvb[:, h, :])
                        evac(gu, psu[:, 0:256])
                        yield
                        psw = rg()
                        for pr in range(2):
                            S.mm(psw[:, pr * 128:(pr + 1) * 128], gkbg[:, 2 * pr, :], gP[:, 2 * pr, :], start=True, stop=False)
                            S.mm(psw[:, pr * 128:(pr + 1) * 128], gkbg[:, 2 * pr + 1, :], gP[:, 2 * pr + 1, :], start=False, stop=True)
                        evac(gwT.re("p a b -> p (a b)"), psw[:, 0:256])
                        yield
                        for pr in range(2):
                            for hf in range(2):
                                rows = slice(hf * 64, (hf + 1) * 64)
                                S.tt(gqdT[pr][rows, :], qkvc[pr][rows, cols], gEg[rows, 2 * pr + hf, :], mult)
                        egv = gEg.re("p h (ch c) -> p h ch c", ch=2)[:, :, :, 63].re("p (pr hf) ch -> p pr hf ch", hf=2)
                        S.copy(dsel2[0:64], egv[0:64, :, 0, :])
                        S.copy(dsel2[64:128], egv[64:128, :, 1, :])
                        yield
                        for ch in range(2):
                            rows = slice(ch * 64, (ch + 1) * 64)
                            cc = slice(ch * 64, (ch + 1) * 64)
                            psO = pm[0]
                            for pr in range(2):
                                psV = rg()
                                S.mm(psV[:, 0:128], gwT[:, pr, :], Sbd[l][pr])
                                S.tt(gvnE[pr][rows, 0:64], gu[rows, pr * 128:pr * 128 + 64], psV[rows, 0:64], sub)
                                S.tt(gvnO[pr][rows, 64:128], gu[rows, pr * 128 + 64:pr * 128 + 128], psV[rows, 64:128], sub)
                                yield
                                o_ = psO[:, pr * 64:(pr + 1) * 64]
                                S.mm(o_, Sbd[l][pr], gqdT[pr][:, cc], start=True, stop=False)
                                S.mm(o_, gvnE[pr][rows, :], gQKdT[rows, 2 * pr, cc], start=False, stop=False)
                                S.mm(o_, gvnO[pr][rows, :], gQKdT[rows, 2 * pr + 1, cc], start=False, stop=True)
                                psS_ = rg()
                                S.mm(psS_[:, 0:128], gkdec[rows, 2 * pr, :], gvnE[pr][rows, :], start=True, stop=False)
                                S.mm(psS_[:, 0:128], gkdec[rows, 2 * pr + 1, :], gvnO[pr][rows, :], start=False, stop=True)
                                S.stt(Sbd[l][pr], Sbd[l][pr], dsel2[:, pr, ch:ch + 1], psS_[:, 0:128], mult, add)
                                yield
                            evac(oT2[:, :, j * 128 + ch * 64:j * 128 + (ch + 1) * 64], psO[:, 0:128].re("p (pr c) -> p pr c", pr=2))
                            yield
                gens = [gdn_gen(), swa_gen(), mlstm_gen()]
                while gens:
                    for g_ in list(gens):
                        try:
                            next(g_)
                        except StopIteration:
                            gens.remove(g_)
                S.handoff(cW.views, hn)
                for pr in range(2):
                    headnorm(oT[:, pr, :], P_MLN + l * 2 + pr, mos[pr], yA[2 + pr])

                for pr in range(2):
                    headnorm(oT2[:, pr, :], P_GDNN + l, gzs[pr], yA[pr])

                if STOP <= 4:
                    break
                ymix = [yA[0], yA[1], yA[2], yA[3], ysw[:, 0, :], ysw[:, 1, :], ysw[:, 2, :], ysw[:, 3, :]]
                for c in range(2):
                    sl = use_w(l, 8 + c)
                    for mi in range(4):
                        ps = dense_fm(sl, 8, 512, mi, ymix)
                        m = c * 4 + mi
                        S.tt(xT[m], xT[m], ps, add)
                    done_w()
                rmsnorm(G_MLP + l * 8, hn)
                arena_switch(cU)
                for c in range(8):
                    sl = use_w(l, 10 + c)
                    for mi in range(4):
                        ps = dense_fm(sl, 8, 512, mi, hn)
                        tr_ = tmpA if mi % 2 == 0 else tmpB
                        S.act(tr_, ps, AF.Relu)
                        S.tt(u[c * 4 + mi], tr_, tr_, mult, eng=("pool" if mi % 2 == 0 else "dve"))
                    done_w()
                for c in range(8):
                    sl = use_w(l, 18 + c)
                    ps = dense_fm(sl, 32, 128, 0, u)
                    S.tt(xT[c], xT[c], ps, add)
                    done_w()
                rmsnorm(G_PLE + l * 8, hn)
                slp = use_w(l, 26)
                for c in range(2):
                    sl = use_w(l, 27 + c)
                    for mi in range(4):
                        m = c * 4 + mi
                        psg = dense_fm(sl, 8, 512, mi, hn)
                        psp = dense_fm(slp, 2, 1024, m, pT)
                        S.act(tmpA, psg, AF.Sigmoid)
                        S.tt(tmpB, tmpA, psp, mult)
                        S.tt(xT[m], xT[m], tmpB, add, eng="pool")
                done_w()
                done_w()
                done_w()

            if 'f' not in SKIP:
                rmsnorm(G_FIN, xT, inplace=True)
            for j in range(4):
                for half in range(2):
                    ps = pdn()
                    for q in range(4):
                        kt = half * 4 + q
                        S.tr(ps[:, q * 128:(q + 1) * 128], xT[kt][:, j * 128:(j + 1) * 128], cs(C_ID), signal=(q == 3))
                    evac(xin[:, half * 512:(half + 1) * 512], ps)
                ev = S.dma(XQ, out_d[tok0 + j * 128:tok0 + (j + 1) * 128, :], xin)
                out_events.append(ev)
        S.finish(out_events[-8:])
        print("instructions:", S.ninstr, "dma sems:", S.ndsem, flush=True)
    return nc


def _bcast_inputs(inp, b, NT, par, cst):
    SEQ = NT * TT
    return {
        "x": np.ascontiguousarray(inp["x"][b, :SEQ]),
        "p": np.ascontiguousarray(inp["p"][:, b, :SEQ]),
        "pos": np.ascontiguousarray(inp["positions"][b:b + 1, :SEQ]).astype(np.int32),
        "w_in": inp["w_in"], "w_out": inp["w_out"], "w_up": inp["w_up"], "w_down": inp["w_down"],
        "w_gate": inp["w_ple_gate"], "w_proj": inp["w_ple_proj"],
        "par": par, "cst": cst,
    }


def kernel(**inputs):
    inp = {k: np.asarray(v) for k, v in inputs.items()}
    NT, NL = 8, 4
    par, cst = host_tables(inp)
    nc = build(NT, NL)
    in_maps = [_bcast_inputs(inp, b, NT, par, cst) for b in range(8)]
    res = run_bass_kernel_spmd(nc, in_maps, core_ids=list(range(8)))
    return np.stack([res.results[b]["out"] for b in range(8)], axis=0).astype(np.float32)
```

```python
import contextlib
import os
import numpy as np
import concourse.bass as bass
import concourse.mybir as mybir
from concourse.bass_utils import run_bass_kernel_spmd
import ml_dtypes

F32 = mybir.dt.float32
BF16 = mybir.dt.bfloat16
I32 = mybir.dt.int32
AF = mybir.ActivationFunctionType
ALU = mybir.AluOpType
AX = mybir.AxisListType


class Buf:
    __slots__ = ("name", "w", "r", "dsem", "dcnt", "kind")

    def __init__(self, name, kind="sb"):
        self.name = name
        self.kind = kind
        self.w = None
        self.r = {}
        self.dsem = None
        self.dcnt = 0


class V:
    __slots__ = ("ap", "bufs")

    def __init__(self, ap, bufs):
        self.ap = ap
        self.bufs = bufs

    def __getitem__(self, key):
        return V(self.ap[key], self.bufs)

    def re(self, s, **kw):
        return V(self.ap.rearrange(s, **kw), self.bufs)

    def bc(self, shape):
        return V(self.ap.to_broadcast(shape), self.bufs)

    def bitcast(self, dt):
        return V(self.ap.bitcast(dt), self.bufs)

    @property
    def shape(self):
        return self.ap.shape


class Sched:
    def __init__(self, nc, es):
        self.nc = nc
        self.es = es
        self.sems = {}
        self.cnt = {}
        self.known = {}
        self.eng = {"pe": nc.tensor, "act": nc.scalar, "dve": nc.vector,
                    "pool": nc.gpsimd, "sp": nc.sync}
        for k in self.eng:
            self.sems[k] = es.enter_context(nc.semaphore("s_" + k))
            self.cnt[k] = 0
            self.known[k] = {}
        self.ndsem = 0
        self.pe_pending = False
        self.out_events = []
        self.ninstr = 0

    def sb(self, name, shape, dt):
        t = self.es.enter_context(self.nc.sbuf_tensor(name, list(shape), dt))
        return V(t[:], [Buf(name)])

    def ps(self, name, shape, dt):
        t = self.es.enter_context(self.nc.psum_tensor(name, list(shape), dt))
        return V(t[:], [Buf(name, "ps")])

    def dram(self, ap, name):
        return V(ap, [Buf(name, "dram")])

    def _dsem(self, buf):
        if buf.dsem is None:
            key = "d%d" % self.ndsem
            self.ndsem += 1
            self.sems[key] = self.es.enter_context(self.nc.semaphore(key))
            buf.dsem = key
        return buf.dsem

    def _deps(self, ek, reads, writes):
        deps = {}

        def add(ev):
            if ev is None:
                return
            k, v = ev
            if deps.get(k, 0) < v:
                deps[k] = v
        for b in reads:
            add(b.w)
            if b.kind == "ps":
                for k, v in b.r.items():
                    if k != ek:
                        add((k, v))
        for b in writes:
            add(b.w)
            for k, v in b.r.items():
                add((k, v))
        kn = self.known[ek]
        for k, v in deps.items():
            if k == "pe" and ek == "pe":
                continue
            if kn.get(k, 0) >= v:
                continue
            assert not (k == "pe" and v > self.cnt["pe"]), "wait on unsignalled PE event (deadlock)"
            self.eng[ek].wait_ge(self.sems[k], v)
            kn[k] = v

    def _commit(self, ev, reads, writes):
        k, v = ev
        for b in writes:
            b.w = ev
            b.r = {}
        for b in reads:
            if b.r.get(k, 0) < v:
                b.r[k] = v

    def op(self, ek, fn, outs, ins, signal=True):
        reads = [b for v in ins for b in v.bufs]
        writes = [b for v in outs for b in v.bufs]
        self._deps(ek, reads, writes)
        ins_ = fn()
        self.ninstr += 1
        if ek == "pe" and not signal:
            ev = ("pe", self.cnt["pe"] + 1)
            self.pe_pending = True
        else:
            self.cnt[ek] += 1
            ev = (ek, self.cnt[ek])
            ins_.then_inc(self.sems[ek], 1)
            if ek == "pe":
                self.pe_pending = False
        self._commit(ev, reads, writes)
        return ev

    def dma(self, qk, out, in_, cast=False, **kw):
        reads = list(in_.bufs)
        writes = list(out.bufs)
        self._deps(qk, reads, writes)
        owner = writes[0]
        if owner.kind == "dram" and reads and reads[0].kind != "dram":
            owner = reads[0]
        key = self._dsem(owner)
        owner.dcnt += 16
        ev = (key, owner.dcnt)
        self.eng[qk].dma_start(out=out.ap, in_=in_.ap, **kw).then_inc(self.sems[key], 16)
        self.ninstr += 1
        self._commit(ev, reads, writes)
        return ev

    def handoff(self, old, new):
        evs = {}
        for v in old:
            for b in v.bufs:
                if b.w is not None:
                    evs[b.w[0]] = max(evs.get(b.w[0], 0), b.w[1])
                for k, val in b.r.items():
                    evs[k] = max(evs.get(k, 0), val)
        for v in new:
            for b in v.bufs:
                b.w = None
                b.r = dict(evs)

    def finish(self, evs):
        for k, v in evs:
            self.eng["sp"].wait_ge(self.sems[k], v)

    def mm(self, out, lhsT, rhs, start=True, stop=True, signal=None, **kw):
        if signal is None:
            signal = stop
        rows = lhsT.ap.shape[0]
        rg = (lhsT.ap.base_partition(), rows) if rows < 128 else None
        b = out.bufs[0]
        if not hasattr(self, "bank_rg"):
            self.bank_rg = {}
        if rg is not None:
            signal = True
            prev = self.bank_rg.get(b)
            if prev is not None and prev[0] != rg:
                self.nc.tensor.wait_ge(self.sems["pe"], prev[1])
        else:
            self.bank_rg.pop(b, None)
        ev = self.op("pe", lambda: self.nc.tensor.matmul(
            out.ap, lhsT=lhsT.ap, rhs=rhs.ap, start=start, stop=stop, **kw),
            [out], [lhsT, rhs], signal=signal)
        if rg is not None:
            self.bank_rg[b] = (rg, ev[1])
        return ev

    def tr(self, out, in_, ident, signal=True):
        if hasattr(self, "bank_rg"):
            self.bank_rg.pop(out.bufs[0], None)
        return self.op("pe", lambda: self.nc.tensor.transpose(out.ap, in_.ap, ident.ap),
                       [out], [in_, ident], signal=signal)

    def act(self, out, in_, func, bias=None, scale=1.0, accum=None, eng="act"):
        ins = [in_]
        kw = {}
        if bias is not None:
            if isinstance(bias, V):
                ins.append(bias)
                kw["bias"] = bias.ap
            else:
                kw["bias"] = bias
        if isinstance(scale, V):
            ins.append(scale)
            kw["scale"] = scale.ap
        else:
            kw["scale"] = scale
        outs = [out]
        if accum is not None:
            outs.append(accum)
            kw["accum_out"] = accum.ap
        return self.op("act", lambda: self.nc.scalar.activation(
            out=out.ap, in_=in_.ap, func=func, **kw), outs, ins)

    pool_busy = False

    def _ve(self, eng):
        return self.nc.vector if eng == "dve" else self.nc.gpsimd

    def _rm(self, eng):
        return "dve" if (eng == "pool" and self.pool_busy) else eng

    def tt(self, out, a, b, op, eng="dve"):
        eng = self._rm(eng)
        return self.op(eng, lambda: self._ve(eng).tensor_tensor(
            out=out.ap, in0=a.ap, in1=b.ap, op=op), [out], [a, b])

    def ts(self, out, a, s1, op0, s2=None, op1=None, eng="dve", accum=None):
        eng = self._rm(eng)
        ins = [a]
        s1a = s1.ap if isinstance(s1, V) else s1
        s2a = s2.ap if isinstance(s2, V) else s2
        if isinstance(s1, V):
            ins.append(s1)
        if isinstance(s2, V):
            ins.append(s2)
        kw = {}
        if op1 is not None:
            kw["op1"] = op1
        outs = [out]
        if accum is not None:
            outs.append(accum)
            kw["accum_out"] = accum.ap
        return self.op(eng, lambda: self._ve(eng).tensor_scalar(
            out=out.ap, in0=a.ap, scalar1=s1a, scalar2=s2a, op0=op0, **kw), outs, ins)

    def stt(self, out, a, s, b, op0, op1, eng="dve"):
        eng = "dve"
        ins = [a, b]
        sa = s.ap if isinstance(s, V) else s
        if isinstance(s, V):
            ins.append(s)
        return self.op(eng, lambda: self._ve(eng).scalar_tensor_tensor(
            out=out.ap, in0=a.ap, scalar=sa, in1=b.ap, op0=op0, op1=op1), [out], ins)

    def copy(self, out, in_, eng="dve"):
        eng = self._rm(eng)
        if eng == "act":
            return self.op("act", lambda: self.nc.scalar.copy(out=out.ap, in_=in_.ap), [out], [in_])
        return self.op(eng, lambda: self._ve(eng).tensor_copy(out=out.ap, in_=in_.ap), [out], [in_])

    def memset(self, out, val, eng="dve"):
        eng = self._rm(eng)
        return self.op(eng, lambda: self._ve(eng).memset(out.ap, val), [out], [])

    def reduce(self, out, in_, op, axis=AX.X, eng="dve"):
        eng = self._rm(eng)
        return self.op(eng, lambda: self._ve(eng).tensor_reduce(
            out=out.ap, in_=in_.ap, axis=axis, op=op), [out], [in_])

    def recip(self, out, in_):
        return self.op("dve", lambda: self.nc.vector.reciprocal(out=out.ap, in_=in_.ap), [out], [in_])


DM = 1024
TT = 512
NEG = -30000.0
EPS = 1e-6
NSLOT = 3
NCHUNK = 30
G_MIX, G_MLP, G_PLE, G_FIN = 0, 32, 64, 96
P_CONV, P_GDNN, P_MLN, P_ALOG, P_DTB, P_IB, P_FB, P_SINK, P_C1, P_SGN = 104, 200, 204, 212, 228, 244, 260, 276, 308, 309
NPAR = 310
C_ID, C_BD, C_U, C_SU, C_MT, C_MCS, C_STR, C_E, C_O, C_SW, C_SW0 = 0, 128, 256, 384, 512, 640, 768, 896, 1024, 1152, 1408
NCST = 1664


def host_tables(inp):
    par = np.zeros((128, NPAR), np.float32)
    p = np.arange(128)
    for l in range(4):
        for kt in range(8):
            par[:, G_MIX + l * 8 + kt] = inp["norm_mix"][l, kt * 128:(kt + 1) * 128]
            par[:, G_MLP + l * 8 + kt] = inp["norm_mlp"][l, kt * 128:(kt + 1) * 128]
            par[:, G_PLE + l * 8 + kt] = inp["norm_ple"][l, kt * 128:(kt + 1) * 128]
        for m in range(6):
            for j in range(4):
                par[:, P_CONV + l * 24 + m * 4 + j] = inp["conv_w"][l, j, m * 128:(m + 1) * 128]
        par[:, P_GDNN + l] = inp["gdn_norm"][l, p % 64]
        for pr in range(2):
            par[:, P_MLN + l * 2 + pr] = inp["mlstm_norm"][l, pr * 128:(pr + 1) * 128]
        for h in range(4):
            par[:, P_ALOG + l * 4 + h] = inp["gdn_a_log"][l, h]
            par[:, P_DTB + l * 4 + h] = inp["gdn_dt_bias"][l, h]
            par[:, P_IB + l * 4 + h] = inp["mlstm_i_bias"][l, h]
            par[:, P_FB + l * 4 + h] = inp["mlstm_f_bias"][l, h]
        for i in range(4):
            for g in range(2):
                par[:, P_SINK + l * 8 + i * 2 + g] = inp["attn_sinks"][l, i + 4 * g]
    for kt in range(8):
        par[:, G_FIN + kt] = inp["norm_final"][kt * 128:(kt + 1) * 128]
    inv_freq = (500000.0 ** (-np.arange(0, 16, 2, dtype=np.float32) / 16)).astype(np.float32)
    d = p % 64
    par[:, P_C1] = np.where(d < 16, inv_freq[d % 8] / (2 * np.pi), 0.0)
    par[:, P_SGN] = np.where(d < 8, -1.0, np.where(d < 16, 1.0, 0.0))
    cst = np.zeros((128, NCST), np.float32)
    a = p[:, None]
    b = p[None, :]
    same = (a // 64) == (b // 64)
    cst[:, C_ID:C_ID + 128] = (a == b)
    cst[:, C_BD:C_BD + 128] = same
    cst[:, C_U:C_U + 128] = same & (a <= b)
    cst[:, C_SU:C_SU + 128] = same & (a > b)
    cst[:, C_MT:C_MT + 128] = np.where(same & (a <= b), 0.0, NEG)
    cst[:, C_MCS:C_MCS + 128] = np.where(same & (b <= a), 0.0, NEG)
    cst[:, C_STR:C_STR + 128] = same & (b < a)
    cst[:, C_E:C_E + 128] = (b < 64)
    cst[:, C_O:C_O + 128] = (b >= 64)
    k = np.arange(256)[None, :]
    ok = (k > a) & (k <= a + 128)
    cst[:, C_SW:C_SW + 256] = np.where(ok, 0.0, NEG)
    cst[:, C_SW0:C_SW0 + 256] = np.where(ok & (k >= 128), 0.0, NEG)
    return par, cst


def build(NT, NL):
    STOP = int(os.environ.get('MK_STOP', '99'))
    XQ = os.environ.get('MK_XQ', 'act')
    SKIP = os.environ.get('MK_SKIP', '')
    GW = [int(v) for v in os.environ.get('MK_GW', '2,1,1').split(',')]
    SWA_ST = int(os.environ.get('MK_SWA', '9'))
    SEQ = NT * TT
    nc = bass.Bass("TRN2", target_bir_lowering=False)
    es = contextlib.ExitStack()
    with es:
        S = Sched(nc, es)
        mult, add, sub, mx_, mn_ = ALU.mult, ALU.add, ALU.subtract, ALU.max, ALU.min

        def din(name, shape, dt=F32):
            return S.dram(nc.dram_tensor(name, list(shape), dt, kind="ExternalInput").ap(), name)
        x_d = din("x", [SEQ, DM])
        p_d = din("p", [4, SEQ, 256])
        pos_d = din("pos", [1, SEQ], I32)
        w_in = din("w_in", [4, DM, 2832])
        w_out = din("w_out", [4, DM, DM])
        w_up = din("w_up", [4, DM, 4096])
        w_dn = din("w_down", [4, 4096, DM])
        w_gt = din("w_gate", [4, DM, DM])
        w_pj = din("w_proj", [4, 256, DM])
        par_d = din("par", [128, NPAR])
        cst_d = din("cst", [128, NCST])
        out_d = S.dram(nc.dram_tensor("out", [SEQ, DM], F32, kind="ExternalOutput").ap(), "out")
        wsc_ap = nc.dram_tensor("wsc", [4, NCHUNK, 128, 4096], BF16, kind="Internal").ap()
        GROUPS = {"in": range(0, 8), "out": range(8, 10), "up": range(10, 18),
                  "dn": range(18, 26), "pl": range(26, 29)}
        wbuf = [{g: Buf("w%d%s" % (l, g), "dram") for g in GROUPS} for l in range(4)]

        def wsc(l, c):
            for g, r in GROUPS.items():
                if c in r:
                    return V(wsc_ap[l, c], [wbuf[l][g]])

        par = S.sb("par_sb", [128, NPAR], F32)
        cst = S.sb("cst_sb", [128, NCST], F32)
        xT = [S.sb("xT%d" % k, [128, TT], F32) for k in range(8)]
        hnbuf = es.enter_context(nc.sbuf_tensor("hnbuf", [128, 2048], F32))
        hnbuf_ap = hnbuf[:]
        hn = [V(hnbuf_ap[:, k * 256:(k + 1) * 256].bitcast(BF16), [Buf("hn%d" % k)]) for k in range(8)]
        sqb = [S.sb("sqb%d" % k, [128, TT], BF16) for k in range(2)]
        rstd = S.sb("rstd", [128, TT], F32)
        tmpA = S.sb("tmpA", [128, TT], F32)
        tmpB = S.sb("tmpB", [128, TT], F32)
        slots = [S.sb("slot%d" % k, [128, 4096], BF16) for k in range(NSLOT)]
        identb = S.sb("identb", [128, 128], BF16)
        onesb = S.sb("onesb", [128, 128], BF16)
        negonesb = S.sb("negonesb", [128, 128], BF16)
        bdonesb = S.sb("bdonesb", [128, 128], BF16)
        maskTb = S.sb("maskTb", [128, 128], BF16)
        maskCSb = S.sb("maskCSb", [128, 128], BF16)
        onesEb = S.sb("onesEb", [128, 128], BF16)
        onesOb = S.sb("onesOb", [128, 128], BF16)
        nA = S.sb("nA", [128, 16], F32)
        negsink = S.sb("negsink", [128, 32], F32)
        raw = [S.sb("raw%d" % m, [128, TT + 3], F32) for m in range(6)]
        qkvc = [S.sb("qkvc%d" % m, [128, TT], F32) for m in range(6)]
        gzs = [S.sb("gzs%d" % m, [128, TT], BF16) for m in range(2)]
        mos = [S.sb("mos%d" % m, [128, TT], BF16) for m in range(2)]
        ov2 = es.enter_context(nc.sbuf_tensor("ov2", [128, 2048], F32))
        ov2_ap = ov2[:]
        mqf = [V(ov2_ap[:, m * 512:(m + 1) * 512], [Buf("mqf%d" % m)]) for m in range(2)]
        mkf = [V(ov2_ap[:, 1024 + m * 512:1024 + (m + 1) * 512], [Buf("mkf%d" % m)]) for m in range(2)]
        sqf = [S.sb("sqf%d" % m, [128, TT], BF16) for m in range(4)]
        skf = S.sb("skf", [128, 128 + TT], BF16)
        ropeC = S.sb("ropeC", [128, TT], F32)
        ropeS = S.sb("ropeS", [128, TT], F32)
        posi = S.sb("posi", [128, TT], I32)
        mk_tm = S.sb("mk_tm", [128, 4, 256], F32)
        mvpad = S.sb("mvpad", [128, 4, 2, 2, 128], BF16)
        svpad = S.sb("svpad", [128, 5, 2, 128], BF16)
        graw = S.sb("graw", [128, 4, 16], F32)
        yA = [S.sb("yA%d" % m, [128, TT], BF16) for m in range(4)]
        ysw = S.sb("ysw", [128, 4, TT], BF16)
        oT = S.sb("oT", [128, 2, TT], F32)
        pin = S.sb("pin", [128, 4, 256], F32)
        pT = [S.sb("pT%d" % k, [128, TT], BF16) for k in range(2)]
        xin = S.sb("xin", [128, DM], F32)
        Sbd = [[S.sb("Sbd%d_%d" % (l, pr), [128, 128], F32) for pr in range(2)] for l in range(NL)]
        CN = [[S.sb("CN%d_%d" % (l, pr), [128, 256], F32) for pr in range(2)] for l in range(NL)]
        CNb = [[S.sb("CNb%d_%d" % (l, pr), [128, 256], BF16) for pr in range(2)] for l in range(NL)]
        ctail = [S.sb("ctail%d" % l, [128, 6, 3], F32) for l in range(NL)]
        kprev = [S.sb("kprev%d" % l, [128, 128], BF16) for l in range(NL)]
        vprev = [S.sb("vprev%d" % l, [128, 2, 128], BF16) for l in range(NL)]
        g_ig = S.sb("g_ig", [128, 4, 4], F32)
        g_lf = S.sb("g_lf", [128, 4, 4], F32)
        g_b16 = S.sb("g_b16", [128, 4, 4], BF16)
        g_beta = S.sb("g_beta", [128, 4, 4], F32)
        g_g = S.sb("g_g", [128, 4, 4], F32)
        g_t = S.sb("g_t", [128, 4, 4], F32)
        g_t2 = S.sb("g_t2", [128, 4, 4], F32)
        g_b162 = S.sb("g_b162", [128, 4, 4], BF16)
        st8 = [S.sb("st8_%d" % k, [128, 8], F32) for k in range(6)]
        st4 = [S.sb("st4_%d" % k, [128, 4], F32) for k in range(4)]
        dsel = S.sb("dsel", [128, 2, 2], F32)
        dsel2 = S.sb("dsel2", [128, 2, 2], F32)
        st4b = [S.sb("st4b_%d" % k, [128, 4], F32) for k in range(4)]
        oT2 = S.sb("oT2", [128, 2, TT], F32)
        ARN = 9216
        arena = es.enter_context(nc.sbuf_tensor("arena", [128, ARN], F32))
        arena_ap = arena[:]

        class Carver:
            def __init__(self, tag, base=None, size=None):
                self.off = 0
                self.tag = tag
                self.views = []
                self.base = arena_ap if base is None else base
                self.size = ARN if size is None else size

            def f32(self, shape):
                n = int(np.prod(shape[1:]))
                ap = self.base[:, self.off:self.off + n]
                self.off += n
                assert self.off <= self.size, (self.tag, self.off)
                v = V(ap, [Buf("%s%d" % (self.tag, len(self.views)))])
                self.views.append(v)
                if len(shape) == 3:
                    return v.re("p (a b) -> p a b", a=shape[1])
                return v

            def bf16(self, shape):
                n = int(np.prod(shape[1:]))
                assert n % 2 == 0
                ap = self.base[:, self.off:self.off + n // 2].bitcast(BF16)
                self.off += n // 2
                assert self.off <= self.size, (self.tag, self.off)
                v = V(ap, [Buf("%s%d" % (self.tag, len(self.views)))])
                self.views.append(v)
                if len(shape) == 3:
                    return v.re("p (a b) -> p a b", a=shape[1])
                return v
        cU = Carver("u")
        u = [cU.bf16([128, TT]) for _ in range(32)]
        cH = Carver("h")
        hn32 = [cH.f32([128, TT]) for _ in range(8)]
        w32 = cH.f32([128, 8, 512])
        cR = Carver("r")
        sqt1 = [cR.f32([128, TT]) for _ in range(4)]
        cW = Carver("w", base=hnbuf_ap, size=2048)
        sm = [cW.f32([128, 2, 256]) for _ in range(2)]
        Pn = [cW.bf16([128, 2, 256]) for _ in range(2)]
        PTs = cW.bf16([128, 8, 128])
        cM = Carver("gm")
        cG = cM
        mG1 = cM.bf16([128, 4, 128])
        mEb = cM.f32([128, 4, 128])
        mPT = cM.f32([128, 4, 128])
        mW = cM.bf16([128, 4, 128])
        mqdec = [cM.bf16([128, 128]) for _ in range(2)]
        mkdec = cM.bf16([128, 4, 128])
        mdab = cM.f32([128, 2, 64])
        gG1 = cG.bf16([128, 4, 128])
        gEg = cG.f32([128, 4, 128])
        gdec = cG.f32([128, 4, 128])
        gA = cG.f32([128, 4, 128])
        gL = cG.f32([128, 4, 128])
        gQKd = cG.f32([128, 4, 128])
        gMt = cG.f32([128, 4, 128])
        gQKdT = cG.f32([128, 4, 128])
        gLp = [gL, gdec]
        gMp = [gMt, gA]
        gP = gQKd
        gkbg = cG.f32([128, 4, 128])
        gkdec = cG.f32([128, 4, 128])
        gvb = cG.f32([128, 4, 64])
        gu = cG.f32([128, 256])
        gwT = cG.f32([128, 2, 128])
        gqdT = [cG.f32([128, 128]) for _ in range(2)]
        gvnE = [cG.f32([128, 128]) for _ in range(2)]
        gvnO = [cG.f32([128, 128]) for _ in range(2)]
        arena_user = [None]

        def arena_switch(c):
            if arena_user[0] is not None and arena_user[0] is not c:
                S.handoff(arena_user[0].views, c.views)
            arena_user[0] = c

        pd = [S.ps("pd%d" % k, [128, 512], F32) for k in range(4)]
        pm = [S.ps("pm%d" % k, [128, 512], F32) for k in range(4)]
        rot = {"d": 0, "m": 0}

        def pdn():
            rot["d"] = (rot["d"] + 1) % 4
            return pd[rot["d"]]

        def pmn():
            rot["m"] = (rot["m"] + 1) % 4
            return pm[rot["m"]]

        rot3 = {"m": 0}

        def pmn3():
            rot3["m"] = rot3["m"] % 3 + 1
            return pm[rot3["m"]]

        def mkrot(banks):
            st_ = {"i": -1}

            def nxt():
                st_["i"] = (st_["i"] + 1) % len(banks)
                return banks[st_["i"]]
            return nxt
        rs = mkrot([pd[2], pd[3]])
        rm = mkrot([pm[3], pd[0], pd[1]])
        rg = mkrot([pm[1], pm[2]])

        alt = {"e": 0}

        def evac_eng():
            alt["e"] ^= 1
            return "act" if alt["e"] else "dve"

        def evac(out, in_):
            S.copy(out, in_, eng=evac_eng())

        def cs(off, n=128):
            return cst[:, off:off + n]

        def pc(col, n=1):
            return par[:, col:col + n]

        uses = []
        for t in range(NT):
            for l in range(NL):
                for c in [1] + list(range(3, 29)):
                    uses.append((l, c))
        wstate = {"next_use": 0, "issued": 0}

        def issue_w():
            i = wstate["issued"]
            if i >= len(uses):
                return
            l, c = uses[i]
            n_ = {7: 1152, 26: 2048}.get(c, 4096)
            S.dma("sp", slots[i % NSLOT][:, 0:n_], wsc(l, c)[:, 0:n_])
            wstate["issued"] = i + 1

        def use_w(l, c):
            i = wstate["next_use"]
            assert uses[i] == (l, c), (uses[i], l, c)
            assert wstate["issued"] > i
            wstate["next_use"] = i + 1
            return slots[i % NSLOT]

        def done_w():
            issue_w()

        CASTSEL = os.environ.get('MK_CAST', 'all')
        ncast = [0]

        def castdma(dst, src):
            ncast[0] += 1
            if CASTSEL != 'all':
                lo, hi = [int(v) for v in CASTSEL.split(':')]
                if not (lo <= ncast[0] - 1 < hi):
                    return
            S.dma("pool", dst, src)

        def wview(l, c, kt, ncol):
            return wsc(l, c)[:, 0:kt * ncol].re("p (k c) -> p k c", k=kt)

        def srcv(w, l, r0, nrow, c0, ncol):
            return V(w.ap[l, r0:r0 + nrow, c0:c0 + ncol], w.bufs).re("(k p) c -> p k c", p=128)

        def emit_casts(l):
            castdma(wview(l, 1, 8, 512), srcv(w_in, l, 0, DM, 512, 512))
            c3 = wview(l, 3, 8, 512)
            castdma(c3[:, :, 0:256], srcv(w_in, l, 0, DM, 1800, 256))
            castdma(c3[:, :, 256:384], srcv(w_in, l, 0, DM, 2576, 128))
            skr = c3[:, :, 384:512].re("p k (g d) -> p k g d", g=2)
            sks = srcv(w_in, l, 0, DM, 2576, 128).re("p k (g d) -> p k g d", g=2)
            for g in range(2):
                castdma(skr[:, :, g, 0:8], sks[:, :, g, 8:16])
                castdma(skr[:, :, g, 8:16], sks[:, :, g, 0:8])
                castdma(skr[:, :, g, 16:64], sks[:, :, g, 16:64])
            c4 = wview(l, 4, 8, 512).re("p k (i g d) -> p k i g d", i=4, g=2)
            c5 = wview(l, 5, 8, 512).re("p k (i g d) -> p k i g d", i=4, g=2)
            for g in range(2):
                sv_ = srcv(w_in, l, 0, DM, 2064 + g * 256, 256).re("p k (i d) -> p k i d", i=4)
                for i in range(4):
                    castdma(c4[:, :, i, g, :], sv_[:, :, i, :])
                    castdma(c5[:, :, i, g, 0:8], sv_[:, :, i, 8:16])
                    castdma(c5[:, :, i, g, 8:16], sv_[:, :, i, 0:8])
                    castdma(c5[:, :, i, g, 16:64], sv_[:, :, i, 16:64])
            castdma(wview(l, 6, 8, 512), srcv(w_in, l, 0, DM, 1288, 512))
            c7 = wview(l, 7, 8, 144)
            castdma(c7[:, :, 0:128], srcv(w_in, l, 0, DM, 2704, 128))
            castdma(c7[:, :, 128:136], srcv(w_in, l, 0, DM, 1024, 8))
            castdma(c7[:, :, 136:144], srcv(w_in, l, 0, DM, 2056, 8))
            for c in range(2):
                dv = wview(l, 8 + c, 8, 512)
                castdma(dv[:, 0:4, :], srcv(w_out, l, 0, 512, c * 512, 512))
                for i in range(4):
                    for g in range(2):
                        r0 = 512 + (i + 4 * g) * 64
                        castdma(V(dv.ap[g * 64:(g + 1) * 64, 4 + i, :], dv.bufs),
                                V(w_out.ap[l, r0:r0 + 64, c * 512:(c + 1) * 512], w_out.bufs))
            for c in range(8):
                castdma(wview(l, 10 + c, 8, 512), srcv(w_up, l, 0, DM, c * 512, 512))
            for c in range(8):
                castdma(wview(l, 18 + c, 32, 128), srcv(w_dn, l, 0, 4096, c * 128, 128))
            castdma(wview(l, 26, 2, 1024), srcv(w_pj, l, 0, 256, 0, 1024))
            for c in range(2):
                castdma(wview(l, 27 + c, 8, 512), srcv(w_gt, l, 0, DM, c * 512, 512))

        S.dma("sp", par, par_d)
        S.dma("sp", cst, cst_d)
        S.pool_busy = True
        if 'c' not in SKIP:
            emit_casts(0)
        S.copy(identb, cs(C_ID))
        S.memset(onesb, 1.0)
        S.memset(negonesb, -1.0)
        S.copy(bdonesb, cs(C_BD))
        S.copy(maskTb, cs(C_MT))
        S.copy(maskCSb, cs(C_MCS))
        S.copy(onesEb, cs(C_E))
        S.copy(onesOb, cs(C_O))
        S.act(nA, par[:, P_ALOG:P_ALOG + 16], AF.Exp)
        S.ts(nA, nA, -1.0, mult)
        S.ts(negsink, par[:, P_SINK:P_SINK + 32], -1.0, mult)
        S.memset(mvpad, 0.0)
        S.memset(svpad, 0.0)
        for l in range(NL):
            for pr in range(2):
                S.memset(Sbd[l][pr], 0.0)
                S.memset(CN[l][pr], 0.0)
                S.memset(CNb[l][pr], 0.0, eng="pool")
            S.memset(ctail[l], 0.0)
            S.memset(kprev[l], 0.0)
            S.memset(vprev[l], 0.0, eng="pool")
        for _ in range(NSLOT if 'w' not in SKIP else 0):
            issue_w()

        Uv = cst[:, C_U:C_U + 128].re("p (o c) -> p o c", o=1).bc([128, 4, 128])

        def rsqrt_from(ps, scale, out):
            S.act(out, ps, AF.Ln, bias=EPS, scale=scale)
            S.act(out, out, AF.Exp, scale=-0.5)

        def rmsnorm(gbase, outs, inplace=False, outs32=None):
            ps = pdn()
            for kt in range(8):
                S.act(sqb[kt % 2], xT[kt], AF.Square)
                S.mm(ps, onesb, sqb[kt % 2], start=(kt == 0), stop=(kt == 7), signal=True)
            rsqrt_from(ps, 1.0 / DM, rstd)
            for kt in range(8):
                if outs32 is not None:
                    S.stt(outs32[kt], xT[kt], pc(gbase + kt), rstd, mult, mult)
                    S.copy(outs[kt], outs32[kt], eng="act")
                else:
                    S.stt(outs[kt], xT[kt], pc(gbase + kt), rstd, mult, mult)

        def dense_fm(slot, kt_n, ncol, mi, rhs_list):
            ps = pdn()
            sv = slot[:, 0:kt_n * ncol].re("p (k c) -> p k c", k=kt_n)
            for k in range(kt_n):
                S.mm(ps, sv[:, k, mi * 128:(mi + 1) * 128], rhs_list[k],
                     start=(k == 0), stop=(k == kt_n - 1))
            return ps

        def headnorm(src, gcol, gate, out):
            S.act(sqb[0], src, AF.Square)
            ps = pmn()
            S.mm(ps, bdonesb, sqb[0])
            rsqrt_from(ps, 1.0 / 64, tmpA)
            S.stt(tmpB, src, pc(gcol), tmpA, mult, mult)
            S.tt(out, tmpB, gate, mult, eng="pool")

        out_events = []
        for t in range(NT):
            tok0 = t * TT
            S.pool_busy = (t == 0)
            for j in range(4 if 'x' not in SKIP else 0):
                S.dma(XQ, xin, x_d[tok0 + j * 128: tok0 + (j + 1) * 128, :])
                for half in range(2):
                    ps = pdn()
                    for q in range(4):
                        kt = half * 4 + q
                        S.tr(ps[:, q * 128:(q + 1) * 128], xin[:, kt * 128:(kt + 1) * 128], cs(C_ID),
                             signal=(q == 3))
                    for q in range(4):
                        kt = half * 4 + q
                        evac(xT[kt][:, j * 128:(j + 1) * 128], ps[:, q * 128:(q + 1) * 128])
            if 'r' not in SKIP:
              S.dma("pool", posi, V(pos_d.ap[0:1, tok0:tok0 + TT].partition_broadcast(128), pos_d.bufs))
              S.copy(tmpA, posi)
              for which, dst in ((0, ropeC), (1, ropeS)):
                  S.ts(tmpB, tmpA, pc(P_C1), mult)
                  if which == 0:
                      S.ts(tmpB, tmpB, 0.25, add)
                  S.copy(posi, tmpB)
                  S.copy(dst, posi)
                  S.tt(tmpB, tmpB, dst, sub)
                  S.stt(tmpB, tmpB, 0.5, tmpB, ALU.is_gt, sub)
                  S.act(dst, tmpB, AF.Sin, scale=-6.28318)
              S.ts(ropeS, ropeS, pc(P_SGN), mult)

            for l in range(NL):
                if STOP <= 0:
                    break
                if t == 0 and l + 1 < NL:
                    emit_casts(l + 1)
                S.dma(XQ, pin, V(p_d.ap[l, tok0:tok0 + TT, :], p_d.bufs).re("(j p) c -> p j c", p=128))
                for k2 in range(2):
                    ps = pdn()
                    for j in range(4):
                        S.tr(ps[:, j * 128:(j + 1) * 128], pin[:, j, k2 * 128:(k2 + 1) * 128], cs(C_ID),
                             signal=(j == 3))
                    evac(pT[k2], ps)

                arena_switch(cH)
                rmsnorm(G_MIX + l * 8, hn, outs32=hn32)
                for m in range(6):
                    S.copy(raw[m][:, 0:3], ctail[l][:, m, :], eng="pool")
                S.copy(skf[:, 0:128], kprev[l], eng="pool")
                S.copy(svpad[:, 0], vprev[l], eng="pool")
                def dense32(mi):
                    ps = pdn()
                    for k in range(8):
                        S.mm(ps, w32[:, k, mi * 128:(mi + 1) * 128], hn32[k], start=(k == 0), stop=(k == 7))
                    return ps
                S.dma("sp", w32, srcv(w_in, l, 0, DM, 0, 512))
                for mi in range(4):
                    ps = dense32(mi)
                    evac(raw[mi][:, 3:TT + 3], ps)
                S.dma("sp", w32, srcv(w_in, l, 0, DM, 1032, 512))
                for mi in range(4):
                    ps = dense32(mi)
                    if mi < 2:
                        evac(mqf[mi], ps)
                    else:
                        S.act(mkf[mi - 2], ps, AF.Copy, scale=0.125)
                arena_switch(cR)
                sl = use_w(l, 1)
                for mi in range(4):
                    ps = dense_fm(sl, 8, 512, mi, hn)
                    if mi < 2:
                        evac(raw[4 + mi][:, 3:TT + 3], ps)
                    else:
                        S.act(gzs[mi - 2], ps, AF.Silu)
                done_w()
                sl = use_w(l, 3)
                for mi in range(2):
                    ps = dense_fm(sl, 8, 512, mi, hn)
                    S.act(mos[mi], ps, AF.Sigmoid)
                ps = dense_fm(sl, 8, 512, 2, hn)
                S.tt(tmpA, ps, ropeC, mult)
                ps = dense_fm(sl, 8, 512, 3, hn)
                S.tt(tmpB, ps, ropeS, mult)
                S.tt(skf[:, 128:128 + TT], tmpA, tmpB, add, eng="pool")
                done_w()
                S.copy(kprev[l], skf[:, TT:TT + 128], eng="pool")
                sl = use_w(l, 4)
                for mi in range(4):
                    ps = dense_fm(sl, 8, 512, mi, hn)
                    S.tt(sqt1[mi], ps, ropeC, mult)
                done_w()
                sl = use_w(l, 5)
                for mi in range(4):
                    ps = dense_fm(sl, 8, 512, mi, hn)
                    S.tt(tmpB, ps, ropeS, mult)
                    S.tt(sqf[mi], sqt1[mi], tmpB, add, eng="pool")
                done_w()
                sl = use_w(l, 6)
                sv3 = sl[:, 0:4096].re("p (k c) -> p k c", k=8)
                for j in range(4):
                    ps = pdn()
                    for kt in range(8):
                        S.mm(ps, hn[kt][:, j * 128:(j + 1) * 128], sv3[:, kt, :], start=(kt == 0), stop=(kt == 7))
                    S.act(mk_tm[:, j, :], ps[:, 0:256], AF.Copy, scale=0.125)
                    pv = ps[:, 256:512].re("p (pr hf d) -> p pr hf d", pr=2, hf=2)
                    S.copy(mvpad[:, j, :, 0, 0:64], pv[:, :, 0, :])
                    S.copy(mvpad[:, j, :, 1, 64:128], pv[:, :, 1, :])
                done_w()
                sl = use_w(l, 7)
                sv3 = sl[:, 0:8 * 144].re("p (k c) -> p k c", k=8)
                for j in range(4):
                    ps = pdn()
                    for kt in range(8):
                        S.mm(ps[:, 0:144], hn[kt][:, j * 128:(j + 1) * 128], sv3[:, kt, :], start=(kt == 0), stop=(kt == 7))
                    S.copy(svpad[:, 1 + j, 0, 0:64], ps[:, 0:64], eng="act")
                    S.copy(svpad[:, 1 + j, 1, 64:128], ps[:, 64:128])
                    S.copy(graw[:, j, :], ps[:, 128:144])
                done_w()
                S.copy(vprev[l], svpad[:, 4], eng="pool")

                if STOP <= 1:
                    break
                arena_switch(cM)
                S.handoff(hn, cW.views)
                def swa_gen():
                    mxs, nbs, ess, rss, dens, t8 = st8
                    for j in range(4):
                        mcol = C_SW0 if (t == 0 and j == 0) else C_SW
                        maskv = cst[:, mcol:mcol + 256].re("p (o k) -> p o k", o=1).bc([128, 2, 256])
                        for hv in range(2):
                            for ii in range(2):
                                i = hv * 2 + ii
                                ps = rs()
                                psv = ps.re("p (g k) -> p g k", g=2)
                                for g in range(2):
                                    S.mm(psv[:, g, :], sqf[i][g * 64:(g + 1) * 64, j * 128:(j + 1) * 128],
                                         skf[g * 64:(g + 1) * 64, j * 128:j * 128 + 256])
                                S.tt(sm[ii], psv, maskv, add)
                                S.reduce(mxs[:, ii * 2:ii * 2 + 2], sm[ii], mx_)
                                yield
                            ns = negsink[:, l * 8 + hv * 4:l * 8 + hv * 4 + 4]
                            S.stt(nbs[:, 0:4], mxs[:, 0:4], -0.125, ns, mult, mn_)
                            S.tt(t8[:, 0:4], nbs[:, 0:4], ns, sub)
                            S.act(ess[:, 0:4], t8[:, 0:4], AF.Exp)
                            yield
                            for ii in range(2):
                                for g in range(2):
                                    s_ = ii * 2 + g
                                    S.act(sm[ii][:, g, :], sm[ii][:, g, :], AF.Exp, bias=nbs[:, s_:s_ + 1], scale=0.125,
                                          accum=rss[:, s_:s_ + 1])
                                    yield
                            S.tt(dens[:, 0:4], rss[:, 0:4], ess[:, 0:4], add)
                            S.recip(dens[:, 0:4], dens[:, 0:4])
                            yield
                            for ii in range(2):
                                rb = dens[:, ii * 2:ii * 2 + 2].re("p (g o) -> p g o", o=1).bc([128, 2, 256])
                                S.tt(Pn[ii], sm[ii], rb, mult)
                                yield
                            ps = rs()
                            psb = ps.bitcast(BF16)
                            for q in range(8):
                                ii, g, kb = q // 4, (q // 2) % 2, q % 2
                                S.tr(psb[:, q * 128:(q + 1) * 128], Pn[ii][:, g, kb * 128:(kb + 1) * 128], identb,
                                     signal=(q == 7))
                            evac(PTs.re("p a b -> p (a b)"), psb)
                            yield
                            ps = rs()
                            for ii in range(2):
                                n_ = 0
                                for g in range(2):
                                    for kb in range(2):
                                        S.mm(ps[:, ii * 128:(ii + 1) * 128], svpad[:, j + kb, g, :],
                                             PTs[:, (ii * 2 + g) * 2 + kb, :], start=(n_ == 0), stop=(n_ == 3))
                                        n_ += 1
                            evac(ysw[:, hv * 2:hv * 2 + 2, j * 128:(j + 1) * 128],
                                 ps[:, 0:256].re("p (i q) -> p i q", i=2))
                            yield

                def mlstm_gen():
                    S.memset(mkdec, 0.0, eng="pool")
                    ibv = par[:, P_IB + l * 4:P_IB + l * 4 + 4].re("p (o h) -> p o h", o=1).bc([128, 4, 4])
                    fbv = par[:, P_FB + l * 4:P_FB + l * 4 + 4].re("p (o h) -> p o h", o=1).bc([128, 4, 4])
                    S.tt(g_t, graw[:, :, 8:12], ibv, add)
                    S.act(g_t, g_t, AF.Tanh, scale=1.0 / 15.0)
                    S.ts(g_ig, g_t, 15.0, mult)
                    S.tt(g_t, graw[:, :, 12:16], fbv, add)
                    S.act(g_t, g_t, AF.Tanh, scale=1.0 / 15.0)
                    S.act(g_t, g_t, AF.Exp, scale=-15.0)
                    S.act(g_t, g_t, AF.Ln, bias=1.0)
                    S.ts(g_b16, g_t, -1.0, mult)
                    S.copy(g_lf, g_b16)
                    yield
                    Uv = cst[:, C_U:C_U + 128].re("p (o c) -> p o c", o=1).bc([128, 4, 128])
                    for j in range(4):
                        cols = slice(j * 128, (j + 1) * 128)
                        lfj = g_lf[:, j, :]
                        S.tt(mG1, Uv, lfj.re("p (h o) -> p h o", o=1).bc([128, 4, 128]), mult)
                        ps = rm()
                        S.mm(ps, onesb, mG1.re("p h c -> p (h c)"))
                        S.act(mEb.re("p h c -> p (h c)"), ps, AF.Exp)
                        yield
                        psD = rm()
                        for h in range(4):
                            o_ = psD[:, h * 128:(h + 1) * 128]
                            S.mm(o_, onesb, mG1[:, h, :], start=True, stop=False)
                            S.mm(o_, mG1[:, h, :], negonesb, start=False, stop=False)
                            S.mm(o_, identb, maskTb, start=False, stop=True)
                            yield
                        for h in range(4):
                            S.act(mPT[:, h, :], psD[:, h * 128:(h + 1) * 128], AF.Exp, bias=g_ig[:, j, h:h + 1])
                            yield
                        psS = rm()
                        for h in (0, 2, 1, 3):
                            pr, hf = h // 2, h % 2
                            S.mm(psS[:, h * 128:(h + 1) * 128], mkf[pr][hf * 64:(hf + 1) * 64, cols],
                                 mqf[pr][hf * 64:(hf + 1) * 64, cols])
                        S.tt(mW.re("p h c -> p (h c)"), mPT.re("p h c -> p (h c)"), psS, mult)
                        yield
                        for pr in range(2):
                            for hf in range(2):
                                rows = slice(hf * 64, (hf + 1) * 64)
                                S.tt(mqdec[pr][rows, :], mqf[pr][rows, cols], mEb[rows, 2 * pr + hf, :], mult)
                        psw = rm()
                        S.mm(psw[:, 0:4], cs(C_SU), lfj)
                        S.tt(st4[0], psw[:, 0:4], g_ig[:, j, :], add)
                        S.act(st4[1], st4[0], AF.Exp)
                        yield
                        for h in range(4):
                            hf = h % 2
                            S.act(mkdec[:, h, hf * 64:(hf + 1) * 64], mk_tm[:, j, h * 64:(h + 1) * 64],
                                  AF.Copy, scale=st4[1][:, h:h + 1])
                            if t == 0 and l == 0 and j == 0:
                                pass
                        ebv = mEb.re("p h (ch c) -> p h ch c", ch=2)[:, :, :, 63].re("p (pr hf) ch -> p pr hf ch", hf=2)
                        S.copy(dsel[0:64], ebv[0:64, :, 0, :])
                        S.copy(dsel[64:128], ebv[64:128, :, 1, :])
                        yield
                        for ch in range(2):
                            rows = slice(ch * 64, (ch + 1) * 64)
                            cc = slice(ch * 64, (ch + 1) * 64)
                            psN = rm()
                            nv = psN[:, 0:256].re("p (pr kd c) -> p pr kd c", pr=2, kd=2)
                            psU = [rm(), rm()]
                            for pr in range(2):
                                S.mm(nv[:, pr, 0, :], CNb[l][pr][:, 0:128], mqdec[pr][:, cc], start=True, stop=False)
                                S.mm(nv[:, pr, 0, :], mvpad[rows, j, pr, 0, :], mW[rows, 2 * pr, cc], start=False, stop=False)
                                S.mm(nv[:, pr, 0, :], mvpad[rows, j, pr, 1, :], mW[rows, 2 * pr + 1, cc], start=False, stop=True)
                                S.mm(nv[:, pr, 1, :], CNb[l][pr][:, 128:256], mqdec[pr][:, cc], start=True, stop=False)
                                S.mm(nv[:, pr, 1, :], onesEb[rows, :], mW[rows, 2 * pr, cc], start=False, stop=False)
                                S.mm(nv[:, pr, 1, :], onesOb[rows, :], mW[rows, 2 * pr + 1, cc], start=False, stop=True)
                                uu = psU[pr]
                                S.mm(uu[:, 0:128], mkdec[rows, 2 * pr, :], mvpad[rows, j, pr, 0, :], start=True, stop=False)
                                S.mm(uu[:, 0:128], mkdec[rows, 2 * pr + 1, :], mvpad[rows, j, pr, 1, :], start=False, stop=True)
                                S.mm(uu[:, 128:256], mkdec[rows, 2 * pr, :], onesEb[rows, :], start=True, stop=False)
                                S.mm(uu[:, 128:256], mkdec[rows, 2 * pr + 1, :], onesOb[rows, :], start=False, stop=True)
                                yield
                            for pr in range(2):
                                S.stt(CN[l][pr], CN[l][pr], dsel[:, pr, ch:ch + 1], psU[pr][:, 0:256], mult, add)
                                S.copy(CNb[l][pr], CN[l][pr], eng="act")
                                yield
                            S.act(mdab, nv[:, :, 1, :], AF.Abs)
                            S.ts(mdab, mdab, 1.0, mx_)
                            S.recip(mdab, mdab)
                            S.tt(oT[:, :, j * 128 + ch * 64: j * 128 + (ch + 1) * 64], nv[:, :, 0, :], mdab, mult)
                            yield
                def gdn_gen():
                    for m in range(6):
                        e_ = "dve"
                        cw = P_CONV + l * 24 + m * 4
                        S.ts(tmpA if m % 2 == 0 else tmpB, raw[m][:, 0:TT], pc(cw), mult, eng=e_)
                        acc = tmpA if m % 2 == 0 else tmpB
                        for jj in range(1, 4):
                            S.stt(acc, raw[m][:, jj:jj + TT], pc(cw + jj), acc, mult, add, eng=e_)
                        S.act(qkvc[m], acc, AF.Silu)
                        yield
                        S.copy(ctail[l][:, m, :], raw[m][:, TT:TT + 3], eng="pool")
                    for m in range(4):
                        S.act(sqb[m % 2], qkvc[m], AF.Square)
                        ps = rg()
                        S.mm(ps, bdonesb, sqb[m % 2])
                        rsqrt_from(ps, 1.0, tmpA)
                        S.stt(qkvc[m], qkvc[m], (0.125 if m < 2 else 1.0), tmpA, mult, mult)
                        yield
                    dtv = par[:, P_DTB + l * 4:P_DTB + l * 4 + 4].re("p (o h) -> p o h", o=1).bc([128, 4, 4])
                    nAv = nA[:, l * 4:l * 4 + 4].re("p (o h) -> p o h", o=1).bc([128, 4, 4])
                    S.act(g_beta, graw[:, :, 0:4], AF.Sigmoid)
                    S.tt(g_t2, graw[:, :, 4:8], dtv, add)
                    S.act(g_t2, g_t2, AF.Exp)
                    S.act(g_t2, g_t2, AF.Ln, bias=1.0)
                    S.tt(g_b162, g_t2, nAv, mult)
                    S.copy(g_g, g_b162)
                    yield
                    for j in range(4):
                        ps = rg()
                        for q in range(4):
                            src = qkvc[2 + q] if q < 2 else qkvc[4 + (q - 2)]
                            S.tr(ps[:, q * 128:(q + 1) * 128], src[:, j * 128:(j + 1) * 128], cs(C_ID), signal=(q == 3))
                        evac(raw[j][:, 0:512], ps)
                        yield
                    for pr in range(2):
                        S.memset(gvnE[pr], 0.0, eng="pool")
                        S.memset(gvnO[pr], 0.0, eng="pool")
                    S.memset(gkbg, 0.0, eng="pool")
                    S.memset(gkdec, 0.0, eng="pool")
                    yield
                    strv = cst[:, C_STR:C_STR + 128].re("p (o c) -> p o c", o=1).bc([128, 4, 128])
                    idv = cst[:, C_ID:C_ID + 128].re("p (o c) -> p o c", o=1).bc([128, 4, 128])
                    for j in range(4):
                        cols = slice(j * 128, (j + 1) * 128)
                        gj = g_g[:, j, :]
                        bj = g_beta[:, j, :]
                        S.tt(gG1, Uv, gj.re("p (h o) -> p h o", o=1).bc([128, 4, 128]), mult)
                        ps = rg()
                        S.mm(ps, onesb, gG1.re("p h c -> p (h c)"))
                        S.act(gEg.re("p h c -> p (h c)"), ps, AF.Exp)
                        yield
                        psD = rg()
                        for h in range(4):
                            o_ = psD[:, h * 128:(h + 1) * 128]
                            S.mm(o_, gG1[:, h, :], onesb, start=True, stop=False)
                            S.mm(o_, negonesb, gG1[:, h, :], start=False, stop=False)
                            S.mm(o_, identb, maskCSb, start=False, stop=True)
                        S.act(gdec.re("p h c -> p (h c)"), psD, AF.Exp)
                        yield
                        psK = rg()
                        psQ = rg()
                        for h in (0, 2, 1, 3):
                            pr, hf = h // 2, h % 2
                            rows = slice(hf * 64, (hf + 1) * 64)
                            S.mm(psK[:, h * 128:(h + 1) * 128], qkvc[2 + pr][rows, cols], qkvc[2 + pr][rows, cols])
                            S.mm(psQ[:, h * 128:(h + 1) * 128], qkvc[pr][rows, cols], qkvc[2 + pr][rows, cols])
                        S.tt(gA.re("p h c -> p (h c)"), gdec.re("p h c -> p (h c)"), psK, mult)
                        S.tt(gQKd.re("p h c -> p (h c)"), gdec.re("p h c -> p (h c)"), psQ, mult)
                        yield
                        S.tt(gA, gA, bj.re("p (h o) -> p h o", o=1).bc([128, 4, 128]), mult)
                        S.tt(gL, gA, strv, mult)
                        yield
                        ps = rg()
                        for h in range(4):
                            S.tr(ps[:, h * 128:(h + 1) * 128], gL[:, h, :], cs(C_ID), signal=(h == 3))
                        evac(gMt.re("p h c -> p (h c)"), ps)
                        yield
                        ps = rg()
                        for h in range(4):
                            S.tr(ps[:, h * 128:(h + 1) * 128], gQKd[:, h, :], cs(C_ID), signal=(h == 3))
                        evac(gQKdT.re("p h c -> p (h c)"), ps)
                        yield
                        S.tt(gP, idv, gMt, sub)
                        yield
                        cur = 0
                        for it in range(1, 6):
                            Lc, Mc = gLp[cur], gMp[cur]
                            Ln_, Mn_ = gLp[1 - cur], gMp[1 - cur]
                            psL = rg()
                            for h in range(4):
                                S.mm(psL[:, h * 128:(h + 1) * 128], Mc[:, h, :], Lc[:, h, :])
                            if it < 5:
                                psM = rg()
                                for h in range(4):
                                    S.mm(psM[:, h * 128:(h + 1) * 128], Lc[:, h, :], Mc[:, h, :])
                            S.copy(Ln_.re("p h c -> p (h c)"), psL, eng="act")
                            yield
                            if it < 5:
                                S.copy(Mn_.re("p h c -> p (h c)"), psM, eng="dve")
                                yield
                            psP = rg()
                            for h in range(4):
                                S.mm(psP[:, h * 128:(h + 1) * 128], Ln_[:, h, :], gP[:, h, :])
                            S.tt(gP.re("p h c -> p (h c)"), gP.re("p h c -> p (h c)"), psP, add)
                            yield
                            cur = 1 - cur
                        psg = rg()
                        S.mm(psg[:, 0:4], cs(C_U), gj)
                        S.mm(psg[:, 4:8], cs(C_SU), gj)
                        S.act(st4b[0], psg[:, 0:4], AF.Exp)
                        S.act(st4b[1], psg[:, 4:8], AF.Exp)
                        S.tt(st4b[2], st4b[0], bj, mult)
                        yield
                        ktv = raw[j][:, 0:256]
                        vtv = raw[j][:, 256:512]
                        for h in range(4):
                            hf = h % 2
                            S.act(gkbg[:, h, hf * 64:(hf + 1) * 64], ktv[:, h * 64:(h + 1) * 64], AF.Copy,
                                  scale=st4b[2][:, h:h + 1])
                            S.act(gkdec[:, h, hf * 64:(hf + 1) * 64], ktv[:, h * 64:(h + 1) * 64], AF.Copy,
                                  scale=st4b[1][:, h:h + 1])
                        S.tt(gvb, vtv.re("p (h e) -> p h e", h=4), bj.re("p (h o) -> p h o", o=1).bc([128, 4, 64]), mult)
                        yield
                        psu = rg()
                        for h in range(4):
                            S.mm(psu[:, h * 64:(h + 1) * 64], gP[:, h, :], gvb[:, h, :])
                        evac(gu, psu[:, 0:256])
                        yield
                        psw = rg()
                        for pr in range(2):
                            S.mm(psw[:, pr * 128:(pr + 1) * 128], gkbg[:, 2 * pr, :], gP[:, 2 * pr, :], start=True, stop=False)
                            S.mm(psw[:, pr * 128:(pr + 1) * 128], gkbg[:, 2 * pr + 1, :], gP[:, 2 * pr + 1, :], start=False, stop=True)
                        evac(gwT.re("p a b -> p (a b)"), psw[:, 0:256])
                        yield
                        for pr in range(2):
                            for hf in range(2):
                                rows = slice(hf * 64, (hf + 1) * 64)
                                S.tt(gqdT[pr][rows, :], qkvc[pr][rows, cols], gEg[rows, 2 * pr + hf, :], mult)
                        egv = gEg.re("p h (ch c) -> p h ch c", ch=2)[:, :, :, 63].re("p (pr hf) ch -> p pr hf ch", hf=2)
                        S.copy(dsel2[0:64], egv[0:64, :, 0, :])
                        S.copy(dsel2[64:128], egv[64:128, :, 1, :])
                        yield
                        for ch in range(2):
                            rows = slice(ch * 64, (ch + 1) * 64)
                            cc = slice(ch * 64, (ch + 1) * 64)
                            psO = pm[0]
                            for pr in range(2):
                                psV = rg()
                                S.mm(psV[:, 0:128], gwT[:, pr, :], Sbd[l][pr])
                                S.tt(gvnE[pr][rows, 0:64], gu[rows, pr * 128:pr * 128 + 64], psV[rows, 0:64], sub)
                                S.tt(gvnO[pr][rows, 64:128], gu[rows, pr * 128 + 64:pr * 128 + 128], psV[rows, 64:128], sub)
                                yield
                                o_ = psO[:, pr * 64:(pr + 1) * 64]
                                S.mm(o_, Sbd[l][pr], gqdT[pr][:, cc], start=True, stop=False)
                                S.mm(o_, gvnE[pr][rows, :], gQKdT[rows, 2 * pr, cc], start=False, stop=False)
                                S.mm(o_, gvnO[pr][rows, :], gQKdT[rows, 2 * pr + 1, cc], start=False, stop=True)
                                psS_ = rg()
                                S.mm(psS_[:, 0:128], gkdec[rows, 2 * pr, :], gvnE[pr][rows, :], start=True, stop=False)
                                S.mm(psS_[:, 0:128], gkdec[rows, 2 * pr + 1, :], gvnO[pr][rows, :], start=False, stop=True)
                                S.stt(Sbd[l][pr], Sbd[l][pr], dsel2[:, pr, ch:ch + 1], psS_[:, 0:128], mult, add)
                                yield
                            evac(oT2[:, :, j * 128 + ch * 64:j * 128 + (ch + 1) * 64], psO[:, 0:128].re("p (pr c) -> p pr c", pr=2))
                            yield
                gens = [[gdn_gen(), GW[0], 0], [swa_gen(), GW[1], 0], [mlstm_gen(), GW[2], 0]]
                live = list(gens)
                while live:
                    for ge in list(live):
                        for _ in range(ge[1]):
                            try:
                                next(ge[0])
                                ge[2] += 1
                            except StopIteration:
                                live.remove(ge)
                                break
                if t == 0 and l == 0:
                    print("mixer chain steps (gdn, swa, mlstm):", [ge[2] for ge in gens], flush=True)
                S.handoff(cW.views, hn)
                for pr in range(2):
                    headnorm(oT[:, pr, :], P_MLN + l * 2 + pr, mos[pr], yA[2 + pr])

                for pr in range(2):
                    headnorm(oT2[:, pr, :], P_GDNN + l, gzs[pr], yA[pr])

                if STOP <= 4:
                    break
                ymix = [yA[0], yA[1], yA[2], yA[3], ysw[:, 0, :], ysw[:, 1, :], ysw[:, 2, :], ysw[:, 3, :]]
                for c in range(2):
                    sl = use_w(l, 8 + c)
                    for mi in range(4):
                        ps = dense_fm(sl, 8, 512, mi, ymix)
                        m = c * 4 + mi
                        S.tt(xT[m], xT[m], ps, add)
                    done_w()
                rmsnorm(G_MLP + l * 8, hn)
                arena_switch(cU)
                for c in range(8):
                    sl = use_w(l, 10 + c)
                    for mi in range(4):
                        ps = dense_fm(sl, 8, 512, mi, hn)
                        tr_ = tmpA if mi % 2 == 0 else tmpB
                        S.act(tr_, ps, AF.Relu)
                        S.tt(u[c * 4 + mi], tr_, tr_, mult, eng=("pool" if mi % 2 == 0 else "dve"))
                    done_w()
                for c in range(8):
                    sl = use_w(l, 18 + c)
                    ps = dense_fm(sl, 32, 128, 0, u)
                    S.tt(xT[c], xT[c], ps, add)
                    done_w()
                rmsnorm(G_PLE + l * 8, hn)
                slp = use_w(l, 26)
                for c in range(2):
                    sl = use_w(l, 27 + c)
                    for mi in range(4):
                        m = c * 4 + mi
                        psg = dense_fm(sl, 8, 512, mi, hn)
                        psp = dense_fm(slp, 2, 1024, m, pT)
                        S.act(tmpA, psg, AF.Sigmoid)
                        S.tt(tmpB, tmpA, psp, mult)
                        S.tt(xT[m], xT[m], tmpB, add, eng="pool")
                done_w()
                done_w()
                done_w()

            if 'f' not in SKIP:
                rmsnorm(G_FIN, xT, inplace=True)
            for j in range(4):
                for half in range(2):
                    ps = pdn()
                    for q in range(4):
                        kt = half * 4 + q
                        S.tr(ps[:, q * 128:(q + 1) * 128], xT[kt][:, j * 128:(j + 1) * 128], cs(C_ID), signal=(q == 3))
                    evac(xin[:, half * 512:(half + 1) * 512], ps)
                ev = S.dma(XQ, out_d[tok0 + j * 128:tok0 + (j + 1) * 128, :], xin)
                out_events.append(ev)
        S.finish(out_events[-8:])
        print("instructions:", S.ninstr, "dma sems:", S.ndsem, flush=True)
    return nc


def _bcast_inputs(inp, b, NT, par, cst):
    SEQ = NT * TT
    return {
        "x": np.ascontiguousarray(inp["x"][b, :SEQ]),
        "p": np.ascontiguousarray(inp["p"][:, b, :SEQ]),
        "pos": np.ascontiguousarray(inp["positions"][b:b + 1, :SEQ]).astype(np.int32),
        "w_in": inp["w_in"], "w_out": inp["w_out"], "w_up": inp["w_up"], "w_down": inp["w_down"],
        "w_gate": inp["w_ple_gate"], "w_proj": inp["w_ple_proj"],
        "par": par, "cst": cst,
    }


def kernel(**inputs):
    inp = {k: np.asarray(v) for k, v in inputs.items()}
    NT, NL = 8, 4
    par, cst = host_tables(inp)
    nc = build(NT, NL)
    in_maps = [_bcast_inputs(inp, b, NT, par, cst) for b in range(8)]
    res = run_bass_kernel_spmd(nc, in_maps, core_ids=list(range(8)))
    return np.stack([res.results[b]["out"] for b in range(8)], axis=0).astype(np.float32)
```
